# Optimizing a Trainium2 kernel written in Bass

```python
import math
import jax, jax.numpy as jnp
from jax import lax
import numpy as np

D_MODEL = 1024
BATCH = 8
SEQ = 2048
DEPTH = 2
DEC_BATCH = 32
DEC_SEQ = 8
PAST_LEN = 16384
PAGE_SIZE = 128

N_EVEN = (DEPTH + 1) // 2
N_ODD = DEPTH // 2

S5_WIDTH = 512
S5_GROUP = 16
S5_GROUPS = S5_WIDTH // S5_GROUP
S5_STATE = 64
S5_DT_MIN = 1e-3
S5_DT_MAX = 1e-1
SSD_INNER = 512
SSD_HEADDIM = 64
SSD_HEADS = SSD_INNER // SSD_HEADDIM
SSD_GROUPS = 2
SSD_STATE = 128
SSD_CONV = 4
SSD_CHUNK = 128
SSD_CONV_DIM = SSD_INNER + 2 * SSD_GROUPS * SSD_STATE
IN0_WIDTH = S5_WIDTH + SSD_INNER + SSD_CONV_DIM + SSD_HEADS
MIX0_WIDTH = S5_WIDTH + SSD_INNER
MLA_HEADS = 8
QK_NOPE = 128
QK_ROPE = 64
V_HEAD = 128
Q_LORA = 384
KV_LORA = 256
IN1_WIDTH = Q_LORA + KV_LORA + QK_ROPE
MIX1_WIDTH = MLA_HEADS * V_HEAD
ROPE_BASE = 10000.0
Q_BLOCK = 128
SM_SCALE = (QK_NOPE + QK_ROPE) ** -0.5
D_FF = 2816
FFN_CONV = 3
ALPHA = (2 * DEPTH) ** 0.25
BETA = (8 * DEPTH) ** -0.25
EPS = 1e-5

kernel_name = 'hybrid_s5_ssd_mla_convffn_step'


def layer_norm(x, g, b):
    xf = x.astype(jnp.float32)
    mu = jnp.mean(xf, -1, keepdims=True)
    var = jnp.mean(jnp.square(xf - mu), -1, keepdims=True)
    return ((xf - mu) * lax.rsqrt(var + EPS) * g + b).astype(x.dtype)


def rms_norm(x, g):
    xf = x.astype(jnp.float32)
    return (xf * lax.rsqrt(jnp.mean(xf * xf, -1, keepdims=True) + EPS) * g).astype(x.dtype)


def group_rms_norm(y, g):
    bsz, L, w = y.shape
    yf = y.astype(jnp.float32).reshape(bsz, L, SSD_GROUPS, w // SSD_GROUPS)
    yf = yf * lax.rsqrt(jnp.mean(yf * yf, -1, keepdims=True) + EPS)
    return (yf.reshape(bsz, L, w) * g).astype(y.dtype)


def causal_dwconv(x, buf, w, b):
    xp = jnp.concatenate([buf.astype(x.dtype), x], axis=1)
    y = lax.conv_general_dilated(xp, w[:, None, :], window_strides=(1,), padding='VALID',
                                 dimension_numbers=('NWC', 'WIO', 'NWC'),
                                 feature_group_count=x.shape[-1])
    k1 = w.shape[0] - 1
    return y + b, xp[:, xp.shape[1] - k1:]


def s5_mix(u, h0_re, h0_im, a_re, a_im, log_dt, b_re, b_im, c_re, c_im, d, w_glu, b_glu):
    bsz, L, _ = u.shape
    f32 = jnp.float32
    ug = u.reshape(bsz, L, S5_GROUPS, S5_GROUP)
    ar, ai = a_re.astype(f32), a_im.astype(f32)
    step = jnp.exp(log_dt.astype(f32))[:, None]
    mag = jnp.exp(ar * step)
    lam_re, lam_im = mag * jnp.cos(ai * step), mag * jnp.sin(ai * step)
    den = ar * ar + ai * ai
    f_re = ((lam_re - 1.0) * ar + lam_im * ai) / den
    f_im = (lam_im * ar - (lam_re - 1.0) * ai) / den
    br, bi = b_re.astype(f32), b_im.astype(f32)
    bb_re = (f_re[..., None] * br - f_im[..., None] * bi).astype(u.dtype)
    bb_im = (f_re[..., None] * bi + f_im[..., None] * br).astype(u.dtype)
    bu_re = jnp.einsum('gnh,blgh->blgn', bb_re, ug)
    bu_im = jnp.einsum('gnh,blgh->blgn', bb_im, ug)
    lr = jnp.broadcast_to(lam_re.astype(u.dtype), bu_re.shape)
    li = jnp.broadcast_to(lam_im.astype(u.dtype), bu_re.shape)

    def combine(e1, e2):
        a1r, a1i, b1r, b1i = e1
        a2r, a2i, b2r, b2i = e2
        return (a2r * a1r - a2i * a1i, a2r * a1i + a2i * a1r,
                a2r * b1r - a2i * b1i + b2r, a2r * b1i + a2i * b1r + b2i)

    pr, pi, sr, si = lax.associative_scan(combine, (lr, li, bu_re, bu_im), axis=1)
    h0r, h0i = h0_re.astype(u.dtype)[:, None], h0_im.astype(u.dtype)[:, None]
    h_re = sr + pr * h0r - pi * h0i
    h_im = si + pr * h0i + pi * h0r
    y = (jnp.einsum('ghn,blgn->blgh', c_re, h_re) - jnp.einsum('ghn,blgn->blgh', c_im, h_im)
         + d * ug).reshape(bsz, L, S5_WIDTH)
    g = jax.nn.gelu(y)
    out = g * jax.nn.sigmoid(g @ w_glu + b_glu)
    return out, h_re[:, -1], h_im[:, -1]


def ssd_scan(x, dt, a, bm, cm, h0):
    bsz, L, H, P = x.shape
    T = min(SSD_CHUNK, L)
    nc = -(-L // T)
    pad = nc * T - L
    if pad:
        padw = lambda t: jnp.pad(t, [(0, 0), (0, pad)] + [(0, 0)] * (t.ndim - 2))
        x, dt, bm, cm = padw(x), padw(dt), padw(bm), padw(cm)
    G, N = SSD_GROUPS, bm.shape[-1]
    R = H // G
    x = x.reshape(bsz, nc, T, G, R, P)
    dt = dt.reshape(bsz, nc, T, G, R)
    bm = bm.reshape(bsz, nc, T, G, N)
    cm = cm.reshape(bsz, nc, T, G, N)
    cs = jnp.cumsum((dt * a.reshape(G, R)).astype(jnp.float32), axis=2)
    causal = jnp.tril(jnp.ones((T, T), bool))[None, None, :, :, None, None]
    seg = cs[:, :, :, None] - cs[:, :, None, :]
    decay = jnp.exp(jnp.where(causal, seg, -jnp.inf)).astype(x.dtype)
    xdt = x * dt[..., None]
    cb = jnp.einsum('bctgn,bcsgn->bctsg', cm, bm)
    y_diag = jnp.einsum('bctsgr,bcsgrp->bctgrp', cb[..., None] * decay, xdt)
    to_end = jnp.exp(cs[:, :, -1:] - cs).astype(x.dtype)
    states = jnp.einsum('bcsgn,bcsgrp->bcgrpn', bm, xdt * to_end[..., None])
    chunk_decay = jnp.exp(cs[:, :, -1]).astype(x.dtype)

    def step(h, inp):
        st, dec = inp
        return h * dec[..., None, None] + st, h

    h_last, h_prev = lax.scan(step, h0.astype(x.dtype).reshape(bsz, G, R, P, N),
                              (jnp.moveaxis(states, 1, 0), jnp.moveaxis(chunk_decay, 1, 0)))
    h_prev = jnp.moveaxis(h_prev, 0, 1)
    y_off = jnp.einsum('bctgn,bcgrpn->bctgrp', cm, h_prev) * jnp.exp(cs).astype(x.dtype)[..., None]
    y = (y_diag + y_off).reshape(bsz, nc * T, H, P)[:, :L]
    return y, h_last.reshape(bsz, H, P, N)


def ssd_mix(z, xbc, dt_raw, conv_buf, h0, conv_w, conv_b, dt_bias, a_log, d_skip, norm_g):
    xbc, new_buf = causal_dwconv(xbc, conv_buf, conv_w, conv_b)
    xbc = jax.nn.silu(xbc)
    bsz, L, _ = xbc.shape
    o1 = SSD_INNER + SSD_GROUPS * SSD_STATE
    xs = xbc[..., :SSD_INNER].reshape(bsz, L, SSD_HEADS, SSD_HEADDIM)
    bm = xbc[..., SSD_INNER:o1].reshape(bsz, L, SSD_GROUPS, SSD_STATE)
    cm = xbc[..., o1:].reshape(bsz, L, SSD_GROUPS, SSD_STATE)
    dt = jax.nn.softplus(dt_raw + dt_bias)
    a = -jnp.exp(a_log)
    y, h_last = ssd_scan(xs, dt, a, bm, cm, h0)
    y = (y + d_skip[:, None] * xs).reshape(bsz, L, SSD_INNER) * jax.nn.silu(z)
    return group_rms_norm(y, norm_g), new_buf, h_last


def rope_tables(pos):
    inv = ROPE_BASE ** (-jnp.arange(0, QK_ROPE, 2, dtype=jnp.float32) / QK_ROPE)
    ang = pos.astype(jnp.float32)[:, None] * inv
    return jnp.cos(ang), jnp.sin(ang)


def apply_rope(x, cos, sin):
    half = QK_ROPE // 2
    x1, x2 = x[..., :half], x[..., half:]
    return jnp.concatenate([x1 * cos - x2 * sin, x1 * sin + x2 * cos], -1).astype(x.dtype)


def mla_project(x, pos, w_in, g_q, g_kv, w_uq, w_uk):
    bsz, L, _ = x.shape
    h = x @ w_in
    cq = rms_norm(h[..., :Q_LORA], g_q)
    ckv = rms_norm(h[..., Q_LORA:Q_LORA + KV_LORA], g_kv)
    q = (cq @ w_uq).reshape(bsz, L, MLA_HEADS, QK_NOPE + QK_ROPE)
    cos, sin = rope_tables(pos)
    q_rope = apply_rope(q[..., QK_NOPE:], cos[:, None, :], sin[:, None, :])
    k_rope = apply_rope(h[..., Q_LORA + KV_LORA:], cos, sin)
    q_lat = jnp.einsum('blhd,chd->blhc', q[..., :QK_NOPE], w_uk)
    return q_lat, q_rope, ckv, k_rope


def mla_attend(q_lat, q_rope, ckv, kr):
    nq, nk = q_lat.shape[1], ckv.shape[1]
    s = jnp.einsum('bqhc,bkc->bhqk', q_lat, ckv) + jnp.einsum('bqhr,bkr->bhqk', q_rope, kr)
    visible = jnp.arange(nk)[None, :] <= (nk - nq + jnp.arange(nq))[:, None]
    s = jnp.where(visible, s.astype(jnp.float32) * SM_SCALE, -jnp.inf)
    pr = jax.nn.softmax(s, axis=-1).astype(ckv.dtype)
    return jnp.einsum('bhqk,bkc->bqhc', pr, ckv)


def conv_ffn(x, buf, w_up, conv_w, conv_b, w_down):
    h = x @ w_up
    h, new_buf = causal_dwconv(h, buf, conv_w, conv_b)
    return (jax.nn.silu(h[..., :D_FF]) * h[..., D_FF:]) @ w_down, new_buf


def run_group(x, pos0, s5_re0, s5_im0, ssd_h0, ssd_conv0, ffn_conv0, cache_ckv, cache_krope, page_table, p):
    bsz, L, _ = x.shape
    pos = pos0 + jnp.arange(L, dtype=jnp.int32)
    s5_re, s5_im, ssd_h, ssd_cv, ckv_rows, kr_rows, ffn_cv = [], [], [], [], [], [], []
    for i in range(DEPTH):
        j = i // 2
        if i % 2 == 0:
            h = x @ p['w_in0'][j]
            o0 = S5_WIDTH
            o1 = o0 + SSD_INNER
            o2 = o1 + SSD_CONV_DIM
            y5, hr, hi = s5_mix(h[..., :o0], s5_re0[j], s5_im0[j], p['s5_a_re'][j], p['s5_a_im'][j],
                                p['s5_log_dt'][j], p['s5_b_re'][j], p['s5_b_im'][j], p['s5_c_re'][j],
                                p['s5_c_im'][j], p['s5_d'][j], p['s5_w_glu'][j], p['s5_b_glu'][j])
            ys, cbuf, hs = ssd_mix(h[..., o0:o1], h[..., o1:o2], h[..., o2:], ssd_conv0[j], ssd_h0[j],
                                   p['ssd_conv_w'][j], p['ssd_conv_b'][j], p['ssd_dt_bias'][j],
                                   p['ssd_a_log'][j], p['ssd_d'][j], p['ssd_norm_g'][j])
            mix = jnp.concatenate([y5, ys], axis=-1) @ p['w_out0'][j]
            s5_re.append(hr)
            s5_im.append(hi)
            ssd_h.append(hs)
            ssd_cv.append(cbuf)
        else:
            q_lat, q_rope, ckv, kr = mla_project(x, pos, p['w_in1'][j], p['mla_q_norm_g'][j],
                                                 p['mla_kv_norm_g'][j], p['mla_w_uq'][j], p['mla_w_uk'][j])
            if page_table is None:
                o_lat = jnp.concatenate(
                    [mla_attend(q_lat[:, s:s + Q_BLOCK], q_rope[:, s:s + Q_BLOCK],
                                ckv[:, :s + Q_BLOCK], kr[:, :s + Q_BLOCK]) for s in range(0, L, Q_BLOCK)],
                    axis=1)
            else:
                n_past = page_table.shape[1] * PAGE_SIZE
                past_ckv = cache_ckv[page_table, j].reshape(bsz, n_past, KV_LORA).astype(ckv.dtype)
                past_kr = cache_krope[page_table, j].reshape(bsz, n_past, QK_ROPE).astype(kr.dtype)
                o_lat = mla_attend(q_lat, q_rope, jnp.concatenate([past_ckv, ckv], axis=1),
                                   jnp.concatenate([past_kr, kr], axis=1))
            o = jnp.einsum('blhc,chd->blhd', o_lat, p['mla_w_uv'][j]).reshape(bsz, L, MIX1_WIDTH)
            mix = o @ p['w_out1'][j]
            ckv_rows.append(ckv)
            kr_rows.append(kr)
        x = layer_norm(ALPHA * x + mix, p['ln1_g'][i], p['ln1_b'][i])
        f, fbuf = conv_ffn(x, ffn_conv0[i], p['ffn_w_up'][i], p['ffn_conv_w'][i], p['ffn_conv_b'][i],
                           p['ffn_w_down'][i])
        ffn_cv.append(fbuf)
        x = layer_norm(ALPHA * x + f, p['ln2_g'][i], p['ln2_b'][i])
    return x, (jnp.stack(s5_re), jnp.stack(s5_im), jnp.stack(ssd_h), jnp.stack(ssd_cv),
               jnp.stack(ckv_rows, axis=1), jnp.stack(kr_rows, axis=1), jnp.stack(ffn_cv))


def setup_inputs(seed: int = 0) -> dict:
    key = jax.random.key(seed)
    ks = iter(jax.random.split(key, 64))

    def nrm(shape, scale):
        return jax.random.normal(next(ks), shape, jnp.float32) * scale

    NE, NO = N_EVEN, N_ODD
    n_pages = PAST_LEN // PAGE_SIZE
    n_used = DEC_BATCH * n_pages
    n_pool = n_used + n_used // 4
    x_prompt = nrm((BATCH, SEQ, D_MODEL), 1.0)
    x_sample = nrm((DEC_BATCH, DEC_SEQ, D_MODEL), 1.0)
    state_s5_re = nrm((NE, DEC_BATCH, S5_GROUPS, S5_STATE), 0.5)
    state_s5_im = nrm((NE, DEC_BATCH, S5_GROUPS, S5_STATE), 0.5)
    state_ssd = nrm((NE, DEC_BATCH, SSD_HEADS, SSD_HEADDIM, SSD_STATE), 0.3)
    state_ssd_conv = nrm((NE, DEC_BATCH, SSD_CONV - 1, SSD_CONV_DIM), 1.0)
    state_ffn_conv = nrm((DEPTH, DEC_BATCH, FFN_CONV - 1, 2 * D_FF), 1.0)
    cache_ckv = nrm((n_pool, NO, PAGE_SIZE, KV_LORA), 1.0)
    cache_krope = nrm((n_pool, NO, PAGE_SIZE, QK_ROPE), 1.0)
    page_table = jax.random.permutation(next(ks), n_pool)[:n_used].reshape(DEC_BATCH, n_pages).astype(jnp.int32)
    w_in0 = nrm((NE, D_MODEL, IN0_WIDTH), D_MODEL ** -0.5)
    s5_a_re = -0.5 + nrm((NE, S5_GROUPS, S5_STATE), 0.01)
    s5_a_im = math.pi * jnp.arange(S5_STATE, dtype=jnp.float32) + nrm((NE, S5_GROUPS, S5_STATE), 0.01)
    s5_log_dt = jax.random.uniform(next(ks), (NE, S5_GROUPS), jnp.float32,
                                   minval=math.log(S5_DT_MIN), maxval=math.log(S5_DT_MAX))
    s5_b_re = nrm((NE, S5_GROUPS, S5_STATE, S5_GROUP), (2 * S5_GROUP) ** -0.5)
    s5_b_im = nrm((NE, S5_GROUPS, S5_STATE, S5_GROUP), (2 * S5_GROUP) ** -0.5)
    s5_c_re = nrm((NE, S5_GROUPS, S5_GROUP, S5_STATE), (2 * S5_STATE) ** -0.5)
    s5_c_im = nrm((NE, S5_GROUPS, S5_GROUP, S5_STATE), (2 * S5_STATE) ** -0.5)
    s5_d = nrm((NE, S5_GROUPS, S5_GROUP), 0.5)
    s5_w_glu = nrm((NE, S5_WIDTH, S5_WIDTH), S5_WIDTH ** -0.5)
    s5_b_glu = nrm((NE, S5_WIDTH), 0.02)
    ssd_conv_w = nrm((NE, SSD_CONV, SSD_CONV_DIM), SSD_CONV ** -0.5)
    ssd_conv_b = nrm((NE, SSD_CONV_DIM), 0.02)
    dt0 = jnp.exp(jax.random.uniform(next(ks), (NE, SSD_HEADS), jnp.float32,
                                     minval=math.log(1e-3), maxval=math.log(1e-1)))
    ssd_dt_bias = dt0 + jnp.log(-jnp.expm1(-dt0))
    ssd_a_log = jnp.log(jax.random.uniform(next(ks), (NE, SSD_HEADS), jnp.float32, minval=1.0, maxval=16.0))
    ssd_d = 1.0 + nrm((NE, SSD_HEADS), 0.01)
    ssd_norm_g = 1.0 + nrm((NE, SSD_INNER), 0.01)
    w_out0 = nrm((NE, MIX0_WIDTH, D_MODEL), BETA * MIX0_WIDTH ** -0.5)
    w_in1 = nrm((NO, D_MODEL, IN1_WIDTH), D_MODEL ** -0.5)
    mla_q_norm_g = 1.0 + nrm((NO, Q_LORA), 0.01)
    mla_kv_norm_g = 1.0 + nrm((NO, KV_LORA), 0.01)
    mla_w_uq = nrm((NO, Q_LORA, MLA_HEADS * (QK_NOPE + QK_ROPE)), Q_LORA ** -0.5)
    mla_w_uk = nrm((NO, KV_LORA, MLA_HEADS, QK_NOPE), KV_LORA ** -0.5)
    mla_w_uv = nrm((NO, KV_LORA, MLA_HEADS, V_HEAD), KV_LORA ** -0.5)
    w_out1 = nrm((NO, MIX1_WIDTH, D_MODEL), BETA * MIX1_WIDTH ** -0.5)
    ln1_g = 1.0 + nrm((DEPTH, D_MODEL), 0.01)
    ln1_b = nrm((DEPTH, D_MODEL), 0.01)
    ffn_w_up = nrm((DEPTH, D_MODEL, 2 * D_FF), D_MODEL ** -0.5)
    ffn_conv_w = nrm((DEPTH, FFN_CONV, 2 * D_FF), FFN_CONV ** -0.5)
    ffn_conv_b = nrm((DEPTH, 2 * D_FF), 0.02)
    ffn_w_down = nrm((DEPTH, D_FF, D_MODEL), BETA * D_FF ** -0.5)
    ln2_g = 1.0 + nrm((DEPTH, D_MODEL), 0.01)
    ln2_b = nrm((DEPTH, D_MODEL), 0.01)
    return {'x_prompt': x_prompt, 'x_sample': x_sample,
            'state_s5_re': state_s5_re, 'state_s5_im': state_s5_im, 'state_ssd': state_ssd,
            'state_ssd_conv': state_ssd_conv, 'state_ffn_conv': state_ffn_conv,
            'cache_ckv': cache_ckv, 'cache_krope': cache_krope, 'page_table': page_table,
            'w_in0': w_in0, 's5_a_re': s5_a_re, 's5_a_im': s5_a_im, 's5_log_dt': s5_log_dt,
            's5_b_re': s5_b_re, 's5_b_im': s5_b_im, 's5_c_re': s5_c_re, 's5_c_im': s5_c_im,
            's5_d': s5_d, 's5_w_glu': s5_w_glu, 's5_b_glu': s5_b_glu,
            'ssd_conv_w': ssd_conv_w, 'ssd_conv_b': ssd_conv_b, 'ssd_dt_bias': ssd_dt_bias,
            'ssd_a_log': ssd_a_log, 'ssd_d': ssd_d, 'ssd_norm_g': ssd_norm_g, 'w_out0': w_out0,
            'w_in1': w_in1, 'mla_q_norm_g': mla_q_norm_g, 'mla_kv_norm_g': mla_kv_norm_g,
            'mla_w_uq': mla_w_uq, 'mla_w_uk': mla_w_uk, 'mla_w_uv': mla_w_uv, 'w_out1': w_out1,
            'ln1_g': ln1_g, 'ln1_b': ln1_b, 'ffn_w_up': ffn_w_up, 'ffn_conv_w': ffn_conv_w,
            'ffn_conv_b': ffn_conv_b, 'ffn_w_down': ffn_w_down, 'ln2_g': ln2_g, 'ln2_b': ln2_b}


def reference(x_prompt, x_sample, state_s5_re, state_s5_im, state_ssd, state_ssd_conv, state_ffn_conv,
              cache_ckv, cache_krope, page_table,
              w_in0, s5_a_re, s5_a_im, s5_log_dt, s5_b_re, s5_b_im, s5_c_re, s5_c_im, s5_d, s5_w_glu, s5_b_glu,
              ssd_conv_w, ssd_conv_b, ssd_dt_bias, ssd_a_log, ssd_d, ssd_norm_g, w_out0,
              w_in1, mla_q_norm_g, mla_kv_norm_g, mla_w_uq, mla_w_uk, mla_w_uv, w_out1,
              ln1_g, ln1_b, ffn_w_up, ffn_conv_w, ffn_conv_b, ffn_w_down, ln2_g, ln2_b):
    p = dict(w_in0=w_in0, s5_a_re=s5_a_re, s5_a_im=s5_a_im, s5_log_dt=s5_log_dt, s5_b_re=s5_b_re,
             s5_b_im=s5_b_im, s5_c_re=s5_c_re, s5_c_im=s5_c_im, s5_d=s5_d, s5_w_glu=s5_w_glu,
             s5_b_glu=s5_b_glu, ssd_conv_w=ssd_conv_w, ssd_conv_b=ssd_conv_b, ssd_dt_bias=ssd_dt_bias,
             ssd_a_log=ssd_a_log, ssd_d=ssd_d, ssd_norm_g=ssd_norm_g, w_out0=w_out0,
             w_in1=w_in1, mla_q_norm_g=mla_q_norm_g, mla_kv_norm_g=mla_kv_norm_g, mla_w_uq=mla_w_uq,
             mla_w_uk=mla_w_uk, mla_w_uv=mla_w_uv, w_out1=w_out1,
             ln1_g=ln1_g, ln1_b=ln1_b, ffn_w_up=ffn_w_up, ffn_conv_w=ffn_conv_w, ffn_conv_b=ffn_conv_b,
             ffn_w_down=ffn_w_down, ln2_g=ln2_g, ln2_b=ln2_b)
    bp = x_prompt.shape[0]
    dtp = x_prompt.dtype
    y_prompt, (s5_re_p, s5_im_p, ssd_p, ssd_conv_p, ckv_p, krope_p, ffn_conv_p) = run_group(
        x_prompt, 0,
        jnp.zeros((N_EVEN, bp, S5_GROUPS, S5_STATE), dtp), jnp.zeros((N_EVEN, bp, S5_GROUPS, S5_STATE), dtp),
        jnp.zeros((N_EVEN, bp, SSD_HEADS, SSD_HEADDIM, SSD_STATE), dtp),
        jnp.zeros((N_EVEN, bp, SSD_CONV - 1, SSD_CONV_DIM), dtp),
        jnp.zeros((DEPTH, bp, FFN_CONV - 1, 2 * D_FF), dtp),
        None, None, None, p)
    past_len = page_table.shape[1] * PAGE_SIZE
    y_sample, (s5_re_s, s5_im_s, ssd_s, ssd_conv_s, ckv_s, krope_s, ffn_conv_s) = run_group(
        x_sample, past_len, state_s5_re, state_s5_im, state_ssd, state_ssd_conv, state_ffn_conv,
        cache_ckv, cache_krope, page_table, p)
    return (y_prompt, y_sample,
            s5_re_p, s5_im_p, ssd_p, ssd_conv_p, ckv_p, krope_p, ffn_conv_p,
            s5_re_s, s5_im_s, ssd_s, ssd_conv_s, ckv_s, krope_s, ffn_conv_s)
```

```python
import math
from contextlib import ExitStack
import numpy as np
import concourse.bass as bass
import concourse.mybir as mybir
from concourse.bass_utils import run_bass_kernel_spmd

F32 = mybir.dt.float32
BF16 = mybir.dt.bfloat16
I32 = mybir.dt.int32
AF = mybir.ActivationFunctionType
ALU = mybir.AluOpType
AX = mybir.AxisListType

NCORES = 8
D = 1024
LP = 2048
NSQ = 4
LS = 8
NT = LP + NSQ * LS
NCH = NT // 8
DFF = 2816
ALPHA = 4 ** 0.25
EPS = 1e-5
SM_SCALE = 192 ** -0.5
NPAGES = 128

ENGS = ("pe", "act", "dve", "pool", "sp")
N_DMA_SLOTS = 24
SEM_EPOCH = 30000


class Op:
    __slots__ = ("eng", "fn", "deps", "is_dma", "sig", "dma_k", "idx", "sem_id", "sem_val", "slot_prev", "reads", "writes")

    def __init__(self, eng, fn, is_dma):
        self.eng = eng
        self.fn = fn
        self.is_dma = is_dma
        self.deps = set()
        self.sig = False
        self.dma_k = None
        self.sem_id = None
        self.sem_val = None
        self.slot_prev = None


def _flat(items):
    out = []
    for t in items:
        if isinstance(t, str):
            out.append(t)
        elif isinstance(t, View):
            out.extend(t.toks)
        else:
            out.extend(_flat(t))
    return out


class Prog:
    def __init__(self, nc):
        self.nc = nc
        self.ops = []
        self.cur = self.ops
        self.n_dma = 0

    def op(self, eng, fn, reads=(), writes=(), dma=False):
        o = Op(eng, fn, dma)
        o.reads = _flat(reads)
        o.writes = _flat(writes)
        self.cur.append(o)
        return o

    def capture(self):
        prog = self

        class _Cap:
            def __enter__(self_):
                self_.saved = prog.cur
                self_.lst = []
                prog.cur = self_.lst
                return self_.lst

            def __exit__(self_, *a):
                prog.cur = self_.saved
                return False
        return _Cap()

    def zip(self, *lists):
        lists = [l for l in lists if l]
        pos = [0] * len(lists)
        total = sum(len(l) for l in lists)
        for _ in range(total):
            best, bi = None, None
            for i, l in enumerate(lists):
                if pos[i] < len(l):
                    frac = (pos[i] + 0.5) / len(l)
                    if best is None or frac < best:
                        best, bi = frac, i
            self.cur.append(lists[bi][pos[bi]])
            pos[bi] += 1

    def finalize(self):
        last_w, readers = {}, {}
        for idx, o in enumerate(self.ops):
            o.idx = idx
            o.deps = set()
            for t in o.reads:
                w = last_w.get(t)
                if w is not None:
                    o.deps.add(w)
            for t in o.writes:
                w = last_w.get(t)
                if w is not None:
                    o.deps.add(w)
                for r in readers.get(t, ()):
                    o.deps.add(r)
            for t in o.reads:
                readers.setdefault(t, []).append(idx)
            for t in o.writes:
                last_w[t] = idx
                readers[t] = []
            o.deps.discard(idx)
            if o.is_dma:
                o.dma_k = self.n_dma
                self.n_dma += 1

    def pe(self, fn, reads=(), writes=()):
        return self.op("pe", fn, reads, writes)

    def act(self, fn, reads=(), writes=()):
        return self.op("act", fn, reads, writes)

    def dve(self, fn, reads=(), writes=()):
        return self.op("dve", fn, reads, writes)

    def pool(self, fn, reads=(), writes=()):
        return self.op("pool", fn, reads, writes)

    def dma(self, fn, reads=(), writes=(), eng="sp"):
        return self.op(eng, fn, reads, writes, dma=True)

    def emit(self, stack):
        nc = self.nc
        self.finalize()
        ops = self.ops
        for o in ops:
            for d in o.deps:
                p = ops[d]
                if p.is_dma:
                    continue
                if p.eng == "pe" and o.eng == "pe":
                    continue
                p.sig = True
        cnt = {e: 0 for e in ENGS}
        eng_sems = {e: [] for e in ENGS}
        for o in ops:
            if o.is_dma or not o.sig:
                continue
            ep = cnt[o.eng] // SEM_EPOCH
            while len(eng_sems[o.eng]) <= ep:
                eng_sems[o.eng].append(stack.enter_context(nc.semaphore(f"s_{o.eng}_{len(eng_sems[o.eng])}")))
            o.sem_id = eng_sems[o.eng][ep]
            o.sem_val = cnt[o.eng] % SEM_EPOCH + 1
            cnt[o.eng] += 1
        dma_sems = [stack.enter_context(nc.semaphore(f"s_dma_{i}")) for i in range(N_DMA_SLOTS)]
        slot_last = {}
        for o in ops:
            if o.is_dma:
                s = o.dma_k % N_DMA_SLOTS
                o.sem_id = dma_sems[s]
                o.sem_val = 16 * (o.dma_k // N_DMA_SLOTS + 1)
                o.slot_prev = slot_last.get(s)
                slot_last[s] = o.idx
        per_eng = {e: [o for o in ops if o.eng == e] for e in ENGS}
        block = stack.enter_context(nc.Block())

        def run(engine, lst):
            seen = {}
            for o in lst:
                waits = {}
                deps = set(o.deps)
                if o.is_dma and o.slot_prev is not None:
                    deps.add(o.slot_prev)
                for d in deps:
                    p = ops[d]
                    if p.sem_id is None:
                        continue
                    k = id(p.sem_id)
                    if k not in waits or waits[k][1] < p.sem_val:
                        waits[k] = (p.sem_id, p.sem_val)
                for k, (s, v) in waits.items():
                    if seen.get(k, 0) >= v:
                        continue
                    engine.wait_ge(s, v)
                    seen[k] = v
                ins = o.fn(engine)
                if o.is_dma:
                    ins.then_inc(o.sem_id, 16)
                elif o.sig:
                    ins.then_inc(o.sem_id, 1)

        @block.tensor
        def _(e):
            run(e, per_eng["pe"])

        @block.scalar
        def _(e):
            run(e, per_eng["act"])

        @block.vector
        def _(e):
            run(e, per_eng["dve"])

        @block.gpsimd
        def _(e):
            run(e, per_eng["pool"])

        @block.sync
        def _(e):
            run(e, per_eng["sp"])
            last = {}
            for o in ops:
                if o.is_dma:
                    last[id(o.sem_id)] = (o.sem_id, o.sem_val)
            for s, v in last.values():
                e.wait_ge(s, v)


class View:
    def __init__(self, ap, toks):
        self.ap = ap
        self.toks = toks

    def __getitem__(self, k):
        return self.ap[k]


class Arena:
    G = 512

    def __init__(self, nc, st, name, nbytes):
        self.name = name
        self.nbytes = nbytes
        self.t = st.enter_context(nc.sbuf_tensor(name, [128, nbytes // 4], F32))
        self.off = 0
        self.peak = 0
        self.top = nbytes

    def alloc_top(self, free_shape, dt=F32):
        n = 1
        for x in free_shape:
            n *= x
        esz = 4 if dt in (F32, I32) else 2
        nb = ((n * esz + self.G - 1) // self.G) * self.G
        self.top -= nb
        save = self.off
        self.off = self.top
        v = self.alloc(free_shape, dt, _top=True)
        self.off = save
        return v

    def mark(self):
        return self.off

    def reset(self, off=0):
        self.off = off

    def alloc(self, free_shape, dt=F32, _top=False):
        n = 1
        for s in free_shape:
            n *= s
        esz = 4 if dt in (F32, I32) else 2
        nb = n * esz
        start = self.off
        self.off = start + ((nb + self.G - 1) // self.G) * self.G
        if not _top:
            assert self.off <= self.top, f"arena {self.name} overflow {self.off} > {self.top}"
            self.peak = max(self.peak, self.off)
        ap = self.t[:, start // 4: start // 4 + (nb + 3) // 4]
        if dt != F32:
            ap = ap.bitcast(dt)
            ap = ap[:, 0:n]
        if len(free_shape) == 2:
            ap = ap.rearrange("p (a b) -> p a b", a=free_shape[0])
        elif len(free_shape) == 3:
            ap = ap.rearrange("p (a b c) -> p a b c", a=free_shape[0], b=free_shape[1])
        elif len(free_shape) == 4:
            ap = ap.rearrange("p (a b c d) -> p a b c d", a=free_shape[0], b=free_shape[1], c=free_shape[2])
        toks = [f"{self.name}.{g}" for g in range(start // self.G, (start + nb - 1) // self.G + 1)]
        return View(ap, toks)


def psb(b):
    return [f"ps{b}.{q}" for q in range(4)]


class Builder:
    def __init__(self, dbg=None, upto=99):
        self.dbgsrc = {}
        self.dbg = dbg or []
        self.upto = upto
        self.nc = bass.Bass("TRN2", target_bir_lowering=False)
        self.P = Prog(self.nc)
        self.ins = {}
        self.outs = {}

    def mark(self, name):
        if not hasattr(self, "marks"):
            self.marks = []
        self.marks.append((name, len(self.P.cur) if self.P.cur is self.P.ops else -1))

    def din(self, name, shape, dt=F32):
        t = self.nc.dram_tensor(name, list(shape), dt, kind="ExternalInput").ap()
        self.ins[name] = t
        return t

    def dout(self, name, shape, dt=F32):
        t = self.nc.dram_tensor(name, list(shape), dt, kind="ExternalOutput").ap()
        self.outs[name] = t
        return t

    def sbt(self, name, shape, dt=F32):
        t = self.st.enter_context(self.nc.sbuf_tensor(name, list(shape), dt))
        return View(t[tuple(slice(None) for _ in shape)], [name])

    def load(self, view, src_ap, eng="sp", wtoks=None, **kw):
        self.P.dma(lambda e: e.dma_start(out=view if not isinstance(view, View) else view.ap, in_=src_ap, **kw),
                   writes=wtoks if wtoks is not None else [view], eng=eng)

    def build(self):
        nc, P = self.nc, self.P
        with ExitStack() as st:
            self.st = st
            self.declare_io()
            self.alloc_persistent()
            self.stage_consts()
            self.stage_load_x()
            if self.upto >= 1:
                self.stage_s5()
            if self.upto >= 2:
                self.stage_ssd()
            if self.upto >= 3:
                g1, b1 = self.ln_params(self.ln1_g_d[0], self.ln1_b_d[0], 0)
                self.stage_outproj_ln(self.w_out0_d, g1, b1, self.mix, 0)
            if self.upto >= 4:
                g2, b2 = self.ln_params(self.ln2_g_d[0], self.ln2_b_d[0], 1)
                self.stage_ffn(0, g2, b2)
            if self.upto >= 5:
                g3, b3 = self.ln_params(self.ln1_g_d[1], self.ln1_b_d[1], 2)
                self.stage_mla(g3, b3)
            if self.upto >= 6:
                g4, b4 = self.ln_params(self.ln2_g_d[1], self.ln2_b_d[1], 3)
                self.ocnt = 0
                self.stage_ffn(1, g4, b4, final=True)
            self.stage_debug()
            P.emit(st)
        return nc

    def declare_io(self):
        di = self.din
        self.xp_d = di("xp", [LP, D])
        self.xs_d = di("xs", [NSQ * LS, D])
        self.ident_d = di("c_ident", [128, 128])
        self.utri_d = di("c_utri", [128, 128])
        self.mask2_d = di("c_mask2", [128, 2])
        self.bdmask_d = di("c_bdmask", [128, 128])
        self.w_in0_d = di("w_in0", [D, 2056])
        self.s5_a_re_d = di("s5_a_re", [2048])
        self.s5_a_im_d = di("s5_a_im", [2048])
        self.s5_log_dt_d = di("s5_log_dt", [32])
        self.s5_b_re_d = di("s5_b_re", [2048, 16])
        self.s5_b_im_d = di("s5_b_im", [2048, 16])
        self.s5_c_re_d = di("s5_c_re", [32, 16, 64])
        self.s5_c_im_d = di("s5_c_im", [32, 16, 64])
        self.s5_d_d = di("s5_d", [512])
        self.s5_w_glu_d = di("s5_w_glu", [512, 512])
        self.s5_b_glu_d = di("s5_b_glu", [512])
        self.s5re0_d = di("s5re0", [NSQ, 2048])
        self.s5im0_d = di("s5im0", [NSQ, 2048])
        self.ssd_conv_w_d = di("ssd_conv_w", [4, 1024])
        self.ssd_conv_b_d = di("ssd_conv_b", [1024])
        self.ssd_dt_bias_d = di("ssd_dt_bias", [8])
        self.ssd_a_log_d = di("ssd_a_log", [8])
        self.ssd_d_d = di("ssd_d", [8])
        self.ssd_norm_g_d = di("ssd_norm_g", [512])
        self.ssd0_d = di("ssd0", [NSQ, 512, 128])
        self.ssdconv0_d = di("ssdconv0", [NSQ * 3, 1024])
        self.w_out0_d = di("w_out0", [D, D])
        self.ln1_g_d = di("ln1_g", [2, D]); self.ln1_b_d = di("ln1_b", [2, D])
        self.ln2_g_d = di("ln2_g", [2, D]); self.ln2_b_d = di("ln2_b", [2, D])
        self.ffn_w_up_d = di("ffn_w_up", [2, D, 2 * DFF])
        self.ffn_conv_w_d = di("ffn_conv_w", [2, 3, 2 * DFF])
        self.ffn_conv_b_d = di("ffn_conv_b", [2, 2 * DFF])
        self.ffn_w_down_d = di("ffn_w_down", [2, DFF, D])
        self.ffnconv0_d = di("ffnconv0", [2, NSQ * 2, 2 * DFF])
        self.w_in1_d = di("w_in1", [D, 704])
        self.mla_q_norm_g_d = di("mla_q_norm_g", [384]); self.mla_kv_norm_g_d = di("mla_kv_norm_g", [256])
        self.mla_w_uq_d = di("mla_w_uq", [384, 1536])
        self.mla_w_uk_d = di("mla_w_uk", [256, 1024]); self.mla_w_uv_d = di("mla_w_uv", [256, 1024])
        self.w_out1_d = di("w_out1", [D, D])
        self.cache_ckv_d = di("cache_ckv", [5120 * 128, 256]); self.cache_krope_d = di("cache_krope", [5120 * 128, 64])
        self.page_table_d = di("page_table", [NSQ, NPAGES], I32)
        self.rope_cos_d = di("c_rope_cos", [NT, 32]); self.rope_sin_d = di("c_rope_sin", [NT, 32])
        self.smask_d = di("c_smask", [64, 8])
        do = self.dout
        self.o_ckv_p = do("o_ckv_p", [LP, 256]); self.o_krope_p = do("o_krope_p", [LP, 64])
        self.o_ckv_s = do("o_ckv_s", [NSQ * LS, 256]); self.o_krope_s = do("o_krope_s", [NSQ * LS, 64])
        self.o_yp = do("o_yp", [LP, D]); self.o_ys = do("o_ys", [NSQ * LS, D])
        self.o_ffnconv_p = do("o_ffnconv_p", [2, 2, 2 * DFF])
        self.o_ffnconv_s = do("o_ffnconv_s", [2, NSQ, 2, 2 * DFF])
        self.o_ssd_p = do("o_ssd_p", [512, 128])
        self.o_ssd_s = do("o_ssd_s", [NSQ, 512, 128])
        self.o_ssdconv_p = do("o_ssdconv_p", [3, 1024])
        self.o_ssdconv_s = do("o_ssdconv_s", [NSQ, 3, 1024])
        self.o_s5re_p = do("o_s5re_p", [2048])
        self.o_s5im_p = do("o_s5im_p", [2048])
        self.o_s5re_s = do("o_s5re_s", [NSQ, 2048])
        self.o_s5im_s = do("o_s5im_s", [NSQ, 2048])

    def alloc_persistent(self):
        nc, st = self.nc, self.st
        self.xhi = self.sbt("xhi", [128, 8, NT], BF16)
        self.xlo = self.sbt("xlo", [128, 8, NT], BF16)
        self.ident = self.sbt("ident", [128, 128])
        self.identb = self.sbt("identb", [128, 128], BF16)
        self.utri = self.sbt("utri", [128, 128])
        self.onesb = self.sbt("onesb", [128, 128], BF16)
        self.utrib = self.sbt("utrib", [128, 128], BF16)
        self.ones = self.sbt("ones", [128, 128])
        self.mask2 = self.sbt("mask2", [128, 2])
        self.bdmask = self.sbt("bdmask", [128, 128])
        self.A = Arena(nc, st, "A", 139 * 1024)
        self.ps = [st.enter_context(nc.psum_tensor(f"psum{b}", [128, 512], F32)) for b in range(8)]

    def stage_consts(self):
        P = self.P
        self.load(self.ident, self.ident_d[:, :])
        self.load(self.utri, self.utri_d[:, :])
        self.load(self.mask2, self.mask2_d[:, :])
        self.load(self.bdmask, self.bdmask_d[:, :])
        P.dve(lambda e: e.tensor_copy(out=self.identb.ap, in_=self.ident.ap), reads=[self.ident], writes=[self.identb])
        P.dve(lambda e: e.tensor_copy(out=self.utrib.ap, in_=self.utri.ap), reads=[self.utri], writes=[self.utrib])
        P.pool(lambda e: e.memset(self.onesb.ap, 1.0), writes=[self.onesb])
        P.pool(lambda e: e.memset(self.ones.ap, 1.0), writes=[self.ones])

    def stage_load_x(self):
        P, A = self.P, self.A
        m0 = A.mark()
        xin = [A.alloc([D]) for _ in range(3)]
        tmp = [A.alloc([4, 128]) for _ in range(2)]
        ntile = LP // 128 + 1
        cnt = 0
        for i in range(ntile):
            rows = 128 if i < LP // 128 else NSQ * LS
            c0 = i * 128
            xi = xin[i % 3]
            src = self.xp_d[c0:c0 + rows, :] if i < LP // 128 else self.xs_d[:, :]
            self.load(xi.ap[0:rows, :], src, wtoks=[xi])
            for half in range(2):
                b = cnt % 2
                cnt += 1
                ps = self.ps[b]
                for kk in range(4):
                    k = half * 4 + kk
                    P.pe(lambda e, ps=ps, kk=kk, k=k, xi=xi, rows=rows: e.transpose(
                        out=ps[:, kk * 128:kk * 128 + rows], in_=xi.ap[0:rows, k * 128:(k + 1) * 128],
                        identity=self.ident.ap[0:rows, 0:rows]), reads=[xi, self.ident], writes=psb(b))
                pv = ps[:, :].rearrange("p (a b) -> p a b", a=4)[:, :, 0:rows]
                hi = self.xhi.ap[:, half * 4:half * 4 + 4, c0:c0 + rows]
                lo = self.xlo.ap[:, half * 4:half * 4 + 4, c0:c0 + rows]
                tv = tmp[b]
                t = tv.ap[:, :, 0:rows]
                thi = self.xt("xhi", half * 4, half * 4 + 4, c0, c0 + rows)
                tlo = self.xt("xlo", half * 4, half * 4 + 4, c0, c0 + rows)
                P.act(lambda e, hi=hi, pv=pv: e.activation(out=hi, in_=pv, func=AF.Copy), reads=psb(b), writes=thi)
                P.dve(lambda e, t=t, pv=pv, hi=hi: e.tensor_tensor(out=t, in0=pv, in1=hi, op=ALU.subtract),
                      reads=psb(b) + thi, writes=[tv])
                P.pool(lambda e, lo=lo, t=t: e.tensor_copy(out=lo, in_=t), reads=[tv], writes=tlo)
        A.reset(m0)

    def xt(self, kind, m0, m1, c0, c1):
        return [f"{kind}.{m}.{ct}" for m in range(m0, m1) for ct in range(c0 // 128, (c1 - 1) // 128 + 1)]

    def ntiles(self):
        return [(0, 512), (512, 512), (1024, 512), (1536, 512), (2048, NT - 2048)]

    def stage_s5(self):
        P, A, nc = self.P, self.A, self.nc
        ps = self.ps
        self.mix = A.alloc_top([8, NT], BF16)
        mix = self.mix
        uTb = A.alloc([4, NT], BF16)
        WBT = A.alloc([8, 4, 2, 128], BF16)
        WC = A.alloc([8, 16, 2, 32], BF16)
        KT = A.alloc([8, 4, 128], BF16)
        wglu = A.alloc([4, 512], BF16)
        bglu = A.alloc([4])
        SM = A.alloc([66, 16])
        FS = A.alloc([2, 5, 16])
        h0 = A.alloc([2, 16, NSQ])
        dq = A.alloc([4])
        m_stage = A.mark()
        sm = lambda i: SM.ap[:, i, :]
        (I_ARE, I_AIM, I_LDT, I_DT, I_ARDT, I_TH, I_MAG, I_S, I_C, I_LR, I_LI, I_T1, I_T2, I_T3, I_T4,
         I_DEN, I_FR, I_FI, I_LM1) = range(19)
        I_L = 20
        I_M = 40
        I_L8N = 64

        def smop(eng, fn):
            self.P.op(eng, fn, reads=[SM], writes=[SM])

        def ld_small(i, src):
            P.dma(lambda e: e.dma_start(out=sm(i), in_=src, allow_slow_non_contiguous=True), writes=[SM])
        ld_small(I_ARE, self.s5_a_re_d.rearrange("(p q) -> q p", q=128))
        ld_small(I_AIM, self.s5_a_im_d.rearrange("(p q) -> q p", q=128))
        for g2 in range(2):
            src = self.s5_log_dt_d.rearrange("(p g) -> g p", g=2)[g2].partition_broadcast(64)
            P.dma(lambda e, g2=g2, src=src: e.dma_start(out=SM.ap[g2 * 64:(g2 + 1) * 64, I_LDT, :], in_=src,
                                                        allow_slow_non_contiguous=True), writes=[SM])
        P.dma(lambda e: e.dma_start(out=dq.ap, in_=self.s5_d_d.rearrange("(q p) -> p q", p=128),
                                    allow_slow_non_contiguous=True), writes=[dq])
        P.dma(lambda e: e.dma_start(out=bglu.ap, in_=self.s5_b_glu_d.rearrange("(q p) -> p q", p=128),
                                    allow_slow_non_contiguous=True), writes=[bglu])
        P.dma(lambda e: e.dma_start(out=wglu.ap, in_=self.s5_w_glu_d.rearrange("(k p) n -> p k n", p=128)),
              writes=[wglu], eng="pool")
        wu = A.alloc([8, 512], BF16)
        Bre = A.alloc([16, 16]); Bim = A.alloc([16, 16])
        BBre = A.alloc([16, 16]); BBim = A.alloc([16, 16])
        EBre = A.alloc([16, 16]); EBim = A.alloc([16, 16])
        T1 = A.alloc([16, 16]); T2 = A.alloc([16, 16])
        Ere = A.alloc([16, 32]); Eim = A.alloc([16, 32])
        ECre = A.alloc([16, 32]); ECimn = A.alloc([16, 32])
        CTre = A.alloc([16, 16]); CTim = A.alloc([16, 16])
        T3 = A.alloc([16, 32]); T4 = A.alloc([16, 32])
        craw = A.alloc([16, 128])
        h0raw = View(craw.ap.rearrange("p a b -> p (a b)"), craw.toks)
        P.dma(lambda e: e.dma_start(out=wu.ap, in_=self.w_in0_d[:, 0:512].rearrange("(k p) n -> p k n", p=128)),
              writes=[wu], eng="pool")
        P.dma(lambda e: e.dma_start(out=Bre.ap, in_=self.s5_b_re_d.rearrange("(p q) h -> q p h", q=128)), writes=[Bre])
        P.dma(lambda e: e.dma_start(out=Bim.ap, in_=self.s5_b_im_d.rearrange("(p q) h -> q p h", q=128)), writes=[Bim])

        for m in range(4):
            for ti, (c0, n) in enumerate(self.ntiles()):
                b = (m * 5 + ti) % 2
                for k in range(8):
                    P.pe(lambda e, b=b, m=m, k=k, c0=c0, n=n: e.matmul(
                        ps[b][:, 0:n], lhsT=wu.ap[:, k, m * 128:(m + 1) * 128], rhs=self.xhi.ap[:, k, c0:c0 + n],
                        start=(k == 0), stop=(k == 7)), reads=[wu] + self.xt("xhi", k, k + 1, c0, c0 + n), writes=psb(b))
                P.act(lambda e, b=b, m=m, c0=c0, n=n: e.activation(out=uTb.ap[:, m, c0:c0 + n], in_=ps[b][:, 0:n], func=AF.Copy),
                      reads=psb(b), writes=[uTb])

        V, G = "dve", "act"
        smop(G, lambda e: e.activation(out=sm(I_DT), in_=sm(I_LDT), func=AF.Exp))
        smop(V, lambda e: e.tensor_tensor(out=sm(I_ARDT), in0=sm(I_ARE), in1=sm(I_DT), op=ALU.mult))
        smop(V, lambda e: e.tensor_tensor(out=sm(I_TH), in0=sm(I_AIM), in1=sm(I_DT), op=ALU.mult))
        smop(G, lambda e: e.activation(out=sm(I_MAG), in_=sm(I_ARDT), func=AF.Exp, scale=1.0 / 16))
        smop(G, lambda e: e.activation(out=sm(I_S), in_=sm(I_TH), func=AF.Sin, scale=1.0 / 16))
        smop(V, lambda e: e.tensor_scalar(out=sm(I_T1), in0=sm(I_TH), scalar1=-1.0 / 16, scalar2=math.pi / 2,
                                          op0=ALU.mult, op1=ALU.add))
        smop(G, lambda e: e.activation(out=sm(I_C), in_=sm(I_T1), func=AF.Sin))
        smop(V, lambda e: e.tensor_tensor(out=sm(I_LR), in0=sm(I_MAG), in1=sm(I_C), op=ALU.mult))
        smop(V, lambda e: e.tensor_tensor(out=sm(I_LI), in0=sm(I_MAG), in1=sm(I_S), op=ALU.mult))

        def csq(ore, oim, ire, iim):
            smop(V, lambda e: e.tensor_tensor(out=sm(I_T1), in0=sm(ire), in1=sm(ire), op=ALU.mult))
            smop(V, lambda e: e.tensor_tensor(out=sm(I_T2), in0=sm(iim), in1=sm(iim), op=ALU.mult))
            smop(V, lambda e: e.scalar_tensor_tensor(out=sm(I_T3), in0=sm(ire), scalar=2.0, in1=sm(iim), op0=ALU.mult, op1=ALU.mult))
            smop(V, lambda e: e.tensor_tensor(out=sm(ore), in0=sm(I_T1), in1=sm(I_T2), op=ALU.subtract))
            smop(V, lambda e: e.tensor_copy(out=sm(oim), in_=sm(I_T3)))

        def cmul(ore, oim, are_, aim_, bre_, bim_):
            smop(V, lambda e: e.tensor_tensor(out=sm(I_T1), in0=sm(are_), in1=sm(bre_), op=ALU.mult))
            smop(V, lambda e: e.tensor_tensor(out=sm(I_T2), in0=sm(aim_), in1=sm(bim_), op=ALU.mult))
            smop(V, lambda e: e.tensor_tensor(out=sm(I_T3), in0=sm(are_), in1=sm(bim_), op=ALU.mult))
            smop(V, lambda e: e.tensor_tensor(out=sm(I_T4), in0=sm(aim_), in1=sm(bre_), op=ALU.mult))
            smop(V, lambda e: e.tensor_tensor(out=sm(ore), in0=sm(I_T1), in1=sm(I_T2), op=ALU.subtract))
            smop(V, lambda e: e.tensor_tensor(out=sm(oim), in0=sm(I_T3), in1=sm(I_T4), op=ALU.add))

        for _ in range(4):
            csq(I_LR, I_LI, I_LR, I_LI)
        smop("pool", lambda e: e.memset(sm(I_L + 0), 1.0))
        smop("pool", lambda e: e.memset(sm(I_L + 9), 0.0))
        for k in range(1, 9):
            cmul(I_L + k, I_L + 9 + k, I_L + k - 1, I_L + 9 + k - 1, I_LR, I_LI)
        smop(V, lambda e: e.tensor_tensor(out=sm(I_T1), in0=sm(I_ARE), in1=sm(I_ARE), op=ALU.mult))
        smop(V, lambda e: e.tensor_tensor(out=sm(I_T2), in0=sm(I_AIM), in1=sm(I_AIM), op=ALU.mult))
        smop(V, lambda e: e.tensor_tensor(out=sm(I_DEN), in0=sm(I_T1), in1=sm(I_T2), op=ALU.add))
        smop(V, lambda e: e.reciprocal(out=sm(I_DEN), in_=sm(I_DEN)))
        smop(V, lambda e: e.tensor_scalar(out=sm(I_LM1), in0=sm(I_LR), scalar1=-1.0, scalar2=None, op0=ALU.add))
        smop(V, lambda e: e.tensor_tensor(out=sm(I_T1), in0=sm(I_LM1), in1=sm(I_ARE), op=ALU.mult))
        smop(V, lambda e: e.tensor_tensor(out=sm(I_T2), in0=sm(I_LI), in1=sm(I_AIM), op=ALU.mult))
        smop(V, lambda e: e.tensor_tensor(out=sm(I_T1), in0=sm(I_T1), in1=sm(I_T2), op=ALU.add))
        smop(V, lambda e: e.tensor_tensor(out=sm(I_FR), in0=sm(I_T1), in1=sm(I_DEN), op=ALU.mult))
        smop(V, lambda e: e.tensor_tensor(out=sm(I_T1), in0=sm(I_LI), in1=sm(I_ARE), op=ALU.mult))
        smop(V, lambda e: e.tensor_tensor(out=sm(I_T2), in0=sm(I_LM1), in1=sm(I_AIM), op=ALU.mult))
        smop(V, lambda e: e.tensor_tensor(out=sm(I_T1), in0=sm(I_T1), in1=sm(I_T2), op=ALU.subtract))
        smop(V, lambda e: e.tensor_tensor(out=sm(I_FI), in0=sm(I_T1), in1=sm(I_DEN), op=ALU.mult))
        smop(V, lambda e: e.tensor_copy(out=sm(I_M), in_=sm(I_L + 8)))
        smop(V, lambda e: e.tensor_copy(out=sm(I_M + 8), in_=sm(I_L + 17)))
        for k in range(1, 8):
            csq(I_M + k, I_M + 8 + k, I_M + k - 1, I_M + 8 + k - 1)
        smop(V, lambda e: e.tensor_scalar(out=SM.ap[:, I_M + 16:I_M + 24, :], in0=SM.ap[:, I_M + 8:I_M + 16, :], scalar1=-1.0,
                                          scalar2=None, op0=ALU.mult))
        smop(V, lambda e: e.tensor_scalar(out=sm(I_L8N), in0=sm(I_L + 17), scalar1=-1.0, scalar2=None, op0=ALU.mult))

        def bcw(i, W):
            return sm(i).unsqueeze(2).to_broadcast([128, 16, W])

        def wide_cmul(ore, oim, xre, xim, ire, iim, W, t1, t2, neg_im=False, eng="dve"):
            op = P.dve if eng == "dve" else P.pool
            op(lambda e: e.tensor_tensor(out=t1.ap, in0=xre.ap, in1=bcw(ire, W), op=ALU.mult), reads=[xre, SM], writes=[t1])
            op(lambda e: e.tensor_tensor(out=t2.ap, in0=xim.ap, in1=bcw(iim, W), op=ALU.mult), reads=[xim, SM], writes=[t2])
            op(lambda e: e.tensor_tensor(out=ore.ap if isinstance(ore, View) else ore, in0=t1.ap, in1=t2.ap, op=ALU.subtract),
               reads=[t1, t2], writes=[ore] if isinstance(ore, View) else [WC])
            op(lambda e: e.tensor_tensor(out=t1.ap, in0=xre.ap, in1=bcw(iim, W), op=ALU.mult), reads=[xre, SM], writes=[t1])
            op(lambda e: e.tensor_tensor(out=t2.ap, in0=xim.ap, in1=bcw(ire, W), op=ALU.mult), reads=[xim, SM], writes=[t2])
            if neg_im:
                op(lambda e: e.scalar_tensor_tensor(out=oim.ap if isinstance(oim, View) else oim, in0=t1.ap, scalar=-1.0, in1=t2.ap,
                                                    op0=ALU.mult, op1=ALU.subtract),
                   reads=[t1, t2], writes=[oim] if isinstance(oim, View) else [WC])
            else:
                op(lambda e: e.tensor_tensor(out=oim.ap if isinstance(oim, View) else oim, in0=t1.ap, in1=t2.ap, op=ALU.add),
                   reads=[t1, t2], writes=[oim] if isinstance(oim, View) else [WC])

        def expand(dst, src):
            o = dst.ap.rearrange("p a (g h) -> p a g h", g=2)
            i0 = src.ap.unsqueeze(2).to_broadcast([128, 16, 2, 16])
            i1 = self.mask2.ap.unsqueeze(1).unsqueeze(3).to_broadcast([128, 16, 2, 16])
            P.dve(lambda e: e.tensor_tensor(out=o, in0=i0, in1=i1, op=ALU.mult), reads=[src, self.mask2], writes=[dst])

        wide_cmul(BBre, BBim, Bre, Bim, I_FR, I_FI, 16, T1, T2)
        for ri, (cd, CT) in enumerate(((self.s5_c_re_d, CTre), (self.s5_c_im_d, CTim))):
            P.dma(lambda e, cd=cd: e.dma_start(out=craw.ap[0:16, :, :].rearrange("h p (g n) -> h p g n", g=2),
                                               in_=cd.rearrange("(p g) h n -> h p g n", g=2)), writes=[craw])
            b = 2 + ri
            for pr in range(16):
                P.pe(lambda e, b=b, pr=pr: e.transpose(out=ps[b][:, pr * 16:(pr + 1) * 16], in_=craw.ap[0:16, pr, :],
                                                       identity=self.ident.ap[0:16, 0:16]),
                     reads=[craw, self.ident], writes=psb(b))
            P.act(lambda e, b=b, CT=CT: e.activation(out=CT.ap.rearrange("p a b -> p (a b)"), in_=ps[b][:, 0:256], func=AF.Copy),
                  reads=psb(b), writes=[CT])
        expand(ECre, CTre)
        expand(ECimn, CTim)
        P.dve(lambda e: e.tensor_scalar(out=ECimn.ap, in0=ECimn.ap, scalar1=-1.0, scalar2=None, op0=ALU.mult),
              reads=[ECimn], writes=[ECimn])
        for tau in range(8):
            lr, li = I_L + tau + 1, I_L + 9 + tau + 1
            ore = WC.ap[:, tau, :, 0, :]
            oim = WC.ap[:, tau, :, 1, :]
            P.dve(lambda e, lr=lr: e.tensor_tensor(out=T3.ap, in0=ECre.ap, in1=bcw(lr, 32), op=ALU.mult), reads=[ECre, SM], writes=[T3])
            P.dve(lambda e, li=li: e.tensor_tensor(out=T4.ap, in0=ECimn.ap, in1=bcw(li, 32), op=ALU.mult), reads=[ECimn, SM], writes=[T4])
            P.dve(lambda e, ore=ore: e.tensor_tensor(out=ore, in0=T3.ap, in1=T4.ap, op=ALU.add), reads=[T3, T4], writes=[WC])
            P.dve(lambda e, lr=lr: e.tensor_tensor(out=T3.ap, in0=ECimn.ap, in1=bcw(lr, 32), op=ALU.mult), reads=[ECimn, SM], writes=[T3])
            P.dve(lambda e, li=li: e.tensor_tensor(out=T4.ap, in0=ECre.ap, in1=bcw(li, 32), op=ALU.mult), reads=[ECre, SM], writes=[T4])
            P.dve(lambda e, oim=oim: e.tensor_tensor(out=oim, in0=T3.ap, in1=T4.ap, op=ALU.subtract), reads=[T3, T4], writes=[WC])
        for k in range(8):
            wide_cmul(EBre, EBim, BBre, BBim, I_L + k, I_L + 9 + k, 16, T1, T2)
            expand(Ere, EBre)
            expand(Eim, EBim)
            for ri, E in enumerate((Ere, Eim)):
                b = 4 + ri
                for q in range(4):
                    P.pe(lambda e, b=b, q=q, E=E: e.transpose(out=ps[b][:, q * 128:(q + 1) * 128],
                                                              in_=E.ap[:, 4 * q:4 * q + 4, :].rearrange("p a b -> p (a b)"),
                                                              identity=self.ident.ap), reads=[E, self.ident], writes=psb(b))
                P.act(lambda e, b=b, ri=ri, k=k: e.activation(out=WBT.ap[:, 7 - k, :, ri, :],
                                                              in_=ps[b][:, :].rearrange("p (q n) -> p q n", q=4), func=AF.Copy),
                      reads=psb(b), writes=[WBT])
            b = 6 + (k % 2)
            for q in range(4):
                P.pe(lambda e, b=b, q=q: e.matmul(ps[b][:, q * 128:(q + 1) * 128],
                                                  lhsT=Ere.ap[:, 4 * q:4 * q + 4, :].rearrange("p a b -> p (a b)"),
                                                  rhs=ECre.ap[:, 4 * q:4 * q + 4, :].rearrange("p a b -> p (a b)"),
                                                  start=True, stop=False), reads=[Ere, ECre], writes=psb(b))
                P.pe(lambda e, b=b, q=q: e.matmul(ps[b][:, q * 128:(q + 1) * 128],
                                                  lhsT=Eim.ap[:, 4 * q:4 * q + 4, :].rearrange("p a b -> p (a b)"),
                                                  rhs=ECimn.ap[:, 4 * q:4 * q + 4, :].rearrange("p a b -> p (a b)"),
                                                  start=False, stop=True), reads=[Eim, ECimn], writes=psb(b))
            bdm = self.bdmask.ap.unsqueeze(1).to_broadcast([128, 4, 128])
            if k == 0:
                P.dve(lambda e, b=b, bdm=bdm: e.tensor_tensor(out=T3.ap.rearrange("p a b -> p (a b)").rearrange("p (q n) -> p q n", q=4),
                                                             in0=ps[b][:, :].rearrange("p (q n) -> p q n", q=4), in1=bdm, op=ALU.mult),
                      reads=psb(b) + [self.bdmask], writes=[T3])
                for q in range(4):
                    P.dve(lambda e, q=q: e.scalar_tensor_tensor(out=KT.ap[:, 0, q, :], in0=self.ident.ap, scalar=dq.ap[:, q:q + 1],
                                                                in1=T3.ap.rearrange("p a b -> p (a b)")[:, q * 128:(q + 1) * 128],
                                                                op0=ALU.mult, op1=ALU.add),
                          reads=[T3, self.ident, dq], writes=[KT])
            else:
                P.dve(lambda e, b=b, k=k, bdm=bdm: e.tensor_tensor(out=KT.ap[:, k, :, :], in0=ps[b][:, :].rearrange("p (q n) -> p q n", q=4),
                                                                  in1=bdm, op=ALU.mult), reads=psb(b) + [self.bdmask], writes=[KT])
        for ri in range(2):
            b = 2 + ri
            srcd = self.s5re0_d if ri == 0 else self.s5im0_d
            P.dma(lambda e, srcd=srcd: e.dma_start(out=h0raw.ap[0:NSQ, :], in_=srcd[:, :]), writes=[h0raw])
            for pr in range(16):
                P.pe(lambda e, b=b, pr=pr, ri=ri: e.transpose(out=ps[b][:, pr * NSQ:(pr + 1) * NSQ],
                                                              in_=h0raw.ap[0:NSQ, pr * 128:(pr + 1) * 128],
                                                              identity=self.ident.ap[0:NSQ, 0:NSQ]),
                     reads=[h0raw, self.ident], writes=psb(b))
            P.act(lambda e, b=b, ri=ri: e.activation(out=h0.ap[:, ri, :, :].rearrange("p a b -> p (a b)"), in_=ps[b][:, 0:16 * NSQ],
                                                     func=AF.Copy), reads=psb(b), writes=[h0])

        A.reset(m_stage)
        AB = [[A.alloc([2, NCH]) for _ in range(2)] for _ in range(2)]
        HP = [A.alloc([4, 2, NCH], BF16) for _ in range(2)]
        sg = A.alloc([4, 512])

        def tpos(row, col):
            return (row, col)

        for quad in range(4):
            hp = HP[quad % 2]
            def pair_body(p4, quad=quad, hp=hp):
                j = quad * 4 + p4
                pb = 32 * p4
                a0, a1 = AB[j % 2]
                for ri in range(2):
                    b = ri + 2 * (j % 2)
                    for tau in range(8):
                        P.pe(lambda e, b=b, ri=ri, tau=tau, pb=pb, quad=quad: e.matmul(
                            ps[b][:, 0:NCH], lhsT=WBT.ap[pb:pb + 32, tau, quad, ri, :], rhs=uTb.ap[pb:pb + 32, quad, tau:NT:8],
                            start=(tau == 0), stop=(tau == 7), tile_position=(pb, 0)), reads=[WBT, uTb], writes=psb(b))
                    P.act(lambda e, b=b, ri=ri, a0=a0: e.activation(out=a0.ap[:, ri, :], in_=ps[b][:, 0:NCH], func=AF.Copy),
                          reads=psb(b), writes=[a0])
                l8r = SM.ap[:, I_L + 8, j:j + 1]
                l8i = SM.ap[:, I_L + 17, j:j + 1]
                l8in = SM.ap[:, I_L8N, j:j + 1]
                sre = a0.ap[:, 0, 256:260]
                sim = a0.ap[:, 1, 256:260]
                P.dve(lambda e, sre=sre, l8r=l8r, j=j: e.scalar_tensor_tensor(out=sre, in0=h0.ap[:, 0, j, :], scalar=l8r, in1=sre, op0=ALU.mult, op1=ALU.add),
                      reads=[a0, h0, SM], writes=[a0])
                P.dve(lambda e, sre=sre, l8in=l8in, j=j: e.scalar_tensor_tensor(out=sre, in0=h0.ap[:, 1, j, :], scalar=l8in, in1=sre, op0=ALU.mult, op1=ALU.add),
                      reads=[a0, h0, SM], writes=[a0])
                P.dve(lambda e, sim=sim, l8r=l8r, j=j: e.scalar_tensor_tensor(out=sim, in0=h0.ap[:, 1, j, :], scalar=l8r, in1=sim, op0=ALU.mult, op1=ALU.add),
                      reads=[a0, h0, SM], writes=[a0])
                P.dve(lambda e, sim=sim, l8i=l8i, j=j: e.scalar_tensor_tensor(out=sim, in0=h0.ap[:, 0, j, :], scalar=l8i, in1=sim, op0=ALU.mult, op1=ALU.add),
                      reads=[a0, h0, SM], writes=[a0])
                src, dst = a0, a1
                for lv in range(8):
                    d = 1 << lv
                    mr = SM.ap[:, I_M + lv, j:j + 1]
                    mi = SM.ap[:, I_M + 8 + lv, j:j + 1]
                    mn = SM.ap[:, I_M + 16 + lv, j:j + 1]
                    n = 256 - d
                    P.dve(lambda e, s=src, t=dst, mr=mr, d=d, n=n: e.scalar_tensor_tensor(
                        out=t.ap[:, 0, d:256], in0=s.ap[:, 0, 0:n], scalar=mr, in1=s.ap[:, 0, d:256], op0=ALU.mult, op1=ALU.add),
                        reads=[src, SM], writes=[dst])
                    P.dve(lambda e, s=src, t=dst, mn=mn, d=d, n=n: e.scalar_tensor_tensor(
                        out=t.ap[:, 0, d:256], in0=s.ap[:, 1, 0:n], scalar=mn, in1=t.ap[:, 0, d:256], op0=ALU.mult, op1=ALU.add),
                        reads=[src, dst, SM], writes=[dst])
                    P.dve(lambda e, s=src, t=dst, mr=mr, d=d, n=n: e.scalar_tensor_tensor(
                        out=t.ap[:, 1, d:256], in0=s.ap[:, 1, 0:n], scalar=mr, in1=s.ap[:, 1, d:256], op0=ALU.mult, op1=ALU.add),
                        reads=[src, SM], writes=[dst])
                    P.dve(lambda e, s=src, t=dst, mi=mi, d=d, n=n: e.scalar_tensor_tensor(
                        out=t.ap[:, 1, d:256], in0=s.ap[:, 0, 0:n], scalar=mi, in1=t.ap[:, 1, d:256], op0=ALU.mult, op1=ALU.add),
                        reads=[src, dst, SM], writes=[dst])
                    P.pool(lambda e, s=src, t=dst, d=d: e.tensor_copy(out=t.ap[:, :, 0:d], in_=s.ap[:, :, 0:d]), reads=[src], writes=[dst])
                    src, dst = dst, src
                fin = src
                P.pool(lambda e, fin=fin, j=j: e.tensor_copy(out=FS.ap[:, :, 0, j], in_=fin.ap[:, :, 255]), reads=[fin], writes=[FS])
                P.pool(lambda e, fin=fin, j=j: e.tensor_copy(out=FS.ap[:, :, 1:5, j], in_=fin.ap[:, :, 256:260]), reads=[fin], writes=[FS])
                P.pool(lambda e, hp=hp, p4=p4: e.memset(hp.ap[:, p4, :, 0:1], 0.0), writes=[hp])
                P.act(lambda e, hp=hp, p4=p4, fin=fin: e.activation(out=hp.ap[:, p4, :, 1:256], in_=fin.ap[:, :, 0:255], func=AF.Copy),
                      reads=[fin], writes=[hp])
                P.act(lambda e, hp=hp, p4=p4, j=j: e.activation(out=hp.ap[:, p4, :, 256:260], in_=h0.ap[:, :, j, :], func=AF.Copy),
                      reads=[h0], writes=[hp])
            caps = []
            for p4 in range(4):
                with P.capture() as cap_:
                    pair_body(p4)
                caps.append(cap_)
            P.zip(caps[0], caps[1])
            P.zip(caps[2], caps[3])
            for half in range(2):
                for t4 in range(4):
                    tau = half * 4 + t4
                    b = 4 + t4
                    for tp in range(tau + 1):
                        P.pe(lambda e, b=b, tau=tau, tp=tp, quad=quad: e.matmul(
                            ps[b][:, 0:NCH], lhsT=KT.ap[:, tau - tp, quad, :], rhs=uTb.ap[:, quad, tp:NT:8],
                            start=(tp == 0), stop=False), reads=[KT, uTb], writes=psb(b))
                    for p4 in range(4):
                        j = quad * 4 + p4
                        pb = 32 * p4
                        out = ps[b][pb:pb + 32, 0:NCH]
                        P.pe(lambda e, out=out, tau=tau, j=j, hp=hp, p4=p4, pb=pb: e.matmul(
                            out, lhsT=WC.ap[:, tau, j, 0, :], rhs=hp.ap[:, p4, 0, :], start=False, stop=False, tile_position=(0, pb)),
                            reads=[WC, hp], writes=psb(b))
                        P.pe(lambda e, out=out, tau=tau, j=j, hp=hp, p4=p4, pb=pb: e.matmul(
                            out, lhsT=WC.ap[:, tau, j, 1, :], rhs=hp.ap[:, p4, 1, :], start=False, stop=True, tile_position=(0, pb)),
                            reads=[WC, hp], writes=psb(b))
                for t4 in range(4):
                    tau = half * 4 + t4
                    b = 4 + t4
                    P.act(lambda e, b=b, tau=tau, quad=quad: e.activation(out=mix.ap[:, quad, tau:NT:8], in_=ps[b][:, 0:NCH],
                                                                           func=AF.Gelu_apprx_tanh), reads=psb(b), writes=[mix])
        for ti, (c0, n) in enumerate(self.ntiles()):
            for m in range(4):
                b = m % 4
                for k in range(4):
                    P.pe(lambda e, b=b, m=m, k=k, c0=c0, n=n: e.matmul(ps[b][:, 0:n], lhsT=wglu.ap[:, k, m * 128:(m + 1) * 128],
                                                                       rhs=mix.ap[:, k, c0:c0 + n], start=(k == 0), stop=(k == 3)),
                         reads=[wglu, mix], writes=psb(b))
                P.act(lambda e, b=b, m=m, n=n: e.activation(out=sg.ap[:, m, 0:n], in_=ps[b][:, 0:n], func=AF.Sigmoid,
                                                            bias=bglu.ap[:, m:m + 1]), reads=psb(b) + [bglu], writes=[sg])
            P.dve(lambda e, c0=c0, n=n: e.tensor_tensor(out=mix.ap[:, 0:4, c0:c0 + n], in0=mix.ap[:, 0:4, c0:c0 + n], in1=sg.ap[:, :, 0:n],
                                                        op=ALU.mult), reads=[mix, sg], writes=[mix])
        FT = A.alloc([2, 128])
        for ri in range(2):
            b = ri
            P.pe(lambda e, b=b, ri=ri: e.transpose(out=ps[b][0:80, 0:128], in_=FS.ap[:, ri, :, :].rearrange("p a b -> p (a b)"),
                                                   identity=self.ident.ap), reads=[FS, self.ident], writes=psb(b))
            P.act(lambda e, b=b, ri=ri: e.activation(out=FT.ap[0:80, ri, :], in_=ps[b][0:80, 0:128], func=AF.Copy),
                  reads=psb(b), writes=[FT])
            op_, os_ = (self.o_s5re_p, self.o_s5re_s) if ri == 0 else (self.o_s5im_p, self.o_s5im_s)
            P.dma(lambda e, ri=ri, op_=op_: e.dma_start(out=op_.rearrange("(p q) -> p q", q=128), in_=FT.ap[0:16, ri, :]),
                  reads=[FT], writes=["o_s5p%d" % ri])
            for s in range(NSQ):
                P.dma(lambda e, ri=ri, os_=os_, s=s: e.dma_start(out=os_[s].rearrange("(p q) -> p q", q=128),
                                                                 in_=FT.ap[16 * (s + 1):16 * (s + 2), ri, :]),
                      reads=[FT], writes=["o_s5s%d_%d" % (ri, s)])
        self.dbgsrc.update({"mix": (mix, [128, 8 * NT], BF16), "uTb": (uTb, [128, 4 * NT], BF16), "KT": (KT, [128, 8 * 4 * 128], BF16),
                       "WBT": (WBT, [128, 8 * 4 * 2 * 128], BF16), "WC": (WC, [128, 8 * 16 * 2 * 32], BF16), "SM": (SM, [128, 66 * 16], F32)})
        self.s5_end_mark = A.mark()

    def stage_ssd(self):
        P, A, nc = self.P, self.A, self.nc
        ps = self.ps
        mix = self.mix
        A.reset(0)
        NE = 2051 + 11 * NSQ
        xb = A.alloc([8, NE], BF16)
        wz = A.alloc([8, 512], BF16)
        wdt = A.alloc([8, 8], BF16)
        cw = A.alloc([8, 4]); cbias = A.alloc([8])
        cst = A.alloc([8, 12]); cso = A.alloc([8, 15])
        dtb = A.alloc([8]); aneg = A.alloc([8]); dsk = A.alloc([8]); ng = A.alloc([512])
        hT = A.alloc([512]); hTb = A.alloc([512], BF16)
        m_stage = A.mark()
        wx = A.alloc([8, 1024], BF16)
        ext = [A.alloc([NE]) for _ in range(2)]
        acc = [A.alloc([NE]) for _ in range(2)]
        craw = A.alloc([1024])
        ncd = lambda src: dict(in_=src, allow_slow_non_contiguous=True)
        P.dma(lambda e: e.dma_start(out=wx.ap, in_=self.w_in0_d[:, 1024:2048].rearrange("(k p) n -> p k n", p=128)), writes=[wx], eng="pool")
        P.dma(lambda e: e.dma_start(out=wz.ap, in_=self.w_in0_d[:, 512:1024].rearrange("(k p) n -> p k n", p=128)), writes=[wz], eng="pool")
        P.dma(lambda e: e.dma_start(out=wdt.ap, in_=self.w_in0_d[:, 2048:2056].rearrange("(k p) n -> p k n", p=128)), writes=[wdt], eng="pool")
        for k in range(4):
            P.dma(lambda e, k=k: e.dma_start(out=cw.ap[:, :, k], **ncd(self.ssd_conv_w_d[k].rearrange("(t p) -> p t", p=128))), writes=[cw])
        P.dma(lambda e: e.dma_start(out=cbias.ap, **ncd(self.ssd_conv_b_d.rearrange("(t p) -> p t", p=128))), writes=[cbias])
        P.dma(lambda e: e.dma_start(out=dtb.ap, in_=self.ssd_dt_bias_d.partition_broadcast(128)), writes=[dtb])
        P.dma(lambda e: e.dma_start(out=aneg.ap, in_=self.ssd_a_log_d.partition_broadcast(128)), writes=[aneg])
        P.dma(lambda e: e.dma_start(out=dsk.ap, in_=self.ssd_d_d.partition_broadcast(128)), writes=[dsk])
        P.dma(lambda e: e.dma_start(out=ng.ap, in_=self.ssd_norm_g_d.partition_broadcast(128)), writes=[ng])
        P.act(lambda e: e.activation(out=aneg.ap, in_=aneg.ap, func=AF.Exp), reads=[aneg], writes=[aneg])
        P.dve(lambda e: e.tensor_scalar(out=aneg.ap, in0=aneg.ap, scalar1=-1.0, scalar2=None, op0=ALU.mult), reads=[aneg], writes=[aneg])
        P.dma(lambda e: e.dma_start(out=craw.ap[0:12, :], in_=self.ssdconv0_d[:, :]), writes=[craw])
        for m in range(8):
            P.pe(lambda e, m=m: e.transpose(out=ps[0][:, m * 12:(m + 1) * 12], in_=craw.ap[0:12, m * 128:(m + 1) * 128],
                                            identity=self.ident.ap[0:12, 0:12]), reads=[craw, self.ident], writes=psb(0))
        P.act(lambda e: e.activation(out=cst.ap.rearrange("p a b -> p (a b)"), in_=ps[0][:, 0:96], func=AF.Copy), reads=psb(0), writes=[cst])
        for i in range(2):
            P.pool(lambda e, i=i: e.memset(ext[i].ap[:, 0:3], 0.0), writes=[ext[i]])
        for m in range(8):
            ex, ac = ext[m % 2], acc[m % 2]
            for ti, (c0, n) in enumerate(self.ntiles()):
                b = 1 + (m * 5 + ti) % 3
                for k in range(8):
                    P.pe(lambda e, b=b, m=m, k=k, c0=c0, n=n: e.matmul(
                        ps[b][:, 0:n], lhsT=wx.ap[:, k, m * 128:(m + 1) * 128], rhs=self.xhi.ap[:, k, c0:c0 + n],
                        start=(k == 0), stop=(k == 7)), reads=[wx] + self.xt("xhi", k, k + 1, c0, c0 + n), writes=psb(b))
                if c0 < LP:
                    P.act(lambda e, b=b, ex=ex, c0=c0, n=n: e.activation(out=ex.ap[:, 3 + c0:3 + c0 + n], in_=ps[b][:, 0:n], func=AF.Copy),
                          reads=psb(b), writes=[ex])
                else:
                    P.act(lambda e, b=b, ex=ex: e.activation(out=ex.ap[:, 2051:NE].rearrange("p (s c) -> p s c", c=11)[:, :, 3:11],
                                                             in_=ps[b][:, 0:NSQ * LS].rearrange("p (s c) -> p s c", c=LS), func=AF.Copy),
                          reads=psb(b), writes=[ex])
            P.pool(lambda e, ex=ex, m=m: e.tensor_copy(out=ex.ap[:, 2051:NE].rearrange("p (s c) -> p s c", c=11)[:, :, 0:3],
                                                       in_=cst.ap[:, m, :].rearrange("p (s c) -> p s c", c=3)), reads=[cst], writes=[ex])
            P.pool(lambda e, ex=ex, m=m: e.tensor_copy(out=cso.ap[:, m, 0:3], in_=ex.ap[:, 2048:2051]), reads=[ex], writes=[cso])
            P.pool(lambda e, ex=ex, m=m: e.tensor_copy(out=cso.ap[:, m, 3:15].rearrange("p (s c) -> p s c", c=3),
                                                       in_=ex.ap[:, 2051:NE].rearrange("p (s c) -> p s c", c=11)[:, :, 8:11]), reads=[ex], writes=[cso])
            P.act(lambda e, ex=ex, ac=ac, m=m: e.activation(out=ac.ap[:, 3:NE], in_=ex.ap[:, 3:NE], func=AF.Identity,
                                                            scale=cw.ap[:, m, 3:4], bias=cbias.ap[:, m:m + 1]), reads=[ex, cw, cbias], writes=[ac])
            for tap in range(3):
                P.dve(lambda e, ex=ex, ac=ac, m=m, tap=tap: e.scalar_tensor_tensor(
                    out=ac.ap[:, 3:NE], in0=ex.ap[:, tap:NE - 3 + tap], scalar=cw.ap[:, m, tap:tap + 1], in1=ac.ap[:, 3:NE],
                    op0=ALU.mult, op1=ALU.add), reads=[ex, ac, cw], writes=[ac])
            P.act(lambda e, ac=ac, m=m: e.activation(out=xb.ap[:, m, 3:NE], in_=ac.ap[:, 3:NE], func=AF.Silu), reads=[ac], writes=[xb])
        A.reset(m_stage)
        cso_t = A.alloc([1024])
        for m in range(8):
            b = 1 + m // 4
            P.pe(lambda e, b=b, m=m: e.transpose(out=ps[b][0:15, (m % 4) * 128:(m % 4 + 1) * 128], in_=cso.ap[:, m, :], identity=self.ident.ap),
                 reads=[cso, self.ident], writes=psb(b))
        for hb in range(2):
            P.act(lambda e, hb=hb: e.activation(out=cso_t.ap[0:15, hb * 512:(hb + 1) * 512], in_=ps[1 + hb][0:15, :], func=AF.Copy),
                  reads=psb(1 + hb), writes=[cso_t])
        P.dma(lambda e: e.dma_start(out=self.o_ssdconv_p[:, :], in_=cso_t.ap[0:3, :]), reads=[cso_t], writes=["o_ssdconv_p"])
        for s_ in range(NSQ):
            P.dma(lambda e, s_=s_: e.dma_start(out=self.o_ssdconv_s[s_], in_=cso_t.ap[3 + 3 * s_:6 + 3 * s_, :]), reads=[cso_t], writes=["o_ssdconv_s%d" % s_])

        dtr = A.alloc([8]); dt_ = A.alloc([8]); dtA = A.alloc([8]); cs = A.alloc([8]); te = A.alloc([8])
        ecT2 = [A.alloc([8]) for _ in range(2)]
        rhsR = A.alloc([8, 128])
        Ebuf = A.alloc([8, 128])
        Mb2 = [A.alloc([8, 128], BF16) for _ in range(2)]
        cbm = A.alloc([2, 128])
        Cd2 = [A.alloc([8, 128], BF16) for _ in range(2)]
        eR = A.alloc([8, 128])
        xs2 = [A.alloc([512], BF16) for _ in range(2)]
        Bt2 = [A.alloc([256], BF16) for _ in range(2)]
        xdt2 = [A.alloc([512], BF16) for _ in range(2)]
        xdte2 = [A.alloc([512], BF16) for _ in range(2)]
        sz2 = [A.alloc([512]) for _ in range(2)]
        y1 = A.alloc([512]); junk = A.alloc([512]); ms = A.alloc([2]); ys_tok = A.alloc([512], BF16)
        stio = A.alloc([4, 128])

        def bq(ap, shape):
            return ap.to_broadcast(shape)

        def front(ci, Tc, tc0, ec0):
            par = ci % 2
            Mb, Cd, xs_tok, B_tok, xdt, xdte, sz, ecT = Mb2[par], Cd2[par], xs2[par], Bt2[par], xdt2[par], xdte2[par], sz2[par], ecT2[par]
            T = slice(0, Tc)
            for k in range(8):
                P.pe(lambda e, k=k: e.matmul(ps[0][T, 0:8], lhsT=self.xhi.ap[:, k, tc0:tc0 + Tc], rhs=wdt.ap[:, k, :], start=(k == 0), stop=(k == 7)),
                     reads=[wdt] + self.xt("xhi", k, k + 1, tc0, tc0 + Tc), writes=psb(0))
            P.dve(lambda e: e.tensor_tensor(out=dtr.ap[T, :], in0=ps[0][T, 0:8], in1=dtb.ap[T, :], op=ALU.add), reads=psb(0) + [dtb], writes=[dtr])
            P.act(lambda e: e.activation(out=dtr.ap[T, :], in_=dtr.ap[T, :], func=AF.Exp), reads=[dtr], writes=[dtr])
            P.act(lambda e: e.activation(out=dt_.ap[T, :], in_=dtr.ap[T, :], func=AF.Ln, bias=1.0), reads=[dtr], writes=[dt_])
            P.dve(lambda e: e.tensor_tensor(out=dtA.ap[T, :], in0=dt_.ap[T, :], in1=aneg.ap[T, :], op=ALU.mult), reads=[dt_, aneg], writes=[dtA])
            P.pe(lambda e: e.matmul(ps[0][T, 8:16], lhsT=self.utri.ap[T, T], rhs=dtA.ap[T, :], start=True, stop=True), reads=[self.utri, dtA], writes=psb(0))
            P.act(lambda e: e.activation(out=cs.ap[T, :], in_=ps[0][T, 8:16], func=AF.Copy), reads=psb(0), writes=[cs])
            P.dve(lambda e: e.tensor_tensor(out=rhsR.ap[T, :, T], in0=bq(self.utri.ap[T, T].unsqueeze(1), [Tc, 8, Tc]),
                                            in1=bq(dtA.ap[T, :].unsqueeze(2), [Tc, 8, Tc]), op=ALU.mult), reads=[self.utri, dtA], writes=[rhsR])
            for hb in range(2):
                P.pe(lambda e, hb=hb: e.matmul(ps[1 + hb][:, 0:4 * Tc].rearrange("p (h t) -> p h t", h=4), lhsT=self.ones.ap[T, :],
                                               rhs=rhsR.ap[T, 4 * hb:4 * hb + 4, T], start=True, stop=True), reads=[self.ones, rhsR], writes=psb(1 + hb))
            Rv = [ps[1 + hb][:, 0:4 * Tc].rearrange("p (h t) -> p h t", h=4) for hb in range(2)]
            for hb in range(2):
                P.dve(lambda e, hb=hb: e.tensor_tensor(out=Ebuf.ap[T, 4 * hb:4 * hb + 4, T], in0=Rv[hb][T, :, :],
                                                       in1=bq(cs.ap[T, 4 * hb:4 * hb + 4].unsqueeze(2), [Tc, 4, Tc]), op=ALU.subtract),
                      reads=psb(1 + hb) + [cs], writes=[Ebuf])
            P.dve(lambda e: e.tensor_scalar(out=Ebuf.ap[T, :, T], in0=Ebuf.ap[T, :, T], scalar1=0.0, scalar2=None, op0=ALU.min), reads=[Ebuf], writes=[Ebuf])
            P.act(lambda e: e.activation(out=Ebuf.ap[T, :, T], in_=Ebuf.ap[T, :, T], func=AF.Exp), reads=[Ebuf], writes=[Ebuf])
            for g in range(2):
                P.pe(lambda e, g=g: e.matmul(ps[3][T, g * 128:g * 128 + Tc], lhsT=xb.ap[:, 4 + g, ec0:ec0 + Tc], rhs=xb.ap[:, 6 + g, ec0:ec0 + Tc],
                                             start=True, stop=True), reads=[xb], writes=psb(3))
            P.dve(lambda e: e.tensor_tensor(out=cbm.ap[T, :, T], in0=ps[3][T, 0:256].rearrange("p (g t) -> p g t", g=2)[:, :, T],
                                            in1=bq(self.utri.ap[T, T].unsqueeze(1), [Tc, 2, Tc]), op=ALU.mult), reads=psb(3) + [self.utri], writes=[cbm])
            for g in range(2):
                P.dve(lambda e, g=g: e.tensor_tensor(out=Mb.ap[T, 4 * g:4 * g + 4, T], in0=Ebuf.ap[T, 4 * g:4 * g + 4, T],
                                                     in1=bq(cbm.ap[T, g, T].unsqueeze(1), [Tc, 4, Tc]), op=ALU.mult), reads=[Ebuf, cbm], writes=[Mb])
            for hb in range(2):
                P.act(lambda e, hb=hb: e.activation(out=eR.ap[:, 4 * hb:4 * hb + 4, T], in_=Rv[hb], func=AF.Exp), reads=psb(1 + hb), writes=[eR])
            for g in range(2):
                P.dve(lambda e, g=g: e.tensor_tensor(out=Cd.ap[:, 4 * g:4 * g + 4, T], in0=eR.ap[:, 4 * g:4 * g + 4, T],
                                                     in1=bq(xb.ap[:, 6 + g, ec0:ec0 + Tc].unsqueeze(1), [128, 4, Tc]), op=ALU.mult), reads=[eR, xb], writes=[Cd])
            for hb in range(2):
                P.dve(lambda e, hb=hb: e.tensor_tensor(out=te.ap[T, 4 * hb:4 * hb + 4], in0=Rv[hb][T, :, Tc - 1], in1=cs.ap[T, 4 * hb:4 * hb + 4],
                                                       op=ALU.subtract), reads=psb(1 + hb) + [cs], writes=[te])
                P.act(lambda e, hb=hb: e.activation(out=ecT.ap[:, 4 * hb:4 * hb + 4], in_=Rv[hb][:, :, Tc - 1], func=AF.Exp), reads=psb(1 + hb), writes=[ecT])
            P.act(lambda e: e.activation(out=te.ap[T, :], in_=te.ap[T, :], func=AF.Exp), reads=[te], writes=[te])
            pt = ps[4][:, :].bitcast(BF16)
            for m in range(6):
                P.pe(lambda e, m=m: e.transpose(out=pt[T, m * 128:(m + 1) * 128], in_=xb.ap[:, m, ec0:ec0 + Tc], identity=self.identb.ap),
                     reads=[xb, self.identb], writes=psb(4))
            P.act(lambda e: e.activation(out=xs_tok.ap[T, :], in_=pt[T, 0:512], func=AF.Copy), reads=psb(4), writes=[xs_tok])
            P.act(lambda e: e.activation(out=B_tok.ap[T, :], in_=pt[T, 512:768], func=AF.Copy), reads=psb(4), writes=[B_tok])
            P.dve(lambda e: e.tensor_tensor(out=xdt.ap[T, :].rearrange("p (h q) -> p h q", h=8), in0=xs_tok.ap[T, :].rearrange("p (h q) -> p h q", h=8),
                                            in1=bq(dt_.ap[T, :].unsqueeze(2), [Tc, 8, 64]), op=ALU.mult), reads=[xs_tok, dt_], writes=[xdt])
            P.dve(lambda e: e.tensor_tensor(out=xdte.ap[T, :].rearrange("p (h q) -> p h q", h=8), in0=xdt.ap[T, :].rearrange("p (h q) -> p h q", h=8),
                                            in1=bq(te.ap[T, :].unsqueeze(2), [Tc, 8, 64]), op=ALU.mult), reads=[xdt, te], writes=[xdte])
            for k in range(8):
                P.pe(lambda e, k=k: e.matmul(ps[7][T, :], lhsT=self.xhi.ap[:, k, tc0:tc0 + Tc], rhs=wz.ap[:, k, :], start=(k == 0), stop=(k == 7)),
                     reads=[wz] + self.xt("xhi", k, k + 1, tc0, tc0 + Tc), writes=psb(7))
            P.act(lambda e: e.activation(out=sz.ap[T, :], in_=ps[7][T, :], func=AF.Silu), reads=psb(7), writes=[sz])

        def back(ci, Tc, tc0):
            par = ci % 2
            Mb, Cd, xs_tok, B_tok, xdt, xdte, sz, ecT = Mb2[par], Cd2[par], xs2[par], Bt2[par], xdt2[par], xdte2[par], sz2[par], ecT2[par]
            T = slice(0, Tc)
            for h in range(8):
                P.pe(lambda e, h=h: e.matmul(ps[5][T, 64 * h:64 * h + 64], lhsT=Mb.ap[T, h, T], rhs=xdt.ap[T, 64 * h:64 * h + 64], start=True, stop=False),
                     reads=[Mb, xdt], writes=psb(5))
                P.pe(lambda e, h=h: e.matmul(ps[5][T, 64 * h:64 * h + 64], lhsT=Cd.ap[:, h, T], rhs=hTb.ap[:, 64 * h:64 * h + 64], start=False, stop=True),
                     reads=[Cd, hTb], writes=psb(5))
            for g in range(2):
                P.pe(lambda e, g=g: e.matmul(ps[6][:, 256 * g:256 * g + 256], lhsT=B_tok.ap[T, 128 * g:128 * g + 128], rhs=xdte.ap[T, 256 * g:256 * g + 256],
                                             start=True, stop=True), reads=[B_tok, xdte], writes=psb(6))
            P.dve(lambda e: e.tensor_tensor(out=hT.ap.rearrange("p (h q) -> p h q", h=8), in0=hT.ap.rearrange("p (h q) -> p h q", h=8),
                                            in1=bq(ecT.ap.unsqueeze(2), [128, 8, 64]), op=ALU.mult), reads=[hT, ecT], writes=[hT])
            P.dve(lambda e: e.tensor_tensor(out=hT.ap, in0=hT.ap, in1=ps[6][:, :], op=ALU.add), reads=[hT] + psb(6), writes=[hT])
            P.pool(lambda e: e.tensor_copy(out=hTb.ap, in_=hT.ap), reads=[hT], writes=[hTb])
            P.dve(lambda e: e.tensor_tensor(out=y1.ap[T, :].rearrange("p (h q) -> p h q", h=8), in0=xs_tok.ap[T, :].rearrange("p (h q) -> p h q", h=8),
                                            in1=bq(dsk.ap[T, :].unsqueeze(2), [Tc, 8, 64]), op=ALU.mult), reads=[xs_tok, dsk], writes=[y1])
            P.dve(lambda e: e.tensor_tensor(out=y1.ap[T, :], in0=y1.ap[T, :], in1=ps[5][T, :], op=ALU.add), reads=[y1] + psb(5), writes=[y1])
            P.dve(lambda e: e.tensor_tensor(out=y1.ap[T, :], in0=y1.ap[T, :], in1=sz.ap[T, :], op=ALU.mult), reads=[y1, sz], writes=[y1])
            for g in range(2):
                P.act(lambda e, g=g: e.activation(out=junk.ap[T, 256 * g:256 * g + 256], in_=y1.ap[T, 256 * g:256 * g + 256], func=AF.Square,
                                                  accum_out=ms.ap[T, g:g + 1]), reads=[y1], writes=[junk, ms])
            P.dve(lambda e: e.tensor_scalar(out=ms.ap[T, :], in0=ms.ap[T, :], scalar1=1.0 / 256, scalar2=EPS, op0=ALU.mult, op1=ALU.add), reads=[ms], writes=[ms])
            P.act(lambda e: e.activation(out=ms.ap[T, :], in_=ms.ap[T, :], func=AF.Sqrt), reads=[ms], writes=[ms])
            P.dve(lambda e: e.reciprocal(out=ms.ap[T, :], in_=ms.ap[T, :]), reads=[ms], writes=[ms])
            P.dve(lambda e: e.tensor_tensor(out=y1.ap[T, :].rearrange("p (g q) -> p g q", g=2), in0=y1.ap[T, :].rearrange("p (g q) -> p g q", g=2),
                                            in1=bq(ms.ap[T, :].unsqueeze(2), [Tc, 2, 256]), op=ALU.mult), reads=[y1, ms], writes=[y1])
            P.dve(lambda e: e.tensor_tensor(out=ys_tok.ap[T, :], in0=y1.ap[T, :], in1=ng.ap[T, :], op=ALU.mult), reads=[y1, ng], writes=[ys_tok])
            pt = ps[6][:, :].bitcast(BF16)
            for c in range(4):
                P.pe(lambda e, c=c: e.transpose(out=pt[:, c * 128:c * 128 + Tc], in_=ys_tok.ap[T, c * 128:(c + 1) * 128], identity=self.identb.ap[T, T]),
                     reads=[ys_tok, self.identb], writes=psb(6))
            P.act(lambda e: e.activation(out=mix.ap[:, 4:8, tc0:tc0 + Tc], in_=pt[:, 0:512].rearrange("p (c t) -> p c t", c=4)[:, :, T], func=AF.Copy),
                  reads=psb(6), writes=[mix])

        def state_out(dst):
            for c in range(4):
                P.pe(lambda e, c=c: e.transpose(out=ps[6][:, c * 128:(c + 1) * 128], in_=hT.ap[:, c * 128:(c + 1) * 128], identity=self.ident.ap),
                     reads=[hT, self.ident], writes=psb(6))
            P.act(lambda e: e.activation(out=stio.ap.rearrange("p a b -> p (a b)"), in_=ps[6][:, :], func=AF.Copy), reads=psb(6), writes=[stio])
            P.dma(lambda e: e.dma_start(out=dst.rearrange("(c p) n -> p c n", p=128), in_=stio.ap), reads=[stio], writes=["o_ssd_state"])

        def state_in(s_):
            P.dma(lambda e: e.dma_start(out=stio.ap, in_=self.ssd0_d[s_].rearrange("(c p) n -> p c n", p=128)), writes=[stio])
            for c in range(4):
                P.pe(lambda e, c=c: e.transpose(out=ps[6][:, c * 128:(c + 1) * 128], in_=stio.ap[:, c, :], identity=self.ident.ap),
                     reads=[stio, self.ident], writes=psb(6))
            P.act(lambda e: e.activation(out=hT.ap, in_=ps[6][:, :], func=AF.Copy), reads=psb(6), writes=[hT])
            P.pool(lambda e: e.tensor_copy(out=hTb.ap, in_=hT.ap), reads=[hT], writes=[hTb])

        P.pool(lambda e: e.memset(hT.ap, 0.0), writes=[hT])
        P.pool(lambda e: e.memset(hTb.ap, 0.0), writes=[hTb])
        chunks = [(128, i * 128, 3 + i * 128, None) for i in range(LP // 128)] + [(LS, LP + LS * s_, 2051 + 11 * s_ + 3, s_) for s_ in range(NSQ)]
        front(0, *chunks[0][0:3])
        for ci, (Tc, tc0, ec0, s_) in enumerate(chunks):
            with P.capture() as la:
                if s_ is not None:
                    state_in(s_)
                back(ci, Tc, tc0)
                if ci == LP // 128 - 1:
                    state_out(self.o_ssd_p)
                if s_ is not None:
                    state_out(self.o_ssd_s[s_])
            with P.capture() as lb:
                if ci + 1 < len(chunks):
                    front(ci + 1, *chunks[ci + 1][0:3])
            P.zip(la, lb)
        self.dbgsrc["mix"] = (mix, [128, 8 * NT], BF16)
        self.dbgsrc["xb"] = (xb, [128, 8 * NE], BF16)

    def ln_params(self, g_d, b_d, idx):
        g = self.sbt(f"lng{idx}", [128, 8])
        b = self.sbt(f"lnb{idx}", [128, 8])
        self.P.dma(lambda e: e.dma_start(out=g.ap, in_=g_d.rearrange("(m p) -> p m", p=128), allow_slow_non_contiguous=True), writes=[g])
        self.P.dma(lambda e: e.dma_start(out=b.ap, in_=b_d.rearrange("(m p) -> p m", p=128), allow_slow_non_contiguous=True), writes=[b])
        return g, b

    def ln_tile(self, R, c0, n, g, b, tmp, final_out=None, stat_banks=(6, 7)):
        P, ps = self.P, self.ps
        rb, rq, mt, msq = tmp
        b6, b7 = stat_banks
        Rn = R.ap[:, :, 0:n]
        P.act(lambda e: e.activation(out=rb.ap[:, :, 0:n], in_=Rn, func=AF.Copy), reads=[R], writes=[rb])
        P.act(lambda e: e.activation(out=rq.ap[:, :, 0:n], in_=Rn, func=AF.Square), reads=[R], writes=[rq])
        for m in range(8):
            P.pe(lambda e, m=m: e.matmul(ps[b6][:, 0:n], lhsT=self.onesb.ap, rhs=rb.ap[:, m, 0:n], start=(m == 0), stop=(m == 7)),
                 reads=[self.onesb, rb], writes=psb(b6))
        for m in range(8):
            P.pe(lambda e, m=m: e.matmul(ps[b7][:, 0:n], lhsT=self.onesb.ap, rhs=rq.ap[:, m, 0:n], start=(m == 0), stop=(m == 7)),
                 reads=[self.onesb, rq], writes=psb(b7))
        P.dve(lambda e: e.tensor_scalar(out=mt.ap[:, 0:n], in0=ps[b6][:, 0:n], scalar1=1.0 / D, scalar2=None, op0=ALU.mult), reads=psb(b6), writes=[mt])
        P.dve(lambda e: e.tensor_tensor(out=msq.ap[:, 0:n], in0=mt.ap[:, 0:n], in1=mt.ap[:, 0:n], op=ALU.mult), reads=[mt], writes=[msq])
        P.dve(lambda e: e.scalar_tensor_tensor(out=msq.ap[:, 0:n], in0=ps[b7][:, 0:n], scalar=1.0 / D, in1=msq.ap[:, 0:n], op0=ALU.mult, op1=ALU.subtract),
              reads=psb(b7) + [msq], writes=[msq])
        P.dve(lambda e: e.tensor_scalar(out=msq.ap[:, 0:n], in0=msq.ap[:, 0:n], scalar1=EPS, scalar2=None, op0=ALU.add), reads=[msq], writes=[msq])
        P.act(lambda e: e.activation(out=msq.ap[:, 0:n], in_=msq.ap[:, 0:n], func=AF.Sqrt), reads=[msq], writes=[msq])
        P.dve(lambda e: e.reciprocal(out=msq.ap[:, 0:n], in_=msq.ap[:, 0:n]), reads=[msq], writes=[msq])
        P.dve(lambda e: e.tensor_tensor(out=Rn, in0=Rn, in1=mt.ap[:, 0:n].unsqueeze(1).to_broadcast([128, 8, n]), op=ALU.subtract), reads=[R, mt], writes=[R])
        P.dve(lambda e: e.tensor_tensor(out=Rn, in0=Rn, in1=msq.ap[:, 0:n].unsqueeze(1).to_broadcast([128, 8, n]), op=ALU.mult), reads=[R, msq], writes=[R])
        for m in range(8):
            P.act(lambda e, m=m: e.activation(out=R.ap[:, m, 0:n], in_=R.ap[:, m, 0:n], func=AF.Identity, scale=g.ap[:, m:m + 1], bias=b.ap[:, m:m + 1]),
                  reads=[R, g, b], writes=[R])
        if final_out is None:
            thi = self.xt("xhi", 0, 8, c0, c0 + n)
            tlo = self.xt("xlo", 0, 8, c0, c0 + n)
            hi = self.xhi.ap[:, :, c0:c0 + n]
            P.act(lambda e: e.activation(out=hi, in_=Rn, func=AF.Copy), reads=[R], writes=thi)
            P.dve(lambda e: e.tensor_tensor(out=Rn, in0=Rn, in1=hi, op=ALU.subtract), reads=[R] + thi, writes=[R])
            P.pool(lambda e: e.tensor_copy(out=self.xlo.ap[:, :, c0:c0 + n], in_=Rn), reads=[R], writes=tlo)
        else:
            final_out(R, c0, n)

    def resid_into(self, Rdst, psrc, m, c0, n):
        P = self.P
        P.dve(lambda e: e.scalar_tensor_tensor(out=Rdst, in0=self.xhi.ap[:, m, c0:c0 + n], scalar=ALPHA, in1=psrc[0], op0=ALU.mult, op1=ALU.add),
              reads=self.xt("xhi", m, m + 1, c0, c0 + n) + psrc[1], writes=psrc[2])
        P.dve(lambda e: e.scalar_tensor_tensor(out=Rdst, in0=self.xlo.ap[:, m, c0:c0 + n], scalar=ALPHA, in1=Rdst, op0=ALU.mult, op1=ALU.add),
              reads=self.xt("xlo", m, m + 1, c0, c0 + n) + psrc[2], writes=psrc[2])

    def stage_outproj_ln(self, w_d, g, b, rhs_view, lo_mark):
        P, A, ps = self.P, self.A, self.ps
        A.reset(lo_mark)
        wo = A.alloc([8, D], BF16)
        Rs = [A.alloc([8, 512]) for _ in range(2)]
        tmps = [(A.alloc([8, 512], BF16), A.alloc([8, 512], BF16), A.alloc([512]), A.alloc([512])) for _ in range(2)]
        P.dma(lambda e: e.dma_start(out=wo.ap, in_=w_d.rearrange("(k p) n -> p k n", p=128)), writes=[wo], eng="pool")
        caps = []
        for ti, (c0, n) in enumerate(self.ntiles()):
            par = ti % 2
            R, tmp = Rs[par], tmps[par]
            banks = (0, 1) if par == 0 else (2, 3)
            stat = (6, 7) if par == 0 else (4, 5)
            with P.capture() as cap_:
                for m in range(8):
                    bnk = banks[m % 2]
                    for k in range(8):
                        P.pe(lambda e, bnk=bnk, m=m, k=k, c0=c0, n=n: e.matmul(ps[bnk][:, 0:n], lhsT=wo.ap[:, k, m * 128:(m + 1) * 128],
                                                                               rhs=rhs_view.ap[:, k, c0:c0 + n], start=(k == 0), stop=(k == 7)),
                             reads=[wo, rhs_view], writes=psb(bnk))
                    self.resid_into(R.ap[:, m, 0:n], (ps[bnk][:, 0:n], psb(bnk), [R]), m, c0, n)
                self.ln_tile(R, c0, n, g, b, tmp, stat_banks=stat)
            caps.append(cap_)
        P.zip(caps[0], caps[1])
        P.zip(caps[2], caps[3])
        P.zip(caps[4])

    def stage_ffn(self, layer, g, b, final=False):
        P, A, ps = self.P, self.A, self.ps
        A.top = A.nbytes
        A.reset(0)
        wup_d = self.ffn_w_up_d[layer]
        wdn_d = self.ffn_w_down_d[layer]
        NEA, NEB = 1026, 1026 + 10 * NSQ
        G = A.alloc([22, NEB], BF16)
        Rbig = A.alloc([8, NEB - 2])
        wd = [A.alloc([22, 128], BF16) for _ in range(2)]
        fcw = A.alloc([44, 3]); fcb = A.alloc([44]); fst = A.alloc([44, 2 * NSQ]); fco = A.alloc([44, 2 + 2 * NSQ]); hsave = A.alloc([44, 2])
        m_t = A.mark()
        wbl = [A.alloc([8, 512], BF16) for _ in range(3)]
        hraw = [A.alloc([NEB]) for _ in range(2)]
        acc = [A.alloc([NEB]) for _ in range(2)]
        m_end = A.mark()
        A.reset(m_t)
        if final:
            tmp = (A.alloc([8, 512], BF16), A.alloc([8, 512], BF16), A.alloc([512]), A.alloc([512]))
            tmp2 = None
        else:
            tmp = (A.alloc([8, 256], BF16), A.alloc([8, 256], BF16), A.alloc([256]), A.alloc([256]))
            tmp2 = (A.alloc([8, 256], BF16), A.alloc([8, 256], BF16), A.alloc([256]), A.alloc([256]))
        io_t = A.alloc([1408])
        if final:
            self.obuf = [A.alloc([D]) for _ in range(2)]
        assert A.mark() <= m_end + 8192
        A.reset(max(m_end, A.mark()))
        ncd = lambda src: dict(in_=src, allow_slow_non_contiguous=True)
        for k in range(3):
            P.dma(lambda e, k=k: e.dma_start(out=fcw.ap[:, :, k], **ncd(self.ffn_conv_w_d[layer, k].rearrange("(t p) -> p t", p=128))), writes=[fcw])
        P.dma(lambda e: e.dma_start(out=fcb.ap, **ncd(self.ffn_conv_b_d[layer].rearrange("(t p) -> p t", p=128))), writes=[fcb])
        for q in range(4):
            P.dma(lambda e, q=q: e.dma_start(out=io_t.ap[0:2 * NSQ, :], in_=self.ffnconv0_d[layer][:, q * 1408:(q + 1) * 1408]), writes=[io_t])
            for t in range(11):
                P.pe(lambda e, q=q, t=t: e.transpose(out=ps[6][:, (q * 11 + t) * 8:(q * 11 + t) * 8 + 8], in_=io_t.ap[0:2 * NSQ, t * 128:(t + 1) * 128],
                                                     identity=self.ident.ap[0:2 * NSQ, 0:2 * NSQ]), reads=[io_t, self.ident], writes=psb(6))
        P.act(lambda e: e.activation(out=fst.ap.rearrange("p a b -> p (a b)"), in_=ps[6][:, 0:44 * 8], func=AF.Copy), reads=psb(6), writes=[fst])

        halves = [
            dict(ne=NEA, up=[(0, 512, "p", 2), (512, 512, "p", 514)], dn=[(2, 512, 0), (514, 512, 512)], sample=False),
            dict(ne=NEB, up=[(1024, 512, "p", 2), (1536, 512, "p", 514), (2048, NSQ * LS, "s", 1026)],
                 dn=[(2, 512, 1024), (514, 512, 1536), (1026, 10 * NSQ, 2048)], sample=True),
        ]
        cnt = 0
        wcnt = 0
        for hf in halves:
            ne = hf["ne"]
            for m in range(2):
                P.dma(lambda e, m=m: e.dma_start(out=wd[m].ap, in_=wdn_d[:, m * 128:(m + 1) * 128].rearrange("(k p) n -> p k n", p=128)),
                      writes=[wd[m]], eng="pool")
            order = []
            for j in range(22):
                order += [j, j + 22]
            def load_blk(q):
                bb, side = q // 2, q % 2
                if bb > 5:
                    return
                ncol = min(512, DFF - 512 * bb)
                c0w = side * DFF + 512 * bb
                wv_ = wbl[q % 3]
                P.dma(lambda e, wv_=wv_, c0w=c0w, ncol=ncol: e.dma_start(out=wv_.ap[:, :, 0:ncol], in_=wup_d[:, c0w:c0w + ncol].rearrange("(k p) n -> p k n", p=128)),
                      writes=[wv_], eng="pool")
            load_blk(0)
            load_blk(1)
            load_blk(2)
            for idx, j in enumerate(order):
                pi_ = idx // 2
                q_ = 2 * (pi_ // 4) + (idx % 2)
                wv_ = wbl[q_ % 3]
                wu = View(wv_.ap[:, :, (pi_ % 4) * 128:(pi_ % 4 + 1) * 128], wv_.toks)
                hr, ac = hraw[idx % 2], acc[idx % 2]
                if not hf["sample"]:
                    P.pool(lambda e, hr=hr: e.memset(hr.ap[:, 0:2], 0.0), writes=[hr])
                else:
                    P.pool(lambda e, hr=hr, j=j: e.tensor_copy(out=hr.ap[:, 0:2], in_=hsave.ap[:, j, :]), reads=[hsave], writes=[hr])
                for (c0, n, kind, e0) in hf["up"]:
                    bnk = cnt % 6
                    cnt += 1
                    for k in range(8):
                        P.pe(lambda e, bnk=bnk, wu=wu, k=k, c0=c0, n=n: e.matmul(ps[bnk][:, 0:n], lhsT=wu.ap[:, k, :], rhs=self.xhi.ap[:, k, c0:c0 + n],
                                                                                 start=(k == 0), stop=(k == 7)),
                             reads=[wu] + self.xt("xhi", k, k + 1, c0, c0 + n), writes=psb(bnk))
                    if kind == "p":
                        P.act(lambda e, bnk=bnk, hr=hr, e0=e0, n=n: e.activation(out=hr.ap[:, e0:e0 + n], in_=ps[bnk][:, 0:n], func=AF.Copy),
                              reads=psb(bnk), writes=[hr])
                    else:
                        P.act(lambda e, bnk=bnk, hr=hr: e.activation(out=hr.ap[:, 1026:NEB].rearrange("p (s c) -> p s c", c=10)[:, :, 2:10],
                                                                     in_=ps[bnk][:, 0:NSQ * LS].rearrange("p (s c) -> p s c", c=LS), func=AF.Copy),
                              reads=psb(bnk), writes=[hr])
                if not hf["sample"]:
                    P.pool(lambda e, hr=hr, j=j: e.tensor_copy(out=hsave.ap[:, j, :], in_=hr.ap[:, 1024:1026]), reads=[hr], writes=[hsave])
                if hf["sample"]:
                    P.pool(lambda e, hr=hr, j=j: e.tensor_copy(out=hr.ap[:, 1026:NEB].rearrange("p (s c) -> p s c", c=10)[:, :, 0:2],
                                                               in_=fst.ap[:, j, :].rearrange("p (s c) -> p s c", c=2)), reads=[fst], writes=[hr])
                    P.pool(lambda e, hr=hr, j=j: e.tensor_copy(out=fco.ap[:, j, 0:2], in_=hr.ap[:, 1024:1026]), reads=[hr], writes=[fco])
                    P.pool(lambda e, hr=hr, j=j: e.tensor_copy(out=fco.ap[:, j, 2:2 + 2 * NSQ].rearrange("p (s c) -> p s c", c=2),
                                                               in_=hr.ap[:, 1026:NEB].rearrange("p (s c) -> p s c", c=10)[:, :, 8:10]), reads=[hr], writes=[fco])
                P.act(lambda e, hr=hr, ac=ac, j=j, ne=ne: e.activation(out=ac.ap[:, 2:ne], in_=hr.ap[:, 2:ne], func=AF.Identity,
                                                                       scale=fcw.ap[:, j, 2:3], bias=fcb.ap[:, j:j + 1]), reads=[hr, fcw, fcb], writes=[ac])
                for tap in range(2):
                    P.dve(lambda e, hr=hr, ac=ac, j=j, tap=tap, ne=ne: e.scalar_tensor_tensor(
                        out=ac.ap[:, 2:ne], in0=hr.ap[:, tap:ne - 2 + tap], scalar=fcw.ap[:, j, tap:tap + 1], in1=ac.ap[:, 2:ne],
                        op0=ALU.mult, op1=ALU.add), reads=[hr, ac, fcw], writes=[ac])
                if j < 22:
                    P.act(lambda e, ac=ac, ne=ne: e.activation(out=ac.ap[:, 2:ne], in_=ac.ap[:, 2:ne], func=AF.Silu), reads=[ac], writes=[ac])
                else:
                    sa = acc[(idx - 1) % 2]
                    P.dve(lambda e, ac=ac, sa=sa, j=j, ne=ne: e.tensor_tensor(out=G.ap[:, j - 22, 2:ne], in0=sa.ap[:, 2:ne], in1=ac.ap[:, 2:ne], op=ALU.mult),
                          reads=[ac, sa], writes=[G])
                npair_in_grp = 4 if pi_ // 4 < 5 else 2
                if pi_ % 4 == npair_in_grp - 1:
                    load_blk(q_ + 3)
            self.mark('ffn%d_up_end' % layer)
            for m in range(8):
                w = wd[m % 2]
                if m >= 1 and m + 1 < 8:
                    wn = wd[(m + 1) % 2]
                    P.dma(lambda e, wn=wn, m=m: e.dma_start(out=wn.ap, in_=wdn_d[:, (m + 1) * 128:(m + 2) * 128].rearrange("(k p) n -> p k n", p=128)),
                          writes=[wn], eng="pool")
                for (e0, n, t0) in hf["dn"]:
                    bnk = cnt % 6
                    cnt += 1
                    for k in range(22):
                        P.pe(lambda e, bnk=bnk, w=w, k=k, e0=e0, n=n: e.matmul(ps[bnk][:, 0:n], lhsT=w.ap[:, k, :], rhs=G.ap[:, k, e0:e0 + n],
                                                                               start=(k == 0), stop=(k == 21)), reads=[w, G], writes=psb(bnk))
                    if t0 < LP:
                        self.resid_into(Rbig.ap[:, m, e0 - 2:e0 - 2 + n], (ps[bnk][:, 0:n], psb(bnk), [Rbig]), m, t0, n)
                    else:
                        Rd = Rbig.ap[:, m, e0 - 2:e0 - 2 + n].rearrange("p (s c) -> p s c", c=10)[:, :, 2:10]
                        pv = ps[bnk][:, 0:n].rearrange("p (s c) -> p s c", c=10)[:, :, 2:10]
                        xh = self.xhi.ap[:, m, LP:NT].rearrange("p (s c) -> p s c", c=LS)
                        xl = self.xlo.ap[:, m, LP:NT].rearrange("p (s c) -> p s c", c=LS)
                        P.dve(lambda e, Rd=Rd, pv=pv, xh=xh: e.scalar_tensor_tensor(out=Rd, in0=xh, scalar=ALPHA, in1=pv, op0=ALU.mult, op1=ALU.add),
                              reads=self.xt("xhi", m, m + 1, LP, NT) + psb(bnk), writes=[Rbig])
                        P.dve(lambda e, Rd=Rd, xl=xl: e.scalar_tensor_tensor(out=Rd, in0=xl, scalar=ALPHA, in1=Rd, op0=ALU.mult, op1=ALU.add),
                              reads=self.xt("xlo", m, m + 1, LP, NT) + [Rbig], writes=[Rbig])
            self.mark('ffn%d_down_end' % layer)
            for (e0, n, t0) in hf["dn"]:
                if t0 < LP and not final:
                    caps_ = []
                    for hh in range(2):
                        Rv = View(Rbig.ap[:, :, e0 - 2 + hh * 256:e0 - 2 + hh * 256 + 256], ["Rbig.%d.%d" % (e0, hh)])
                        with P.capture() as cp_:
                            P.dve(lambda e: e.engine_nop(), reads=[Rbig], writes=Rv.toks)
                            self.ln_tile(Rv, t0 + hh * 256, 256, g, b, tmp if hh == 0 else tmp2, stat_banks=(6, 7) if hh == 0 else (4, 5))
                            P.dve(lambda e: e.engine_nop(), reads=Rv.toks, writes=[Rbig])
                        caps_.append(cp_)
                    P.zip(caps_[0], caps_[1])
                elif t0 < LP:
                    Rv = View(Rbig.ap[:, :, e0 - 2:e0 - 2 + n], Rbig.toks)
                    self.ln_tile(Rv, t0, n, g, b, tmp, final_out=self.final_out if final else None)
                else:
                    Rs = View(Rbig.ap[:, :, 0:NSQ * LS], Rbig.toks)
                    P.dve(lambda e, e0=e0, n=n: e.tensor_copy(out=Rbig.ap[:, :, 0:NSQ * LS].rearrange("p m (s c) -> p m s c", c=LS),
                                                              in_=Rbig.ap[:, :, e0 - 2:e0 - 2 + n].rearrange("p m (s c) -> p m s c", c=10)[:, :, :, 2:10]),
                          reads=[Rbig], writes=[Rbig])
                    self.ln_tile(Rs, LP, NSQ * LS, g, b, tmp, final_out=self.final_out if final else None)
        self.mark('ffn%d_ln_end' % layer)
        for q in range(4):
            for t in range(11):
                bnk = 6 + (t // 4) % 2
                P.pe(lambda e, bnk=bnk, q=q, t=t: e.transpose(out=ps[bnk][0:2 + 2 * NSQ, (t % 4) * 128:(t % 4 + 1) * 128], in_=fco.ap[:, q * 11 + t, :],
                                                              identity=self.ident.ap), reads=[fco, self.ident], writes=psb(bnk))
                if t % 4 == 3 or t == 10:
                    t0 = (t // 4) * 4
                    nn = (t - t0 + 1) * 128
                    P.act(lambda e, bnk=bnk, t0=t0, nn=nn: e.activation(out=io_t.ap[0:2 + 2 * NSQ, t0 * 128:t0 * 128 + nn], in_=ps[bnk][0:2 + 2 * NSQ, 0:nn], func=AF.Copy),
                          reads=psb(bnk), writes=[io_t])
            P.dma(lambda e, q=q: e.dma_start(out=self.o_ffnconv_p[layer][:, q * 1408:(q + 1) * 1408], in_=io_t.ap[0:2, :]), reads=[io_t], writes=["o_ffnconv"])
            for s_ in range(NSQ):
                P.dma(lambda e, q=q, s_=s_: e.dma_start(out=self.o_ffnconv_s[layer][s_][:, q * 1408:(q + 1) * 1408], in_=io_t.ap[2 + 2 * s_:4 + 2 * s_, :]),
                      reads=[io_t], writes=["o_ffnconv"])

    def final_out(self, R, c0, n):
        P, ps = self.P, self.ps
        ob = self.obuf
        for t in range((n + 127) // 128):
            rows = min(128, n - t * 128)
            o = ob[self.ocnt % 2]
            self.ocnt += 1
            for half in range(2):
                bnk = half
                for kk in range(4):
                    m = half * 4 + kk
                    P.pe(lambda e, bnk=bnk, kk=kk, m=m, t=t, rows=rows: e.transpose(out=ps[bnk][0:rows, kk * 128:(kk + 1) * 128],
                                                                                 in_=R.ap[:, m, t * 128:t * 128 + rows], identity=self.ident.ap),
                         reads=[R, self.ident], writes=psb(bnk))
                P.act(lambda e, bnk=bnk, o=o, half=half, rows=rows: e.activation(out=o.ap[0:rows, half * 512:(half + 1) * 512], in_=ps[bnk][0:rows, :], func=AF.Copy),
                      reads=psb(bnk), writes=[o])
            r0 = c0 + t * 128
            dst = self.o_yp[r0:r0 + rows, :] if r0 < LP else self.o_ys[r0 - LP:r0 - LP + rows, :]
            P.dma(lambda e, dst=dst, o=o, rows=rows: e.dma_start(out=dst, in_=o.ap[0:rows, :]), reads=[o], writes=["o_y"])

    def _qrope_tile(self, i, c0, rows, KQ, wuq, cosT, sinT, q1, q2, qrt, bnk_a, bnk_b, qrT_dst, qrS):
        P, ps = self.P, self.ps
        T = slice(0, rows)
        for kc in range(3):
            P.pe(lambda e, kc=kc: e.matmul(ps[bnk_a][T, :], lhsT=KQ.ap[:, kc, c0:c0 + rows],
                                           rhs=wuq.ap[:, kc, :], start=(kc == 0), stop=(kc == 2)),
                 reads=[KQ, wuq], writes=psb(bnk_a))
        qv = ps[bnk_a][T, :].rearrange("p (h a f) -> p h a f", h=8, a=2)
        cos4 = cosT.ap[T, i, :].unsqueeze(1).unsqueeze(1).to_broadcast([rows, 8, 2, 32])
        sin4 = sinT.ap[T, i, :].unsqueeze(1).unsqueeze(1).to_broadcast([rows, 8, 2, 32])
        P.dve(lambda e: e.tensor_tensor(out=q1.ap[T], in0=qv, in1=cos4, op=ALU.mult), reads=psb(bnk_a) + [cosT], writes=[q1])
        P.dve(lambda e: e.tensor_tensor(out=q2.ap[T], in0=qv, in1=sin4, op=ALU.mult), reads=psb(bnk_a) + [sinT], writes=[q2])
        qo = qrt.ap[T, :].rearrange("p (h a f) -> p h a f", h=8, a=2)
        P.dve(lambda e: e.tensor_tensor(out=qo[:, :, 0, :], in0=q1.ap[T, :, 0, :], in1=q2.ap[T, :, 1, :], op=ALU.subtract), reads=[q1, q2], writes=[qrt])
        P.dve(lambda e: e.tensor_tensor(out=qo[:, :, 1, :], in0=q2.ap[T, :, 0, :], in1=q1.ap[T, :, 1, :], op=ALU.add), reads=[q1, q2], writes=[qrt])
        pq = ps[bnk_b][:, :].bitcast(BF16)
        if qrT_dst is not None:
            dst, dcol = qrT_dst
            for pr in range(4):
                P.pe(lambda e, pr=pr: e.transpose(out=pq[:, pr * 128:pr * 128 + rows], in_=qrt.ap[T, pr * 128:(pr + 1) * 128], identity=self.identb.ap[T, T]),
                     reads=[qrt, self.identb], writes=psb(bnk_b))
            P.act(lambda e: e.activation(out=dst.ap[:, :, dcol:dcol + rows], in_=pq[:, 0:512].rearrange("p (c t) -> p c t", c=4)[:, :, 0:rows],
                                         func=AF.Copy, scale=SM_SCALE), reads=psb(bnk_b), writes=[dst])
        else:
            s_ = (c0 - LP) // LS
            for h in range(8):
                P.pe(lambda e, h=h: e.transpose(out=pq[0:64, h * 8:h * 8 + rows], in_=qrt.ap[T, h * 64:(h + 1) * 64], identity=self.identb.ap[T, T]),
                     reads=[qrt, self.identb], writes=psb(bnk_b))
            P.act(lambda e: e.activation(out=qrS.ap[0:64, s_, :], in_=pq[0:64, 0:64], func=AF.Copy, scale=SM_SCALE), reads=psb(bnk_b), writes=[qrS])

    def stage_mla(self, g, b):
        P, A, ps, nc = self.P, self.A, self.ps, self.nc
        A.top = A.nbytes
        A.reset(0)
        NTL = LP // 128 + NSQ
        tiles = [(i * 128, 128) for i in range(LP // 128)] + [(LP + LS * s_, LS) for s_ in range(NSQ)]
        KQ = A.alloc([6, NT], BF16)
        ckv_tok = A.alloc([NTL, 256], BF16)
        qrS = A.alloc([NSQ, 8 * LS], BF16)
        qlatS = A.alloc([2, NSQ, 8, LS], BF16)
        wuq = A.alloc([3, 1536], BF16)
        wukT = A.alloc([8, 256], BF16)
        wuv = A.alloc([2, D], BF16)
        wuqr = A.alloc([3, 512], BF16)
        gq = A.alloc([384]); gkv = A.alloc([256])
        cosT = A.alloc([NTL, 32]); sinT = A.alloc([NTL, 32])
        oT = A.alloc([8, 512], BF16)
        oTs = A.alloc([8, NSQ * LS], BF16)
        smask = A.alloc([8])
        IDX = A.alloc([NSQ * NPAGES], I32)
        kmg = A.alloc([8])
        m_stage = A.mark()
        P.dma(lambda e: e.dma_start(out=wuq.ap, in_=self.mla_w_uq_d.rearrange("(k p) n -> p k n", p=128)), writes=[wuq], eng="pool")
        P.dma(lambda e: e.dma_start(out=wuv.ap, in_=self.mla_w_uv_d.rearrange("(k p) n -> p k n", p=128)), writes=[wuv], eng="pool")
        for kc in range(3):
            P.dve(lambda e, kc=kc: e.tensor_copy(out=wuqr.ap[:, kc, :].rearrange("p (h f) -> p h f", h=8),
                                                 in_=wuq.ap[:, kc, :].rearrange("p (h f) -> p h f", h=8)[:, :, 128:192]), reads=[wuq], writes=[wuqr])
        P.dma(lambda e: e.dma_start(out=gq.ap, in_=self.mla_q_norm_g_d.partition_broadcast(128)), writes=[gq])
        P.dma(lambda e: e.dma_start(out=gkv.ap, in_=self.mla_kv_norm_g_d.partition_broadcast(128)), writes=[gkv])
        P.dma(lambda e: e.dma_start(out=cosT.ap[:, 0:16, :], in_=self.rope_cos_d[0:LP, :].rearrange("(i p) f -> p i f", p=128)), writes=[cosT])
        P.dma(lambda e: e.dma_start(out=sinT.ap[:, 0:16, :], in_=self.rope_sin_d[0:LP, :].rearrange("(i p) f -> p i f", p=128)), writes=[sinT])
        P.dma(lambda e: e.dma_start(out=cosT.ap[0:LS, 16:NTL, :], in_=self.rope_cos_d[LP:NT, :].rearrange("(s p) f -> p s f", p=LS)), writes=[cosT])
        P.dma(lambda e: e.dma_start(out=sinT.ap[0:LS, 16:NTL, :], in_=self.rope_sin_d[LP:NT, :].rearrange("(s p) f -> p s f", p=LS)), writes=[sinT])
        P.dma(lambda e: e.dma_start(out=smask.ap[0:64, :], in_=self.smask_d[:, :]), writes=[smask])
        ptf = A.alloc([NSQ * NPAGES]); iof = A.alloc([1]); ioi = A.alloc([1], I32)
        P.dma(lambda e: e.dma_start(out=IDX.ap, in_=self.page_table_d.rearrange("s j -> (s j)").partition_broadcast(128)), writes=[IDX])
        P.pool(lambda e: e.iota(ioi.ap, pattern=[[0, 1]], base=0, channel_multiplier=1), writes=[ioi])
        P.dve(lambda e: e.tensor_copy(out=ptf.ap, in_=IDX.ap), reads=[IDX], writes=[ptf])
        P.dve(lambda e: e.tensor_copy(out=iof.ap, in_=ioi.ap), reads=[ioi], writes=[iof])
        P.dve(lambda e: e.tensor_scalar(out=IDX.ap, in0=ptf.ap, scalar1=128.0, scalar2=iof.ap[:, 0:1], op0=ALU.mult, op1=ALU.add), reads=[ptf, iof], writes=[IDX])
        wuk = A.alloc([2, D], BF16)
        win = A.alloc([8, 704], BF16)
        P.dma(lambda e: e.dma_start(out=wuk.ap, in_=self.mla_w_uk_d.rearrange("(k p) n -> p k n", p=128)), writes=[wuk], eng="pool")
        P.dma(lambda e: e.dma_start(out=win.ap, in_=self.w_in1_d.rearrange("(k p) n -> p k n", p=128)), writes=[win], eng="pool")
        for h in range(8):
            pt = ps[2 + h % 2][:, :].bitcast(BF16)
            for ch in range(2):
                P.pe(lambda e, pt=pt, h=h, ch=ch: e.transpose(out=pt[:, ch * 128:(ch + 1) * 128], in_=wuk.ap[:, ch, h * 128:(h + 1) * 128], identity=self.identb.ap),
                     reads=[wuk, self.identb], writes=psb(2 + h % 2))
            P.act(lambda e, pt=pt, h=h: e.activation(out=wukT.ap[:, h, :], in_=pt[:, 0:256], func=AF.Copy), reads=psb(2 + h % 2), writes=[wukT])
        stg = [A.alloc([768], BF16) for _ in range(2)]
        ckvf = [A.alloc([256]) for _ in range(2)]
        krf = [A.alloc([64]) for _ in range(2)]
        t1_ = [A.alloc([2, 32]) for _ in range(2)]; t2_ = [A.alloc([2, 32]) for _ in range(2)]; ssq_ = [A.alloc([2]) for _ in range(2)]; junk_ = [A.alloc([384]) for _ in range(2)]
        q1 = A.alloc([8, 2, 32]); q2 = A.alloc([8, 2, 32]); qrt = A.alloc([512], BF16)
        def tile_ops(i, c0, rows):
            T = slice(0, rows)
            st_, cf, kf = stg[i % 2], ckvf[i % 2], krf[i % 2]
            t1, t2, ssq, junk = t1_[i % 2], t2_[i % 2], ssq_[i % 2], junk_[i % 2]
            B0, B1, B2 = (0, 1, 2) if i % 2 == 0 else (3, 4, 5)
            for k in range(8):
                P.pe(lambda e, k=k, c0=c0, rows=rows, T=T: e.matmul(ps[B0][T, 0:384], lhsT=self.xhi.ap[:, k, c0:c0 + rows], rhs=win.ap[:, k, 0:384],
                                                                    start=(k == 0), stop=(k == 7)), reads=[win] + self.xt("xhi", k, k + 1, c0, c0 + rows), writes=psb(B0))
            for k in range(8):
                P.pe(lambda e, k=k, c0=c0, rows=rows, T=T: e.matmul(ps[B1][T, 0:320], lhsT=self.xhi.ap[:, k, c0:c0 + rows], rhs=win.ap[:, k, 384:704],
                                                                    start=(k == 0), stop=(k == 7)), reads=[win] + self.xt("xhi", k, k + 1, c0, c0 + rows), writes=psb(B1))
            P.act(lambda e, T=T: e.activation(out=junk.ap[T, 0:384], in_=ps[B0][T, 0:384], func=AF.Square, accum_out=ssq.ap[T, 0:1]), reads=psb(B0), writes=[junk, ssq])
            P.act(lambda e, T=T: e.activation(out=junk.ap[T, 0:256], in_=ps[B1][T, 0:256], func=AF.Square, accum_out=ssq.ap[T, 1:2]), reads=psb(B1), writes=[junk, ssq])
            P.dve(lambda e, T=T: e.tensor_scalar(out=ssq.ap[T, 0:1], in0=ssq.ap[T, 0:1], scalar1=1.0 / 384, scalar2=EPS, op0=ALU.mult, op1=ALU.add), reads=[ssq], writes=[ssq])
            P.dve(lambda e, T=T: e.tensor_scalar(out=ssq.ap[T, 1:2], in0=ssq.ap[T, 1:2], scalar1=1.0 / 256, scalar2=EPS, op0=ALU.mult, op1=ALU.add), reads=[ssq], writes=[ssq])
            P.act(lambda e, T=T: e.activation(out=ssq.ap[T, :], in_=ssq.ap[T, :], func=AF.Sqrt), reads=[ssq], writes=[ssq])
            P.dve(lambda e, T=T: e.reciprocal(out=ssq.ap[T, :], in_=ssq.ap[T, :]), reads=[ssq], writes=[ssq])
            P.dve(lambda e, T=T, st_=st_: e.scalar_tensor_tensor(out=st_.ap[T, 0:384], in0=ps[B0][T, 0:384], scalar=ssq.ap[T, 0:1], in1=gq.ap[T, :], op0=ALU.mult, op1=ALU.mult),
                  reads=psb(B0) + [ssq, gq], writes=[st_])
            P.dve(lambda e, T=T, cf=cf: e.scalar_tensor_tensor(out=cf.ap[T, :], in0=ps[B1][T, 0:256], scalar=ssq.ap[T, 1:2], in1=gkv.ap[T, :], op0=ALU.mult, op1=ALU.mult),
                  reads=psb(B1) + [ssq, gkv], writes=[cf])
            P.act(lambda e, T=T, cf=cf, st_=st_: e.activation(out=st_.ap[T, 384:640], in_=cf.ap[T, :], func=AF.Copy), reads=[cf], writes=[st_])
            P.pool(lambda e, T=T, cf=cf, i=i: e.tensor_copy(out=ckv_tok.ap[T, i, :], in_=cf.ap[T, :]), reads=[cf], writes=[ckv_tok])
            krv = ps[B1][T, 256:320].rearrange("p (a f) -> p a f", a=2)
            cosb = cosT.ap[T, i, :].unsqueeze(1).to_broadcast([rows, 2, 32])
            sinb = sinT.ap[T, i, :].unsqueeze(1).to_broadcast([rows, 2, 32])
            P.dve(lambda e, T=T, krv=krv, cosb=cosb: e.tensor_tensor(out=t1.ap[T, :, :], in0=krv, in1=cosb, op=ALU.mult), reads=psb(B1) + [cosT], writes=[t1])
            P.dve(lambda e, T=T, krv=krv, sinb=sinb: e.tensor_tensor(out=t2.ap[T, :, :], in0=krv, in1=sinb, op=ALU.mult), reads=psb(B1) + [sinT], writes=[t2])
            P.dve(lambda e, T=T, kf=kf: e.tensor_tensor(out=kf.ap[T, 0:32], in0=t1.ap[T, 0, :], in1=t2.ap[T, 1, :], op=ALU.subtract), reads=[t1, t2], writes=[kf])
            P.dve(lambda e, T=T, kf=kf: e.tensor_tensor(out=kf.ap[T, 32:64], in0=t2.ap[T, 0, :], in1=t1.ap[T, 1, :], op=ALU.add), reads=[t1, t2], writes=[kf])
            P.act(lambda e, T=T, kf=kf, st_=st_, rows=rows: e.activation(out=st_.ap[T, 640:768].rearrange("p (a f) -> p a f", a=2),
                                                            in_=kf.ap[T, :].unsqueeze(1).to_broadcast([rows, 2, 64]), func=AF.Copy), reads=[kf], writes=[st_])
            if c0 < LP:
                P.dma(lambda e, cf=cf, c0=c0, rows=rows: e.dma_start(out=self.o_ckv_p[c0:c0 + rows, :], in_=cf.ap[0:rows, :]), reads=[cf], writes=["o_ckv"])
                P.dma(lambda e, kf=kf, c0=c0, rows=rows: e.dma_start(out=self.o_krope_p[c0:c0 + rows, :], in_=kf.ap[0:rows, :]), reads=[kf], writes=["o_kr"])
            else:
                P.dma(lambda e, cf=cf, c0=c0, rows=rows: e.dma_start(out=self.o_ckv_s[c0 - LP:c0 - LP + rows, :], in_=cf.ap[0:rows, :]), reads=[cf], writes=["o_ckv"])
                P.dma(lambda e, kf=kf, c0=c0, rows=rows: e.dma_start(out=self.o_krope_s[c0 - LP:c0 - LP + rows, :], in_=kf.ap[0:rows, :]), reads=[kf], writes=["o_kr"])
            pt = ps[B2][:, :].bitcast(BF16)
            for c in range(6):
                P.pe(lambda e, pt=pt, c=c, T=T, rows=rows, st_=st_: e.transpose(out=pt[:, c * 128:c * 128 + rows], in_=st_.ap[T, c * 128:(c + 1) * 128], identity=self.identb.ap[T, T]),
                     reads=[st_, self.identb], writes=psb(B2))
            P.act(lambda e, pt=pt, c0=c0, rows=rows: e.activation(out=KQ.ap[:, :, c0:c0 + rows], in_=pt[:, 0:768].rearrange("p (c t) -> p c t", c=6)[:, :, 0:rows], func=AF.Copy),
                  reads=psb(B2), writes=[KQ])
            if c0 >= LP:
                self._qrope_tile(i, c0, rows, KQ, wuqr, cosT, sinT, q1, q2, qrt, 6, 7, None, qrS)
        caps = []
        for i, (c0, rows) in enumerate(tiles):
            with P.capture() as cap_:
                tile_ops(i, c0, rows)
            caps.append(cap_)
        for i in range(0, LP // 128, 2):
            P.zip(caps[i], caps[i + 1])
        for i in range(LP // 128, len(caps)):
            P.zip(caps[i])
        self.mark('mla_phase1_end')
        A.reset(m_stage)
        sq = A.alloc([2, 512], BF16)
        mx = A.alloc([8])
        for ti in range(4):
            c0 = ti * 512
            P.act(lambda e, c0=c0: e.activation(out=sq.ap, in_=KQ.ap[:, 3:5, c0:c0 + 512], func=AF.Square), reads=[KQ], writes=[sq])
            for ch in range(2):
                P.pe(lambda e, ch=ch: e.matmul(ps[5][:, :], lhsT=self.onesb.ap, rhs=sq.ap[:, ch, :], start=(ch == 0), stop=(ch == 1)), reads=[self.onesb, sq], writes=psb(5))
            P.dve(lambda e, ti=ti: e.reduce_max(out=mx.ap[:, ti:ti + 1], in_=ps[5][:, :], axis=AX.X), reads=psb(5), writes=[mx])
            P.act(lambda e, c0=c0: e.activation(out=sq.ap[0:64, 0, :], in_=KQ.ap[0:64, 5, c0:c0 + 512], func=AF.Square), reads=[KQ], writes=[sq])
            P.pe(lambda e: e.matmul(ps[5][:, :], lhsT=self.onesb.ap[0:64, :], rhs=sq.ap[0:64, 0, :], start=True, stop=True), reads=[self.onesb, sq], writes=psb(5))
            P.dve(lambda e, ti=ti: e.reduce_max(out=mx.ap[:, 4 + ti:5 + ti], in_=ps[5][:, :], axis=AX.X), reads=psb(5), writes=[mx])
        P.dve(lambda e: e.reduce_max(out=kmg.ap[:, 0:1], in_=mx.ap[:, 0:4], axis=AX.X), reads=[mx], writes=[kmg])
        P.dve(lambda e: e.reduce_max(out=kmg.ap[:, 1:2], in_=mx.ap[:, 4:8], axis=AX.X), reads=[mx], writes=[kmg])

        wo = [A.alloc([8, 128], BF16) for _ in range(2)]
        m_att = A.mark()
        R = A.alloc([8, 256])
        tmp = (A.alloc([8, 256], BF16), A.alloc([8, 256], BF16), A.alloc([256]), A.alloc([256]))
        m_ln_end = A.mark()
        A.reset(m_att)
        qn = A.alloc([512], BF16)
        PT = [A.alloc([512], BF16) for _ in range(2)]
        rl = A.alloc([512]); olat = A.alloc([2, 512], BF16)
        q1 = A.alloc([8, 2, 32]); q2 = A.alloc([8, 2, 32]); qrt = A.alloc([512], BF16)
        A.reset(max(A.mark(), m_ln_end))
        qlat2 = [A.alloc([2, 512], BF16) for _ in range(2)]
        qrT2 = [A.alloc([4, 512], BF16) for _ in range(2)]
        km2 = [A.alloc([8]) for _ in range(2)]
        wocnt = [0]
        MB = [5]

        def q_lat(h, c0, n, dst, dst_toks, sample=False):
            bnk = MB[0]
            for kc in range(3):
                P.pe(lambda e, bnk=bnk, kc=kc: e.matmul(ps[bnk][:, 0:n], lhsT=wuq.ap[:, kc, 192 * h:192 * h + 128], rhs=KQ.ap[:, kc, c0:c0 + n], start=(kc == 0), stop=(kc == 2)),
                     reads=[wuq, KQ], writes=psb(bnk))
            P.act(lambda e, bnk=bnk: e.activation(out=qn.ap[:, 0:n], in_=ps[bnk][:, 0:n], func=AF.Copy), reads=psb(bnk), writes=[qn])
            for ch in range(2):
                P.pe(lambda e, bnk=bnk, ch=ch: e.matmul(ps[bnk][:, 0:n], lhsT=wukT.ap[:, h, ch * 128:(ch + 1) * 128], rhs=qn.ap[:, 0:n], start=True, stop=True),
                     reads=[wukT, qn], writes=psb(bnk))
                src_ = ps[bnk][:, 0:n].rearrange("p (s t) -> p s t", t=LS) if sample else ps[bnk][:, 0:n]
                P.act(lambda e, bnk=bnk, ch=ch, src_=src_: e.activation(out=dst(ch), in_=src_, func=AF.Copy, scale=SM_SCALE), reads=psb(bnk), writes=dst_toks)

        for h in range(8):
            q_lat(h, LP, NSQ * LS, lambda ch, h=h: qlatS.ap[:, ch, :, h, :], [qlatS], sample=True)

        GP = 4
        NGS = NPAGES // GP
        NBUF = 4
        pgc = [A.alloc([GP, 256], BF16) for _ in range(NBUF)]
        pgr = [A.alloc([GP, 64], BF16) for _ in range(NBUF)]
        KTs = A.alloc([3, GP * 128], BF16)
        Pm = A.alloc([GP * 128], BF16)
        PTs = A.alloc([GP, 64], BF16)
        sm_ = A.alloc([16])
        acc = A.alloc([8])
        Oacc = A.alloc([256]); olS = A.alloc([256], BF16); olST = A.alloc([2, 64], BF16)
        cache_c = self.cache_ckv_d
        cache_r = self.cache_krope_d
        groups = [(s_, g_) for s_ in range(NSQ) for g_ in range(NGS)]

        def issue_gather(gi):
            if gi >= len(groups):
                return
            s_, g_ = groups[gi]
            bc, br = pgc[gi % NBUF], pgr[gi % NBUF]
            for pp in range(GP):
                col = s_ * NPAGES + g_ * GP + pp
                P.dma(lambda e, bc=bc, pp=pp, col=col: e.indirect_dma_start(out=bc.ap[:, pp, :], out_offset=None, in_=cache_c,
                                                                            in_offset=bass.IndirectOffsetOnAxis(ap=IDX.ap[:, col:col + 1], axis=0)),
                      reads=[IDX], writes=[bc], eng="pool")
                P.dma(lambda e, br=br, pp=pp, col=col: e.indirect_dma_start(out=br.ap[:, pp, :], out_offset=None, in_=cache_r,
                                                                            in_offset=bass.IndirectOffsetOnAxis(ap=IDX.ap[:, col:col + 1], axis=0)),
                      reads=[IDX], writes=[br], eng="pool")

        def seq_init():
            P.pool(lambda e: e.memset(acc.ap[0:64, 0:1], -1e30), writes=[acc])
            P.pool(lambda e: e.memset(acc.ap[0:64, 1:2], 0.0), writes=[acc])
            P.pool(lambda e: e.memset(Oacc.ap[0:64, :], 0.0), writes=[Oacc])

        def scores_softmax_pv(s_, nkeys, kt_c, kt_r, vfn, npg, mask=False):
            kn = nkeys
            for ch in range(2):
                P.pe(lambda e, ch=ch: e.matmul(ps[7][0:64, 0:kn], lhsT=qlatS.ap[:, ch, s_, :, :].rearrange("p h t -> p (h t)"), rhs=kt_c[0](ch),
                                               start=(ch == 0), stop=False), reads=[qlatS] + kt_c[1], writes=psb(7))
            P.pe(lambda e: e.matmul(ps[7][0:64, 0:kn], lhsT=qrS.ap[0:64, s_, :], rhs=kt_r[0], start=False, stop=True), reads=[qrS] + kt_r[1], writes=psb(7))
            if mask:
                P.dve(lambda e: e.tensor_tensor(out=ps[7][0:64, 0:kn], in0=ps[7][0:64, 0:kn], in1=smask.ap[0:64, 0:kn], op=ALU.add), reads=psb(7) + [smask], writes=psb(7))
            P.dve(lambda e: e.reduce_max(out=sm_.ap[0:64, 0:1], in_=ps[7][0:64, 0:kn], axis=AX.X), reads=psb(7), writes=[sm_])
            P.dve(lambda e: e.tensor_tensor(out=acc.ap[0:64, 2:3], in0=acc.ap[0:64, 0:1], in1=sm_.ap[0:64, 0:1], op=ALU.max), reads=[acc, sm_], writes=[acc])
            P.dve(lambda e: e.tensor_tensor(out=acc.ap[0:64, 3:4], in0=acc.ap[0:64, 0:1], in1=acc.ap[0:64, 2:3], op=ALU.subtract), reads=[acc], writes=[acc])
            P.dve(lambda e: e.tensor_scalar(out=acc.ap[0:64, 4:5], in0=acc.ap[0:64, 2:3], scalar1=-1.0, scalar2=None, op0=ALU.mult), reads=[acc], writes=[acc])
            P.act(lambda e: e.activation(out=acc.ap[0:64, 3:4], in_=acc.ap[0:64, 3:4], func=AF.Exp), reads=[acc], writes=[acc])
            P.act(lambda e: e.activation(out=Pm.ap[0:64, 0:kn], in_=ps[7][0:64, 0:kn], func=AF.Exp, bias=acc.ap[0:64, 4:5], accum_out=sm_.ap[0:64, 1:2]),
                  reads=psb(7) + [acc], writes=[Pm, sm_])
            P.dve(lambda e: e.scalar_tensor_tensor(out=acc.ap[0:64, 1:2], in0=acc.ap[0:64, 1:2], scalar=acc.ap[0:64, 3:4], in1=sm_.ap[0:64, 1:2], op0=ALU.mult, op1=ALU.add),
                  reads=[acc, sm_], writes=[acc])
            P.dve(lambda e: e.tensor_copy(out=acc.ap[0:64, 0:1], in_=acc.ap[0:64, 2:3]), reads=[acc], writes=[acc])
            P.dve(lambda e: e.tensor_scalar(out=Oacc.ap[0:64, :], in0=Oacc.ap[0:64, :], scalar1=acc.ap[0:64, 3:4], scalar2=None, op0=ALU.mult), reads=[Oacc, acc], writes=[Oacc])
            ptp = ps[6][:, :].bitcast(BF16)
            for pp in range(npg):
                k1 = min(128, nkeys - pp * 128)
                P.pe(lambda e, pp=pp, k1=k1: e.transpose(out=ptp[0:k1, pp * 64:(pp + 1) * 64], in_=Pm.ap[0:64, pp * 128:pp * 128 + k1], identity=self.identb.ap[0:64, 0:64]),
                     reads=[Pm, self.identb], writes=psb(6))
            kmax = min(128, nkeys)
            P.act(lambda e: e.activation(out=PTs.ap[0:kmax, 0:npg, :], in_=ptp[0:kmax, 0:npg * 64].rearrange("p (g c) -> p g c", c=64), func=AF.Copy),
                  reads=psb(6), writes=[PTs])
            for pp in range(npg):
                k1 = min(128, nkeys - pp * 128)
                P.pe(lambda e, pp=pp, k1=k1: e.matmul(ps[7][0:64, 0:256], lhsT=PTs.ap[0:k1, pp, :], rhs=vfn[0](pp, k1), start=(pp == 0), stop=(pp == npg - 1)),
                     reads=[PTs] + vfn[1], writes=psb(7))
            P.dve(lambda e: e.tensor_tensor(out=Oacc.ap[0:64, :], in0=Oacc.ap[0:64, :], in1=ps[7][0:64, 0:256], op=ALU.add), reads=[Oacc] + psb(7), writes=[Oacc])

        def process_group(gi):
            s_, g_ = groups[gi]
            bc, br = pgc[gi % NBUF], pgr[gi % NBUF]
            if g_ == 0:
                seq_init()
            ptk = ps[6][:, :].bitcast(BF16)
            for ch in range(2):
                for pp in range(GP):
                    P.pe(lambda e, ch=ch, pp=pp, bc=bc: e.transpose(out=ptk[:, ch * 512 + pp * 128:ch * 512 + (pp + 1) * 128], in_=bc.ap[:, pp, ch * 128:(ch + 1) * 128], identity=self.identb.ap),
                         reads=[bc, self.identb], writes=psb(6))
            P.act(lambda e: e.activation(out=KTs.ap[:, 0:2, :], in_=ptk[:, 0:1024].rearrange("p (c k) -> p c k", c=2), func=AF.Copy), reads=psb(6), writes=[KTs])
            ptr = ps[7][:, :].bitcast(BF16)
            for pp in range(GP):
                P.pe(lambda e, pp=pp, br=br: e.transpose(out=ptr[0:64, pp * 128:(pp + 1) * 128], in_=br.ap[:, pp, :], identity=self.identb.ap),
                     reads=[br, self.identb], writes=psb(7))
            P.dve(lambda e: e.tensor_copy(out=KTs.ap[0:64, 2, :], in_=ptr[0:64, 0:GP * 128]), reads=psb(7), writes=[KTs])
            scores_softmax_pv(s_, GP * 128, (lambda ch: KTs.ap[:, ch, :], [KTs]), (KTs.ap[0:64, 2, :], [KTs]),
                              (lambda pp, k1, bc=bc: bc.ap[0:k1, pp, :], [bc]), GP)
            issue_gather(gi + NBUF)
            if g_ == NGS - 1:
                finish_seq(s_)

        def finish_seq(s_):
            c0 = LP + s_ * LS
            scores_softmax_pv(s_, LS, (lambda ch: KQ.ap[:, 3 + ch, c0:c0 + LS], [KQ]), (KQ.ap[0:64, 5, c0:c0 + LS], [KQ]),
                              (lambda pp, k1: ckv_tok.ap[0:k1, 16 + s_, :], [ckv_tok]), 1, mask=True)
            P.dve(lambda e: e.reciprocal(out=acc.ap[0:64, 5:6], in_=acc.ap[0:64, 1:2]), reads=[acc], writes=[acc])
            P.dve(lambda e: e.tensor_scalar(out=olS.ap[0:64, :], in0=Oacc.ap[0:64, :], scalar1=acc.ap[0:64, 5:6], scalar2=None, op0=ALU.mult), reads=[Oacc, acc], writes=[olS])
            ptp = ps[6][:, :].bitcast(BF16)
            for ch in range(2):
                P.pe(lambda e, ch=ch: e.transpose(out=ptp[:, ch * 64:(ch + 1) * 64], in_=olS.ap[0:64, ch * 128:(ch + 1) * 128], identity=self.identb.ap[0:64, 0:64]),
                     reads=[olS, self.identb], writes=psb(6))
            P.act(lambda e: e.activation(out=olST.ap, in_=ptp[:, 0:128].rearrange("p (c t) -> p c t", c=2), func=AF.Copy), reads=psb(6), writes=[olST])
            for h in range(8):
                for ch in range(2):
                    P.pe(lambda e, h=h, ch=ch: e.matmul(ps[7][:, h * LS:(h + 1) * LS], lhsT=wuv.ap[:, ch, h * 128:(h + 1) * 128], rhs=olST.ap[:, ch, h * LS:(h + 1) * LS],
                                                        start=(ch == 0), stop=(ch == 1)), reads=[wuv, olST], writes=psb(7))
            P.act(lambda e: e.activation(out=oTs.ap[:, :, s_ * LS:(s_ + 1) * LS], in_=ps[7][:, 0:64].rearrange("p (h t) -> p h t", h=8), func=AF.Copy), reads=psb(7), writes=[oTs])

        def outproj_ln(src, sc0, tc0, n, banks):
            for m in range(8):
                w = wo[wocnt[0] % 2]
                wocnt[0] += 1
                P.dma(lambda e, w=w, m=m: e.dma_start(out=w.ap, in_=self.w_out1_d[:, m * 128:(m + 1) * 128].rearrange("(k p) n -> p k n", p=128)), writes=[w], eng="pool")
                bnk = banks[m % len(banks)]
                for k in range(8):
                    P.pe(lambda e, bnk=bnk, w=w, k=k: e.matmul(ps[bnk][:, 0:n], lhsT=w.ap[:, k, :], rhs=src.ap[:, k, sc0:sc0 + n], start=(k == 0), stop=(k == 7)),
                         reads=[w, src], writes=psb(bnk))
                self.resid_into(R.ap[:, m, 0:n], (ps[bnk][:, 0:n], psb(bnk), [R]), m, tc0, n)
            self.ln_tile(R, tc0, n, g, b, tmp, stat_banks=(banks[0], banks[1]))

        def prologue(Q, h, par):
            q0 = Q * 512
            hb_ = 64 * (h % 2)
            qlat, km, qrT = qlat2[par], km2[par], qrT2[Q % 2]
            if h == 0:
                for tt in range(4):
                    self._qrope_tile(Q * 4 + tt, q0 + tt * 128, 128, KQ, wuqr, cosT, sinT, q1, q2, qrt, 5, 5, (qrT, tt * 128), qrS)
            q_lat(h, q0, 512, lambda ch: qlat.ap[:, ch, :], [qlat])
            P.act(lambda e: e.activation(out=sq.ap, in_=qlat.ap, func=AF.Square), reads=[qlat], writes=[sq])
            for ch in range(2):
                P.pe(lambda e, ch=ch: e.matmul(ps[5][:, :], lhsT=self.onesb.ap, rhs=sq.ap[:, ch, :], start=(ch == 0), stop=(ch == 1)), reads=[self.onesb, sq], writes=psb(5))
            P.dve(lambda e: e.reduce_max(out=km.ap[:, 2:3], in_=ps[5][:, :], axis=AX.X), reads=psb(5), writes=[km])
            P.act(lambda e: e.activation(out=sq.ap[hb_:hb_ + 64, 0, :], in_=qrT.ap[hb_:hb_ + 64, h // 2, :], func=AF.Square), reads=[qrT], writes=[sq])
            P.pe(lambda e: e.matmul(ps[5][:, :], lhsT=self.onesb.ap[hb_:hb_ + 64, :], rhs=sq.ap[hb_:hb_ + 64, 0, :], start=True, stop=True), reads=[self.onesb, sq], writes=psb(5))
            P.dve(lambda e: e.reduce_max(out=km.ap[:, 3:4], in_=ps[5][:, :], axis=AX.X), reads=psb(5), writes=[km])
            P.dve(lambda e: e.tensor_tensor(out=km.ap[:, 4:6], in0=kmg.ap[:, 0:2], in1=km.ap[:, 2:4], op=ALU.mult), reads=[km, kmg], writes=[km])
            P.act(lambda e: e.activation(out=km.ap[:, 4:6], in_=km.ap[:, 4:6], func=AF.Sqrt), reads=[km], writes=[km])
            P.dve(lambda e: e.tensor_tensor(out=km.ap[:, 6:7], in0=km.ap[:, 4:5], in1=km.ap[:, 5:6], op=ALU.add), reads=[km], writes=[km])
            P.dve(lambda e: e.tensor_scalar(out=km.ap[:, 7:8], in0=km.ap[:, 6:7], scalar1=-1.0, scalar2=None, op0=ALU.mult), reads=[km], writes=[km])

        def attend(Q, h, par):
            hb_ = 64 * (h % 2)
            qlat, km, qrT = qlat2[par], km2[par], qrT2[Q % 2]
            nkb = 4 * Q + 4

            def emit_S(kb):
                k0 = kb * 128
                qs = max(0, 128 * (kb - 4 * Q))
                nq = 512 - qs
                sb_ = kb % 2
                for ch in range(2):
                    P.pe(lambda e, ch=ch: e.matmul(ps[sb_][:, 0:nq], lhsT=KQ.ap[:, 3 + ch, k0:k0 + 128], rhs=qlat.ap[:, ch, qs:512], start=(ch == 0), stop=False),
                         reads=[KQ, qlat], writes=psb(sb_))
                P.pe(lambda e: e.matmul(ps[sb_][:, 0:nq], lhsT=KQ.ap[hb_:hb_ + 64, 5, k0:k0 + 128], rhs=qrT.ap[hb_:hb_ + 64, h // 2, qs:512], start=False, stop=True),
                     reads=[KQ, qrT], writes=psb(sb_))

            def emit_PV(kb):
                qs = max(0, 128 * (kb - 4 * Q))
                nq = 512 - qs
                sb_ = kb % 2
                pT = PT[kb % 2]
                P.act(lambda e: e.activation(out=pT.ap[:, 0:nq], in_=ps[sb_][:, 0:nq], func=AF.Exp, bias=km.ap[:, 7:8]), reads=psb(sb_) + [km], writes=[pT])
                if kb >= 4 * Q:
                    P.dve(lambda e: e.tensor_tensor(out=pT.ap[:, 0:128], in0=pT.ap[:, 0:128], in1=self.utrib.ap, op=ALU.mult), reads=[pT, self.utrib], writes=[pT])
                for ch in range(2):
                    P.pe(lambda e, ch=ch: e.matmul(ps[2 + ch][:, qs:512], lhsT=ckv_tok.ap[:, kb, ch * 128:(ch + 1) * 128], rhs=pT.ap[:, 0:nq],
                                                   start=(kb == 0), stop=(kb == nkb - 1)), reads=[ckv_tok, pT], writes=psb(2 + ch))
                P.pe(lambda e: e.matmul(ps[4][:, qs:512], lhsT=self.onesb.ap, rhs=pT.ap[:, 0:nq], start=(kb == 0), stop=(kb == nkb - 1)),
                     reads=[self.onesb, pT], writes=psb(4))

            emit_S(0)
            for kb in range(nkb):
                if kb + 1 < nkb:
                    emit_S(kb + 1)
                emit_PV(kb)
            P.dve(lambda e: e.reciprocal(out=rl.ap, in_=ps[4][:, :]), reads=psb(4), writes=[rl])
            for ch in range(2):
                P.dve(lambda e, ch=ch: e.tensor_tensor(out=olat.ap[:, ch, :], in0=ps[2 + ch][:, :], in1=rl.ap, op=ALU.mult), reads=psb(2 + ch) + [rl], writes=[olat])
            for ch in range(2):
                P.pe(lambda e, ch=ch: e.matmul(ps[0][:, :], lhsT=wuv.ap[:, ch, h * 128:(h + 1) * 128], rhs=olat.ap[:, ch, :], start=(ch == 0), stop=(ch == 1)),
                     reads=[wuv, olat], writes=psb(0))
            P.act(lambda e: e.activation(out=oT.ap[:, h, :], in_=ps[0][:, :], func=AF.Copy), reads=psb(0), writes=[oT])

        self.mark('mla_units_start')
        units = [(Q, h) for Q in range(4) for h in range(8)]
        for gi in range(NBUF):
            issue_gather(gi)
        prologue(0, 0, 0)
        gnext = 0
        LN_GROUPS = 5
        wsum = sum(4 * Q_ + 4 for (Q_, h_) in units)
        wacc = 0
        for ui, (Q, h) in enumerate(units):
            with P.capture() as la:
                attend(Q, h, ui % 2)
            wacc += 4 * Q + 4
            target = -(-(len(groups) - 4 * LN_GROUPS) * wacc // wsum) + LN_GROUPS * Q
            with P.capture() as lb:
                while gnext < min(target, len(groups)):
                    process_group(gnext)
                    gnext += 1
            with P.capture() as lc:
                if ui + 1 < len(units):
                    prologue(units[ui + 1][0], units[ui + 1][1], (ui + 1) % 2)
            P.zip(la, lb, lc)
            if h == 7:
                self.mark('mla_Q%d_attn_end' % Q)
                with P.capture() as lo:
                    for hf2 in range(2):
                        outproj_ln(oT, hf2 * 256, Q * 512 + hf2 * 256, 256, (0, 1, 5))
                with P.capture() as lg:
                    for _ in range(LN_GROUPS):
                        if gnext < len(groups):
                            process_group(gnext)
                            gnext += 1
                P.zip(lo, lg)
        self.mark('mla_units_end')
        while gnext < len(groups):
            process_group(gnext)
            gnext += 1
        outproj_ln(oTs, 0, LP, NSQ * LS, (0, 1, 5))

    def stage_debug(self):
        if not hasattr(self, "dbgsrc"):
            self.dbgsrc = {}
        self.dbgsrc["xhi"] = (self.xhi, [128, 8 * NT], BF16)
        self.dbgsrc["xlo"] = (self.xlo, [128, 8 * NT], BF16)
        for name in self.dbg:
            src, shape, dt = self.dbgsrc[name]
            o = self.dout("dbg_" + name, shape, dt)
            rd = [src]
            if name in ("xhi", "xlo"):
                rd = self.xt(name, 0, 8, 0, NT)
            self.P.dma(lambda e, o=o, src=src: e.dma_start(out=o, in_=src.ap), reads=rd, writes=["dbg_" + name])


def _consts():
    c = {}
    c["c_ident"] = np.eye(128, dtype=np.float32)
    c["c_utri"] = np.triu(np.ones((128, 128), np.float32))
    m2 = np.zeros((128, 2), np.float32)
    m2[:64, 0] = 1.0
    m2[64:, 1] = 1.0
    c["c_mask2"] = m2
    c["c_bdmask"] = np.kron(np.eye(4, dtype=np.float32), np.ones((32, 32), np.float32))
    inv = (10000.0 ** (-np.arange(0, 64, 2, dtype=np.float32) / np.float32(64))).astype(np.float32)
    pos = np.concatenate([np.arange(LP), np.tile(16384 + np.arange(LS), NSQ)]).astype(np.float32)
    ang = (pos[:, None] * inv[None, :]).astype(np.float32)
    c["c_rope_cos"] = np.cos(ang).astype(np.float32)
    c["c_rope_sin"] = np.sin(ang).astype(np.float32)
    sm = np.zeros((8, 8, 8), np.float32)
    for t in range(8):
        sm[:, t, t + 1:] = -1e9
    c["c_smask"] = sm.reshape(64, 8)
    return c


def core_inputs(inp, core):
    f = np.ascontiguousarray
    d = dict(_consts())
    sl = slice(NSQ * core, NSQ * core + NSQ)
    d["xp"] = f(inp["x_prompt"][core])
    d["xs"] = f(inp["x_sample"][sl].reshape(NSQ * LS, D))
    d["w_in0"] = f(inp["w_in0"][0])
    d["s5_a_re"] = f(inp["s5_a_re"][0].reshape(2048))
    d["s5_a_im"] = f(inp["s5_a_im"][0].reshape(2048))
    d["s5_log_dt"] = f(inp["s5_log_dt"][0])
    d["s5_b_re"] = f(inp["s5_b_re"][0].reshape(2048, 16))
    d["s5_b_im"] = f(inp["s5_b_im"][0].reshape(2048, 16))
    d["s5_c_re"] = f(inp["s5_c_re"][0])
    d["s5_c_im"] = f(inp["s5_c_im"][0])
    d["s5_d"] = f(inp["s5_d"][0].reshape(512))
    d["s5_w_glu"] = f(inp["s5_w_glu"][0])
    d["s5_b_glu"] = f(inp["s5_b_glu"][0])
    d["w_in1"] = f(inp["w_in1"][0])
    d["mla_q_norm_g"] = f(inp["mla_q_norm_g"][0]); d["mla_kv_norm_g"] = f(inp["mla_kv_norm_g"][0])
    d["mla_w_uq"] = f(inp["mla_w_uq"][0])
    d["mla_w_uk"] = f(inp["mla_w_uk"][0].reshape(256, 1024)); d["mla_w_uv"] = f(inp["mla_w_uv"][0].reshape(256, 1024))
    d["w_out1"] = f(inp["w_out1"][0])
    d["cache_ckv"] = inp["cache_ckv"].reshape(5120 * 128, 256); d["cache_krope"] = inp["cache_krope"].reshape(5120 * 128, 64)
    d["page_table"] = f(inp["page_table"][sl])
    d["w_out0"] = f(inp["w_out0"][0])
    for k in ("ln1_g", "ln1_b", "ln2_g", "ln2_b", "ffn_w_up", "ffn_conv_w", "ffn_conv_b", "ffn_w_down"):
        d[k] = f(inp[k])
    d["ffnconv0"] = f(inp["state_ffn_conv"][:, sl].reshape(2, NSQ * 2, 2 * DFF))
    d["ssd_conv_w"] = f(inp["ssd_conv_w"][0])
    d["ssd_conv_b"] = f(inp["ssd_conv_b"][0])
    d["ssd_dt_bias"] = f(inp["ssd_dt_bias"][0])
    d["ssd_a_log"] = f(inp["ssd_a_log"][0])
    d["ssd_d"] = f(inp["ssd_d"][0])
    d["ssd_norm_g"] = f(inp["ssd_norm_g"][0])
    d["ssd0"] = f(inp["state_ssd"][0, sl].reshape(NSQ, 512, 128))
    d["ssdconv0"] = f(inp["state_ssd_conv"][0, sl].reshape(NSQ * 3, 1024))
    d["s5re0"] = f(inp["state_s5_re"][0, sl].reshape(NSQ, 2048))
    d["s5im0"] = f(inp["state_s5_im"][0, sl].reshape(NSQ, 2048))
    return d


_NC_CACHE = {}


def kernel(**inputs):
    inp = {k: np.asarray(v) for k, v in inputs.items()}
    if "nc" not in _NC_CACHE:
        B = Builder()
        _NC_CACHE["nc"] = (B.build(), list(B.ins))
    nc, in_names = _NC_CACHE["nc"]
    maps = []
    for c in range(NCORES):
        d = core_inputs(inp, c)
        maps.append({k: d[k] for k in in_names})
    res = run_bass_kernel_spmd(nc, maps, core_ids=list(range(NCORES)))
    r = res.results
    cat = lambda name: np.stack([np.asarray(r[c][name]) for c in range(NCORES)], 0)
    f32 = lambda a: np.ascontiguousarray(a, dtype=np.float32)
    y_p = f32(cat("o_yp"))
    y_s = f32(cat("o_ys").reshape(NCORES * NSQ, LS, D))
    s5re_p = f32(cat("o_s5re_p").reshape(1, NCORES, 32, 64))
    s5im_p = f32(cat("o_s5im_p").reshape(1, NCORES, 32, 64))
    ssd_p = f32(cat("o_ssd_p").reshape(1, NCORES, 8, 64, 128))
    ssdc_p = f32(cat("o_ssdconv_p").reshape(1, NCORES, 3, 1024))
    ckv_p = f32(cat("o_ckv_p").reshape(NCORES, 1, LP, 256))
    kr_p = f32(cat("o_krope_p").reshape(NCORES, 1, LP, 64))
    ffc_p = f32(cat("o_ffnconv_p").transpose(1, 0, 2, 3))
    s5re_s = f32(cat("o_s5re_s").reshape(1, NCORES * NSQ, 32, 64))
    s5im_s = f32(cat("o_s5im_s").reshape(1, NCORES * NSQ, 32, 64))
    ssd_s = f32(cat("o_ssd_s").reshape(1, NCORES * NSQ, 8, 64, 128))
    ssdc_s = f32(cat("o_ssdconv_s").reshape(1, NCORES * NSQ, 3, 1024))
    ckv_s = f32(cat("o_ckv_s").reshape(NCORES * NSQ, 1, LS, 256))
    kr_s = f32(cat("o_krope_s").reshape(NCORES * NSQ, 1, LS, 64))
    ffc_s = f32(cat("o_ffnconv_s").transpose(1, 0, 2, 3, 4).reshape(2, NCORES * NSQ, 2, 2 * DFF))
    return (y_p, y_s, s5re_p, s5im_p, ssd_p, ssdc_p, ckv_p, kr_p, ffc_p, s5re_s, s5im_s, ssd_s, ssdc_s, ckv_s, kr_s, ffc_s)
```

```python
import math
from contextlib import ExitStack
import numpy as np
import concourse.bass as bass
import concourse.mybir as mybir
from concourse.bass_utils import run_bass_kernel_spmd

F32 = mybir.dt.float32
BF16 = mybir.dt.bfloat16
I32 = mybir.dt.int32
AF = mybir.ActivationFunctionType
ALU = mybir.AluOpType
AX = mybir.AxisListType

NCORES = 8
D = 1024
LP = 2048
NSQ = 4
LS = 8
NT = LP + NSQ * LS
NCH = NT // 8
DFF = 2816
ALPHA = 4 ** 0.25
EPS = 1e-5
SM_SCALE = 192 ** -0.5
NPAGES = 128

ENGS = ("pe", "act", "dve", "pool", "sp")
N_DMA_SLOTS = 24
SEM_EPOCH = 30000


class Op:
    __slots__ = ("eng", "fn", "deps", "is_dma", "sig", "dma_k", "idx", "sem_id", "sem_val", "slot_prev", "reads", "writes")

    def __init__(self, eng, fn, is_dma):
        self.eng = eng
        self.fn = fn
        self.is_dma = is_dma
        self.deps = set()
        self.sig = False
        self.dma_k = None
        self.sem_id = None
        self.sem_val = None
        self.slot_prev = None


def _flat(items):
    out = []
    for t in items:
        if isinstance(t, str):
            out.append(t)
        elif isinstance(t, View):
            out.extend(t.toks)
        else:
            out.extend(_flat(t))
    return out


class Prog:
    def __init__(self, nc):
        self.nc = nc
        self.ops = []
        self.cur = self.ops
        self.n_dma = 0

    def op(self, eng, fn, reads=(), writes=(), dma=False):
        o = Op(eng, fn, dma)
        o.reads = _flat(reads)
        o.writes = _flat(writes)
        self.cur.append(o)
        return o

    def capture(self):
        prog = self

        class _Cap:
            def __enter__(self_):
                self_.saved = prog.cur
                self_.lst = []
                prog.cur = self_.lst
                return self_.lst

            def __exit__(self_, *a):
                prog.cur = self_.saved
                return False
        return _Cap()

    def zip(self, *lists):
        lists = [l for l in lists if l]
        pos = [0] * len(lists)
        total = sum(len(l) for l in lists)
        for _ in range(total):
            best, bi = None, None
            for i, l in enumerate(lists):
                if pos[i] < len(l):
                    frac = (pos[i] + 0.5) / len(l)
                    if best is None or frac < best:
                        best, bi = frac, i
            self.cur.append(lists[bi][pos[bi]])
            pos[bi] += 1

    def finalize(self):
        last_w, readers = {}, {}
        for idx, o in enumerate(self.ops):
            o.idx = idx
            o.deps = set()
            for t in o.reads:
                w = last_w.get(t)
                if w is not None:
                    o.deps.add(w)
            for t in o.writes:
                w = last_w.get(t)
                if w is not None:
                    o.deps.add(w)
                for r in readers.get(t, ()):
                    o.deps.add(r)
            for t in o.reads:
                readers.setdefault(t, []).append(idx)
            for t in o.writes:
                last_w[t] = idx
                readers[t] = []
            o.deps.discard(idx)
            if o.is_dma:
                o.dma_k = self.n_dma
                self.n_dma += 1

    def pe(self, fn, reads=(), writes=()):
        return self.op("pe", fn, reads, writes)

    def act(self, fn, reads=(), writes=()):
        return self.op("act", fn, reads, writes)

    def dve(self, fn, reads=(), writes=()):
        return self.op("dve", fn, reads, writes)

    def pool(self, fn, reads=(), writes=()):
        return self.op("pool", fn, reads, writes)

    def dma(self, fn, reads=(), writes=(), eng="sp"):
        return self.op(eng, fn, reads, writes, dma=True)

    def emit(self, stack):
        nc = self.nc
        self.finalize()
        ops = self.ops
        for o in ops:
            for d in o.deps:
                p = ops[d]
                if p.is_dma:
                    continue
                if p.eng == "pe" and o.eng == "pe":
                    continue
                p.sig = True
        cnt = {e: 0 for e in ENGS}
        eng_sems = {e: [] for e in ENGS}
        for o in ops:
            if o.is_dma or not o.sig:
                continue
            ep = cnt[o.eng] // SEM_EPOCH
            while len(eng_sems[o.eng]) <= ep:
                eng_sems[o.eng].append(stack.enter_context(nc.semaphore(f"s_{o.eng}_{len(eng_sems[o.eng])}")))
            o.sem_id = eng_sems[o.eng][ep]
            o.sem_val = cnt[o.eng] % SEM_EPOCH + 1
            cnt[o.eng] += 1
        dma_sems = [stack.enter_context(nc.semaphore(f"s_dma_{i}")) for i in range(N_DMA_SLOTS)]
        slot_last = {}
        for o in ops:
            if o.is_dma:
                s = o.dma_k % N_DMA_SLOTS
                o.sem_id = dma_sems[s]
                o.sem_val = 16 * (o.dma_k // N_DMA_SLOTS + 1)
                o.slot_prev = slot_last.get(s)
                slot_last[s] = o.idx
        per_eng = {e: [o for o in ops if o.eng == e] for e in ENGS}
        block = stack.enter_context(nc.Block())

        def run(engine, lst):
            seen = {}
            for o in lst:
                waits = {}
                deps = set(o.deps)
                if o.is_dma and o.slot_prev is not None:
                    deps.add(o.slot_prev)
                for d in deps:
                    p = ops[d]
                    if p.sem_id is None:
                        continue
                    k = id(p.sem_id)
                    if k not in waits or waits[k][1] < p.sem_val:
                        waits[k] = (p.sem_id, p.sem_val)
                for k, (s, v) in waits.items():
                    if seen.get(k, 0) >= v:
                        continue
                    engine.wait_ge(s, v)
                    seen[k] = v
                ins = o.fn(engine)
                if o.is_dma:
                    ins.then_inc(o.sem_id, 16)
                elif o.sig:
                    ins.then_inc(o.sem_id, 1)

        @block.tensor
        def _(e):
            run(e, per_eng["pe"])

        @block.scalar
        def _(e):
            run(e, per_eng["act"])

        @block.vector
        def _(e):
            run(e, per_eng["dve"])

        @block.gpsimd
        def _(e):
            run(e, per_eng["pool"])

        @block.sync
        def _(e):
            run(e, per_eng["sp"])
            last = {}
            for o in ops:
                if o.is_dma:
                    last[id(o.sem_id)] = (o.sem_id, o.sem_val)
            for s, v in last.values():
                e.wait_ge(s, v)


class View:
    def __init__(self, ap, toks):
        self.ap = ap
        self.toks = toks

    def __getitem__(self, k):
        return self.ap[k]


class Arena:
    G = 512

    def __init__(self, nc, st, name, nbytes):
        self.name = name
        self.nbytes = nbytes
        self.t = st.enter_context(nc.sbuf_tensor(name, [128, nbytes // 4], F32))
        self.off = 0
        self.peak = 0
        self.top = nbytes

    def alloc_top(self, free_shape, dt=F32):
        n = 1
        for x in free_shape:
            n *= x
        esz = 4 if dt in (F32, I32) else 2
        nb = ((n * esz + self.G - 1) // self.G) * self.G
        self.top -= nb
        save = self.off
        self.off = self.top
        v = self.alloc(free_shape, dt, _top=True)
        self.off = save
        return v

    def mark(self):
        return self.off

    def reset(self, off=0):
        self.off = off

    def alloc(self, free_shape, dt=F32, _top=False):
        n = 1
        for s in free_shape:
            n *= s
        esz = 4 if dt in (F32, I32) else 2
        nb = n * esz
        start = self.off
        self.off = start + ((nb + self.G - 1) // self.G) * self.G
        if not _top:
            assert self.off <= self.top, f"arena {self.name} overflow {self.off} > {self.top}"
            self.peak = max(self.peak, self.off)
        ap = self.t[:, start // 4: start // 4 + (nb + 3) // 4]
        if dt != F32:
            ap = ap.bitcast(dt)
            ap = ap[:, 0:n]
        if len(free_shape) == 2:
            ap = ap.rearrange("p (a b) -> p a b", a=free_shape[0])
        elif len(free_shape) == 3:
            ap = ap.rearrange("p (a b c) -> p a b c", a=free_shape[0], b=free_shape[1])
        elif len(free_shape) == 4:
            ap = ap.rearrange("p (a b c d) -> p a b c d", a=free_shape[0], b=free_shape[1], c=free_shape[2])
        toks = [f"{self.name}.{g}" for g in range(start // self.G, (start + nb - 1) // self.G + 1)]
        return View(ap, toks)


def psb(b):
    return [f"ps{b}.{q}" for q in range(4)]


class Builder:
    def __init__(self, dbg=None, upto=99):
        self.dbgsrc = {}
        self.dbg = dbg or []
        self.upto = upto
        self.nc = bass.Bass("TRN2", target_bir_lowering=False)
        self.P = Prog(self.nc)
        self.ins = {}
        self.outs = {}

    def mark(self, name):
        if not hasattr(self, "marks"):
            self.marks = []
        self.marks.append((name, len(self.P.cur) if self.P.cur is self.P.ops else -1))

    def din(self, name, shape, dt=F32):
        t = self.nc.dram_tensor(name, list(shape), dt, kind="ExternalInput").ap()
        self.ins[name] = t
        return t

    def dout(self, name, shape, dt=F32):
        t = self.nc.dram_tensor(name, list(shape), dt, kind="ExternalOutput").ap()
        self.outs[name] = t
        return t

    def sbt(self, name, shape, dt=F32):
        t = self.st.enter_context(self.nc.sbuf_tensor(name, list(shape), dt))
        return View(t[tuple(slice(None) for _ in shape)], [name])

    def load(self, view, src_ap, eng="sp", wtoks=None, **kw):
        self.P.dma(lambda e: e.dma_start(out=view if not isinstance(view, View) else view.ap, in_=src_ap, **kw),
                   writes=wtoks if wtoks is not None else [view], eng=eng)

    def build(self):
        nc, P = self.nc, self.P
        with ExitStack() as st:
            self.st = st
            self.declare_io()
            self.alloc_persistent()
            self.stage_consts()
            self.stage_load_x()
            if self.upto >= 1:
                self.stage_s5()
            if self.upto >= 2:
                self.stage_ssd()
            if self.upto >= 3:
                g1, b1 = self.ln_params(self.ln1_g_d[0], self.ln1_b_d[0], 0)
                self.stage_outproj_ln(self.w_out0_d, g1, b1, self.mix, 0)
            if self.upto >= 4:
                g2, b2 = self.ln_params(self.ln2_g_d[0], self.ln2_b_d[0], 1)
                self.stage_ffn(0, g2, b2)
            if self.upto >= 5:
                g3, b3 = self.ln_params(self.ln1_g_d[1], self.ln1_b_d[1], 2)
                self.stage_mla(g3, b3)
            if self.upto >= 6:
                g4, b4 = self.ln_params(self.ln2_g_d[1], self.ln2_b_d[1], 3)
                self.ocnt = 0
                self.stage_ffn(1, g4, b4, final=True)
            self.stage_debug()
            P.emit(st)
        return nc

    def declare_io(self):
        di = self.din
        self.xp_d = di("xp", [LP, D])
        self.xs_d = di("xs", [NSQ * LS, D])
        self.ident_d = di("c_ident", [128, 128])
        self.utri_d = di("c_utri", [128, 128])
        self.mask2_d = di("c_mask2", [128, 2])
        self.bdmask_d = di("c_bdmask", [128, 128])
        self.w_in0_d = di("w_in0", [D, 2056])
        self.s5_a_re_d = di("s5_a_re", [2048])
        self.s5_a_im_d = di("s5_a_im", [2048])
        self.s5_log_dt_d = di("s5_log_dt", [32])
        self.s5_b_re_d = di("s5_b_re", [2048, 16])
        self.s5_b_im_d = di("s5_b_im", [2048, 16])
        self.s5_c_re_d = di("s5_c_re", [32, 16, 64])
        self.s5_c_im_d = di("s5_c_im", [32, 16, 64])
        self.s5_d_d = di("s5_d", [512])
        self.s5_w_glu_d = di("s5_w_glu", [512, 512])
        self.s5_b_glu_d = di("s5_b_glu", [512])
        self.s5re0_d = di("s5re0", [NSQ, 2048])
        self.s5im0_d = di("s5im0", [NSQ, 2048])
        self.ssd_conv_w_d = di("ssd_conv_w", [4, 1024])
        self.ssd_conv_b_d = di("ssd_conv_b", [1024])
        self.ssd_dt_bias_d = di("ssd_dt_bias", [8])
        self.ssd_a_log_d = di("ssd_a_log", [8])
        self.ssd_d_d = di("ssd_d", [8])
        self.ssd_norm_g_d = di("ssd_norm_g", [512])
        self.ssd0_d = di("ssd0", [NSQ, 512, 128])
        self.ssdconv0_d = di("ssdconv0", [NSQ * 3, 1024])
        self.w_out0_d = di("w_out0", [D, D])
        self.ln1_g_d = di("ln1_g", [2, D]); self.ln1_b_d = di("ln1_b", [2, D])
        self.ln2_g_d = di("ln2_g", [2, D]); self.ln2_b_d = di("ln2_b", [2, D])
        self.ffn_w_up_d = di("ffn_w_up", [2, D, 2 * DFF])
        self.ffn_conv_w_d = di("ffn_conv_w", [2, 3, 2 * DFF])
        self.ffn_conv_b_d = di("ffn_conv_b", [2, 2 * DFF])
        self.ffn_w_down_d = di("ffn_w_down", [2, DFF, D])
        self.ffnconv0_d = di("ffnconv0", [2, NSQ * 2, 2 * DFF])
        self.w_in1_d = di("w_in1", [D, 704])
        self.mla_q_norm_g_d = di("mla_q_norm_g", [384]); self.mla_kv_norm_g_d = di("mla_kv_norm_g", [256])
        self.mla_w_uq_d = di("mla_w_uq", [384, 1536])
        self.mla_w_uk_d = di("mla_w_uk", [256, 1024]); self.mla_w_uv_d = di("mla_w_uv", [256, 1024])
        self.w_out1_d = di("w_out1", [D, D])
        self.cache_ckv_d = di("cache_ckv", [5120 * 128, 256]); self.cache_krope_d = di("cache_krope", [5120 * 128, 64])
        self.page_table_d = di("page_table", [NSQ, NPAGES], I32)
        self.rope_cos_d = di("c_rope_cos", [NT, 32]); self.rope_sin_d = di("c_rope_sin", [NT, 32])
        self.smask_d = di("c_smask", [64, 8])
        do = self.dout
        self.o_ckv_p = do("o_ckv_p", [LP, 256]); self.o_krope_p = do("o_krope_p", [LP, 64])
        self.o_ckv_s = do("o_ckv_s", [NSQ * LS, 256]); self.o_krope_s = do("o_krope_s", [NSQ * LS, 64])
        self.o_yp = do("o_yp", [LP, D]); self.o_ys = do("o_ys", [NSQ * LS, D])
        self.o_ffnconv_p = do("o_ffnconv_p", [2, 2, 2 * DFF])
        self.o_ffnconv_s = do("o_ffnconv_s", [2, NSQ, 2, 2 * DFF])
        self.o_ssd_p = do("o_ssd_p", [512, 128])
        self.o_ssd_s = do("o_ssd_s", [NSQ, 512, 128])
        self.o_ssdconv_p = do("o_ssdconv_p", [3, 1024])
        self.o_ssdconv_s = do("o_ssdconv_s", [NSQ, 3, 1024])
        self.o_s5re_p = do("o_s5re_p", [2048])
        self.o_s5im_p = do("o_s5im_p", [2048])
        self.o_s5re_s = do("o_s5re_s", [NSQ, 2048])
        self.o_s5im_s = do("o_s5im_s", [NSQ, 2048])

    def alloc_persistent(self):
        nc, st = self.nc, self.st
        self.xhi = self.sbt("xhi", [128, 8, NT], BF16)
        self.xlo = self.sbt("xlo", [128, 8, NT], BF16)
        self.ident = self.sbt("ident", [128, 128])
        self.identb = self.sbt("identb", [128, 128], BF16)
        self.utri = self.sbt("utri", [128, 128])
        self.onesb = self.sbt("onesb", [128, 128], BF16)
        self.utrib = self.sbt("utrib", [128, 128], BF16)
        self.ones = self.sbt("ones", [128, 128])
        self.mask2 = self.sbt("mask2", [128, 2])
        self.bdmask = self.sbt("bdmask", [128, 128])
        self.A = Arena(nc, st, "A", 139 * 1024)
        self.ps = [st.enter_context(nc.psum_tensor(f"psum{b}", [128, 512], F32)) for b in range(8)]

    def stage_consts(self):
        P = self.P
        self.load(self.ident, self.ident_d[:, :])
        self.load(self.utri, self.utri_d[:, :])
        self.load(self.mask2, self.mask2_d[:, :])
        self.load(self.bdmask, self.bdmask_d[:, :])
        P.dve(lambda e: e.tensor_copy(out=self.identb.ap, in_=self.ident.ap), reads=[self.ident], writes=[self.identb])
        P.dve(lambda e: e.tensor_copy(out=self.utrib.ap, in_=self.utri.ap), reads=[self.utri], writes=[self.utrib])
        P.pool(lambda e: e.memset(self.onesb.ap, 1.0), writes=[self.onesb])
        P.pool(lambda e: e.memset(self.ones.ap, 1.0), writes=[self.ones])

    def stage_load_x(self):
        P, A = self.P, self.A
        m0 = A.mark()
        xin = [A.alloc([D]) for _ in range(3)]
        tmp = [A.alloc([4, 128]) for _ in range(2)]
        ntile = LP // 128 + 1
        cnt = 0
        for i in range(ntile):
            rows = 128 if i < LP // 128 else NSQ * LS
            c0 = i * 128
            xi = xin[i % 3]
            src = self.xp_d[c0:c0 + rows, :] if i < LP // 128 else self.xs_d[:, :]
            self.load(xi.ap[0:rows, :], src, wtoks=[xi])
            for half in range(2):
                b = cnt % 2
                cnt += 1
                ps = self.ps[b]
                for kk in range(4):
                    k = half * 4 + kk
                    P.pe(lambda e, ps=ps, kk=kk, k=k, xi=xi, rows=rows: e.transpose(
                        out=ps[:, kk * 128:kk * 128 + rows], in_=xi.ap[0:rows, k * 128:(k + 1) * 128],
                        identity=self.ident.ap[0:rows, 0:rows]), reads=[xi, self.ident], writes=psb(b))
                pv = ps[:, :].rearrange("p (a b) -> p a b", a=4)[:, :, 0:rows]
                hi = self.xhi.ap[:, half * 4:half * 4 + 4, c0:c0 + rows]
                lo = self.xlo.ap[:, half * 4:half * 4 + 4, c0:c0 + rows]
                tv = tmp[b]
                t = tv.ap[:, :, 0:rows]
                thi = self.xt("xhi", half * 4, half * 4 + 4, c0, c0 + rows)
                tlo = self.xt("xlo", half * 4, half * 4 + 4, c0, c0 + rows)
                P.act(lambda e, hi=hi, pv=pv: e.activation(out=hi, in_=pv, func=AF.Copy), reads=psb(b), writes=thi)
                P.dve(lambda e, t=t, pv=pv, hi=hi: e.tensor_tensor(out=t, in0=pv, in1=hi, op=ALU.subtract),
                      reads=psb(b) + thi, writes=[tv])
                P.pool(lambda e, lo=lo, t=t: e.tensor_copy(out=lo, in_=t), reads=[tv], writes=tlo)
        A.reset(m0)

    def xt(self, kind, m0, m1, c0, c1):
        return [f"{kind}.{m}.{ct}" for m in range(m0, m1) for ct in range(c0 // 128, (c1 - 1) // 128 + 1)]

    def ntiles(self):
        return [(0, 512), (512, 512), (1024, 512), (1536, 512), (2048, NT - 2048)]

    def stage_s5(self):
        P, A, nc = self.P, self.A, self.nc
        ps = self.ps
        self.mix = A.alloc_top([8, NT], BF16)
        mix = self.mix
        uTb = A.alloc([4, NT], BF16)
        WBT = A.alloc([8, 4, 2, 128], BF16)
        WC = A.alloc([8, 16, 2, 32], BF16)
        KT = A.alloc([8, 4, 128], BF16)
        wglu = A.alloc([4, 512], BF16)
        bglu = A.alloc([4])
        SM = A.alloc([66, 16])
        FS = A.alloc([2, 5, 16])
        h0 = A.alloc([2, 16, NSQ])
        dq = A.alloc([4])
        m_stage = A.mark()
        sm = lambda i: SM.ap[:, i, :]
        (I_ARE, I_AIM, I_LDT, I_DT, I_ARDT, I_TH, I_MAG, I_S, I_C, I_LR, I_LI, I_T1, I_T2, I_T3, I_T4,
         I_DEN, I_FR, I_FI, I_LM1) = range(19)
        I_L = 20
        I_M = 40
        I_L8N = 64

        def smop(eng, fn):
            self.P.op(eng, fn, reads=[SM], writes=[SM])

        def ld_small(i, src):
            P.dma(lambda e: e.dma_start(out=sm(i), in_=src, allow_slow_non_contiguous=True), writes=[SM])
        ld_small(I_ARE, self.s5_a_re_d.rearrange("(p q) -> q p", q=128))
        ld_small(I_AIM, self.s5_a_im_d.rearrange("(p q) -> q p", q=128))
        for g2 in range(2):
            src = self.s5_log_dt_d.rearrange("(p g) -> g p", g=2)[g2].partition_broadcast(64)
            P.dma(lambda e, g2=g2, src=src: e.dma_start(out=SM.ap[g2 * 64:(g2 + 1) * 64, I_LDT, :], in_=src,
                                                        allow_slow_non_contiguous=True), writes=[SM])
        P.dma(lambda e: e.dma_start(out=dq.ap, in_=self.s5_d_d.rearrange("(q p) -> p q", p=128),
                                    allow_slow_non_contiguous=True), writes=[dq])
        P.dma(lambda e: e.dma_start(out=bglu.ap, in_=self.s5_b_glu_d.rearrange("(q p) -> p q", p=128),
                                    allow_slow_non_contiguous=True), writes=[bglu])
        P.dma(lambda e: e.dma_start(out=wglu.ap, in_=self.s5_w_glu_d.rearrange("(k p) n -> p k n", p=128)),
              writes=[wglu], eng="pool")
        wu = A.alloc([8, 512], BF16)
        Bre = A.alloc([16, 16]); Bim = A.alloc([16, 16])
        BBre = A.alloc([16, 16]); BBim = A.alloc([16, 16])
        EBre = A.alloc([16, 16]); EBim = A.alloc([16, 16])
        T1 = A.alloc([16, 16]); T2 = A.alloc([16, 16])
        Ere = A.alloc([16, 32]); Eim = A.alloc([16, 32])
        ECre = A.alloc([16, 32]); ECimn = A.alloc([16, 32])
        CTre = A.alloc([16, 16]); CTim = A.alloc([16, 16])
        T3 = A.alloc([16, 32]); T4 = A.alloc([16, 32])
        craw = A.alloc([16, 128])
        h0raw = View(craw.ap.rearrange("p a b -> p (a b)"), craw.toks)
        P.dma(lambda e: e.dma_start(out=wu.ap, in_=self.w_in0_d[:, 0:512].rearrange("(k p) n -> p k n", p=128)),
              writes=[wu], eng="pool")
        P.dma(lambda e: e.dma_start(out=Bre.ap, in_=self.s5_b_re_d.rearrange("(p q) h -> q p h", q=128)), writes=[Bre])
        P.dma(lambda e: e.dma_start(out=Bim.ap, in_=self.s5_b_im_d.rearrange("(p q) h -> q p h", q=128)), writes=[Bim])

        for m in range(4):
            for ti, (c0, n) in enumerate(self.ntiles()):
                b = (m * 5 + ti) % 2
                for k in range(8):
                    P.pe(lambda e, b=b, m=m, k=k, c0=c0, n=n: e.matmul(
                        ps[b][:, 0:n], lhsT=wu.ap[:, k, m * 128:(m + 1) * 128], rhs=self.xhi.ap[:, k, c0:c0 + n],
                        start=(k == 0), stop=(k == 7)), reads=[wu] + self.xt("xhi", k, k + 1, c0, c0 + n), writes=psb(b))
                P.act(lambda e, b=b, m=m, c0=c0, n=n: e.activation(out=uTb.ap[:, m, c0:c0 + n], in_=ps[b][:, 0:n], func=AF.Copy),
                      reads=psb(b), writes=[uTb])

        V, G = "dve", "act"
        smop(G, lambda e: e.activation(out=sm(I_DT), in_=sm(I_LDT), func=AF.Exp))
        smop(V, lambda e: e.tensor_tensor(out=sm(I_ARDT), in0=sm(I_ARE), in1=sm(I_DT), op=ALU.mult))
        smop(V, lambda e: e.tensor_tensor(out=sm(I_TH), in0=sm(I_AIM), in1=sm(I_DT), op=ALU.mult))
        smop(G, lambda e: e.activation(out=sm(I_MAG), in_=sm(I_ARDT), func=AF.Exp, scale=1.0 / 16))
        smop(G, lambda e: e.activation(out=sm(I_S), in_=sm(I_TH), func=AF.Sin, scale=1.0 / 16))
        smop(V, lambda e: e.tensor_scalar(out=sm(I_T1), in0=sm(I_TH), scalar1=-1.0 / 16, scalar2=math.pi / 2,
                                          op0=ALU.mult, op1=ALU.add))
        smop(G, lambda e: e.activation(out=sm(I_C), in_=sm(I_T1), func=AF.Sin))
        smop(V, lambda e: e.tensor_tensor(out=sm(I_LR), in0=sm(I_MAG), in1=sm(I_C), op=ALU.mult))
        smop(V, lambda e: e.tensor_tensor(out=sm(I_LI), in0=sm(I_MAG), in1=sm(I_S), op=ALU.mult))

        def csq(ore, oim, ire, iim):
            smop(V, lambda e: e.tensor_tensor(out=sm(I_T1), in0=sm(ire), in1=sm(ire), op=ALU.mult))
            smop(V, lambda e: e.tensor_tensor(out=sm(I_T2), in0=sm(iim), in1=sm(iim), op=ALU.mult))
            smop(V, lambda e: e.scalar_tensor_tensor(out=sm(I_T3), in0=sm(ire), scalar=2.0, in1=sm(iim), op0=ALU.mult, op1=ALU.mult))
            smop(V, lambda e: e.tensor_tensor(out=sm(ore), in0=sm(I_T1), in1=sm(I_T2), op=ALU.subtract))
            smop(V, lambda e: e.tensor_copy(out=sm(oim), in_=sm(I_T3)))

        def cmul(ore, oim, are_, aim_, bre_, bim_):
            smop(V, lambda e: e.tensor_tensor(out=sm(I_T1), in0=sm(are_), in1=sm(bre_), op=ALU.mult))
            smop(V, lambda e: e.tensor_tensor(out=sm(I_T2), in0=sm(aim_), in1=sm(bim_), op=ALU.mult))
            smop(V, lambda e: e.tensor_tensor(out=sm(I_T3), in0=sm(are_), in1=sm(bim_), op=ALU.mult))
            smop(V, lambda e: e.tensor_tensor(out=sm(I_T4), in0=sm(aim_), in1=sm(bre_), op=ALU.mult))
            smop(V, lambda e: e.tensor_tensor(out=sm(ore), in0=sm(I_T1), in1=sm(I_T2), op=ALU.subtract))
            smop(V, lambda e: e.tensor_tensor(out=sm(oim), in0=sm(I_T3), in1=sm(I_T4), op=ALU.add))

        for _ in range(4):
            csq(I_LR, I_LI, I_LR, I_LI)
        smop("pool", lambda e: e.memset(sm(I_L + 0), 1.0))
        smop("pool", lambda e: e.memset(sm(I_L + 9), 0.0))
        for k in range(1, 9):
            cmul(I_L + k, I_L + 9 + k, I_L + k - 1, I_L + 9 + k - 1, I_LR, I_LI)
        smop(V, lambda e: e.tensor_tensor(out=sm(I_T1), in0=sm(I_ARE), in1=sm(I_ARE), op=ALU.mult))
        smop(V, lambda e: e.tensor_tensor(out=sm(I_T2), in0=sm(I_AIM), in1=sm(I_AIM), op=ALU.mult))
        smop(V, lambda e: e.tensor_tensor(out=sm(I_DEN), in0=sm(I_T1), in1=sm(I_T2), op=ALU.add))
        smop(V, lambda e: e.reciprocal(out=sm(I_DEN), in_=sm(I_DEN)))
        smop(V, lambda e: e.tensor_scalar(out=sm(I_LM1), in0=sm(I_LR), scalar1=-1.0, scalar2=None, op0=ALU.add))
        smop(V, lambda e: e.tensor_tensor(out=sm(I_T1), in0=sm(I_LM1), in1=sm(I_ARE), op=ALU.mult))
        smop(V, lambda e: e.tensor_tensor(out=sm(I_T2), in0=sm(I_LI), in1=sm(I_AIM), op=ALU.mult))
        smop(V, lambda e: e.tensor_tensor(out=sm(I_T1), in0=sm(I_T1), in1=sm(I_T2), op=ALU.add))
        smop(V, lambda e: e.tensor_tensor(out=sm(I_FR), in0=sm(I_T1), in1=sm(I_DEN), op=ALU.mult))
        smop(V, lambda e: e.tensor_tensor(out=sm(I_T1), in0=sm(I_LI), in1=sm(I_ARE), op=ALU.mult))
        smop(V, lambda e: e.tensor_tensor(out=sm(I_T2), in0=sm(I_LM1), in1=sm(I_AIM), op=ALU.mult))
        smop(V, lambda e: e.tensor_tensor(out=sm(I_T1), in0=sm(I_T1), in1=sm(I_T2), op=ALU.subtract))
        smop(V, lambda e: e.tensor_tensor(out=sm(I_FI), in0=sm(I_T1), in1=sm(I_DEN), op=ALU.mult))
        smop(V, lambda e: e.tensor_copy(out=sm(I_M), in_=sm(I_L + 8)))
        smop(V, lambda e: e.tensor_copy(out=sm(I_M + 8), in_=sm(I_L + 17)))
        for k in range(1, 8):
            csq(I_M + k, I_M + 8 + k, I_M + k - 1, I_M + 8 + k - 1)
        smop(V, lambda e: e.tensor_scalar(out=SM.ap[:, I_M + 16:I_M + 24, :], in0=SM.ap[:, I_M + 8:I_M + 16, :], scalar1=-1.0,
                                          scalar2=None, op0=ALU.mult))
        smop(V, lambda e: e.tensor_scalar(out=sm(I_L8N), in0=sm(I_L + 17), scalar1=-1.0, scalar2=None, op0=ALU.mult))

        def bcw(i, W):
            return sm(i).unsqueeze(2).to_broadcast([128, 16, W])

        def wide_cmul(ore, oim, xre, xim, ire, iim, W, t1, t2, neg_im=False, eng="dve"):
            op = P.dve if eng == "dve" else P.pool
            op(lambda e: e.tensor_tensor(out=t1.ap, in0=xre.ap, in1=bcw(ire, W), op=ALU.mult), reads=[xre, SM], writes=[t1])
            op(lambda e: e.tensor_tensor(out=t2.ap, in0=xim.ap, in1=bcw(iim, W), op=ALU.mult), reads=[xim, SM], writes=[t2])
            op(lambda e: e.tensor_tensor(out=ore.ap if isinstance(ore, View) else ore, in0=t1.ap, in1=t2.ap, op=ALU.subtract),
               reads=[t1, t2], writes=[ore] if isinstance(ore, View) else [WC])
            op(lambda e: e.tensor_tensor(out=t1.ap, in0=xre.ap, in1=bcw(iim, W), op=ALU.mult), reads=[xre, SM], writes=[t1])
            op(lambda e: e.tensor_tensor(out=t2.ap, in0=xim.ap, in1=bcw(ire, W), op=ALU.mult), reads=[xim, SM], writes=[t2])
            if neg_im:
                op(lambda e: e.scalar_tensor_tensor(out=oim.ap if isinstance(oim, View) else oim, in0=t1.ap, scalar=-1.0, in1=t2.ap,
                                                    op0=ALU.mult, op1=ALU.subtract),
                   reads=[t1, t2], writes=[oim] if isinstance(oim, View) else [WC])
            else:
                op(lambda e: e.tensor_tensor(out=oim.ap if isinstance(oim, View) else oim, in0=t1.ap, in1=t2.ap, op=ALU.add),
                   reads=[t1, t2], writes=[oim] if isinstance(oim, View) else [WC])

        def expand(dst, src):
            o = dst.ap.rearrange("p a (g h) -> p a g h", g=2)
            i0 = src.ap.unsqueeze(2).to_broadcast([128, 16, 2, 16])
            i1 = self.mask2.ap.unsqueeze(1).unsqueeze(3).to_broadcast([128, 16, 2, 16])
            P.dve(lambda e: e.tensor_tensor(out=o, in0=i0, in1=i1, op=ALU.mult), reads=[src, self.mask2], writes=[dst])

        wide_cmul(BBre, BBim, Bre, Bim, I_FR, I_FI, 16, T1, T2)
        for ri, (cd, CT) in enumerate(((self.s5_c_re_d, CTre), (self.s5_c_im_d, CTim))):
            P.dma(lambda e, cd=cd: e.dma_start(out=craw.ap[0:16, :, :].rearrange("h p (g n) -> h p g n", g=2),
                                               in_=cd.rearrange("(p g) h n -> h p g n", g=2)), writes=[craw])
            b = 2 + ri
            for pr in range(16):
                P.pe(lambda e, b=b, pr=pr: e.transpose(out=ps[b][:, pr * 16:(pr + 1) * 16], in_=craw.ap[0:16, pr, :],
                                                       identity=self.ident.ap[0:16, 0:16]),
                     reads=[craw, self.ident], writes=psb(b))
            P.act(lambda e, b=b, CT=CT: e.activation(out=CT.ap.rearrange("p a b -> p (a b)"), in_=ps[b][:, 0:256], func=AF.Copy),
                  reads=psb(b), writes=[CT])
        expand(ECre, CTre)
        expand(ECimn, CTim)
        P.dve(lambda e: e.tensor_scalar(out=ECimn.ap, in0=ECimn.ap, scalar1=-1.0, scalar2=None, op0=ALU.mult),
              reads=[ECimn], writes=[ECimn])
        for tau in range(8):
            lr, li = I_L + tau + 1, I_L + 9 + tau + 1
            ore = WC.ap[:, tau, :, 0, :]
            oim = WC.ap[:, tau, :, 1, :]
            P.dve(lambda e, lr=lr: e.tensor_tensor(out=T3.ap, in0=ECre.ap, in1=bcw(lr, 32), op=ALU.mult), reads=[ECre, SM], writes=[T3])
            P.dve(lambda e, li=li: e.tensor_tensor(out=T4.ap, in0=ECimn.ap, in1=bcw(li, 32), op=ALU.mult), reads=[ECimn, SM], writes=[T4])
            P.dve(lambda e, ore=ore: e.tensor_tensor(out=ore, in0=T3.ap, in1=T4.ap, op=ALU.add), reads=[T3, T4], writes=[WC])
            P.dve(lambda e, lr=lr: e.tensor_tensor(out=T3.ap, in0=ECimn.ap, in1=bcw(lr, 32), op=ALU.mult), reads=[ECimn, SM], writes=[T3])
            P.dve(lambda e, li=li: e.tensor_tensor(out=T4.ap, in0=ECre.ap, in1=bcw(li, 32), op=ALU.mult), reads=[ECre, SM], writes=[T4])
            P.dve(lambda e, oim=oim: e.tensor_tensor(out=oim, in0=T3.ap, in1=T4.ap, op=ALU.subtract), reads=[T3, T4], writes=[WC])
        for k in range(8):
            wide_cmul(EBre, EBim, BBre, BBim, I_L + k, I_L + 9 + k, 16, T1, T2)
            expand(Ere, EBre)
            expand(Eim, EBim)
            for ri, E in enumerate((Ere, Eim)):
                b = 4 + ri
                for q in range(4):
                    P.pe(lambda e, b=b, q=q, E=E: e.transpose(out=ps[b][:, q * 128:(q + 1) * 128],
                                                              in_=E.ap[:, 4 * q:4 * q + 4, :].rearrange("p a b -> p (a b)"),
                                                              identity=self.ident.ap), reads=[E, self.ident], writes=psb(b))
                P.act(lambda e, b=b, ri=ri, k=k: e.activation(out=WBT.ap[:, 7 - k, :, ri, :],
                                                              in_=ps[b][:, :].rearrange("p (q n) -> p q n", q=4), func=AF.Copy),
                      reads=psb(b), writes=[WBT])
            b = 6 + (k % 2)
            for q in range(4):
                P.pe(lambda e, b=b, q=q: e.matmul(ps[b][:, q * 128:(q + 1) * 128],
                                                  lhsT=Ere.ap[:, 4 * q:4 * q + 4, :].rearrange("p a b -> p (a b)"),
                                                  rhs=ECre.ap[:, 4 * q:4 * q + 4, :].rearrange("p a b -> p (a b)"),
                                                  start=True, stop=False), reads=[Ere, ECre], writes=psb(b))
                P.pe(lambda e, b=b, q=q: e.matmul(ps[b][:, q * 128:(q + 1) * 128],
                                                  lhsT=Eim.ap[:, 4 * q:4 * q + 4, :].rearrange("p a b -> p (a b)"),
                                                  rhs=ECimn.ap[:, 4 * q:4 * q + 4, :].rearrange("p a b -> p (a b)"),
                                                  start=False, stop=True), reads=[Eim, ECimn], writes=psb(b))
            bdm = self.bdmask.ap.unsqueeze(1).to_broadcast([128, 4, 128])
            if k == 0:
                P.dve(lambda e, b=b, bdm=bdm: e.tensor_tensor(out=T3.ap.rearrange("p a b -> p (a b)").rearrange("p (q n) -> p q n", q=4),
                                                             in0=ps[b][:, :].rearrange("p (q n) -> p q n", q=4), in1=bdm, op=ALU.mult),
                      reads=psb(b) + [self.bdmask], writes=[T3])
                for q in range(4):
                    P.dve(lambda e, q=q: e.scalar_tensor_tensor(out=KT.ap[:, 0, q, :], in0=self.ident.ap, scalar=dq.ap[:, q:q + 1],
                                                                in1=T3.ap.rearrange("p a b -> p (a b)")[:, q * 128:(q + 1) * 128],
                                                                op0=ALU.mult, op1=ALU.add),
                          reads=[T3, self.ident, dq], writes=[KT])
            else:
                P.dve(lambda e, b=b, k=k, bdm=bdm: e.tensor_tensor(out=KT.ap[:, k, :, :], in0=ps[b][:, :].rearrange("p (q n) -> p q n", q=4),
                                                                  in1=bdm, op=ALU.mult), reads=psb(b) + [self.bdmask], writes=[KT])
        for ri in range(2):
            b = 2 + ri
            srcd = self.s5re0_d if ri == 0 else self.s5im0_d
            P.dma(lambda e, srcd=srcd: e.dma_start(out=h0raw.ap[0:NSQ, :], in_=srcd[:, :]), writes=[h0raw])
            for pr in range(16):
                P.pe(lambda e, b=b, pr=pr, ri=ri: e.transpose(out=ps[b][:, pr * NSQ:(pr + 1) * NSQ],
                                                              in_=h0raw.ap[0:NSQ, pr * 128:(pr + 1) * 128],
                                                              identity=self.ident.ap[0:NSQ, 0:NSQ]),
                     reads=[h0raw, self.ident], writes=psb(b))
            P.act(lambda e, b=b, ri=ri: e.activation(out=h0.ap[:, ri, :, :].rearrange("p a b -> p (a b)"), in_=ps[b][:, 0:16 * NSQ],
                                                     func=AF.Copy), reads=psb(b), writes=[h0])

        A.reset(m_stage)
        AB = [[A.alloc([2, NCH]) for _ in range(2)] for _ in range(2)]
        HP = [A.alloc([4, 2, NCH], BF16) for _ in range(2)]
        sg = A.alloc([4, 512])

        def tpos(row, col):
            return (row, col)

        for quad in range(4):
            hp = HP[quad % 2]
            def pair_body(p4, quad=quad, hp=hp):
                j = quad * 4 + p4
                pb = 32 * p4
                a0, a1 = AB[j % 2]
                for ri in range(2):
                    b = ri + 2 * (j % 2)
                    for tau in range(8):
                        P.pe(lambda e, b=b, ri=ri, tau=tau, pb=pb, quad=quad: e.matmul(
                            ps[b][:, 0:NCH], lhsT=WBT.ap[pb:pb + 32, tau, quad, ri, :], rhs=uTb.ap[pb:pb + 32, quad, tau:NT:8],
                            start=(tau == 0), stop=(tau == 7), tile_position=(pb, 0)), reads=[WBT, uTb], writes=psb(b))
                    P.act(lambda e, b=b, ri=ri, a0=a0: e.activation(out=a0.ap[:, ri, :], in_=ps[b][:, 0:NCH], func=AF.Copy),
                          reads=psb(b), writes=[a0])
                l8r = SM.ap[:, I_L + 8, j:j + 1]
                l8i = SM.ap[:, I_L + 17, j:j + 1]
                l8in = SM.ap[:, I_L8N, j:j + 1]
                sre = a0.ap[:, 0, 256:260]
                sim = a0.ap[:, 1, 256:260]
                P.dve(lambda e, sre=sre, l8r=l8r, j=j: e.scalar_tensor_tensor(out=sre, in0=h0.ap[:, 0, j, :], scalar=l8r, in1=sre, op0=ALU.mult, op1=ALU.add),
                      reads=[a0, h0, SM], writes=[a0])
                P.dve(lambda e, sre=sre, l8in=l8in, j=j: e.scalar_tensor_tensor(out=sre, in0=h0.ap[:, 1, j, :], scalar=l8in, in1=sre, op0=ALU.mult, op1=ALU.add),
                      reads=[a0, h0, SM], writes=[a0])
                P.dve(lambda e, sim=sim, l8r=l8r, j=j: e.scalar_tensor_tensor(out=sim, in0=h0.ap[:, 1, j, :], scalar=l8r, in1=sim, op0=ALU.mult, op1=ALU.add),
                      reads=[a0, h0, SM], writes=[a0])
                P.dve(lambda e, sim=sim, l8i=l8i, j=j: e.scalar_tensor_tensor(out=sim, in0=h0.ap[:, 0, j, :], scalar=l8i, in1=sim, op0=ALU.mult, op1=ALU.add),
                      reads=[a0, h0, SM], writes=[a0])
                src, dst = a0, a1
                for lv in range(8):
                    d = 1 << lv
                    mr = SM.ap[:, I_M + lv, j:j + 1]
                    mi = SM.ap[:, I_M + 8 + lv, j:j + 1]
                    mn = SM.ap[:, I_M + 16 + lv, j:j + 1]
                    n = 256 - d
                    P.dve(lambda e, s=src, t=dst, mr=mr, d=d, n=n: e.scalar_tensor_tensor(
                        out=t.ap[:, 0, d:256], in0=s.ap[:, 0, 0:n], scalar=mr, in1=s.ap[:, 0, d:256], op0=ALU.mult, op1=ALU.add),
                        reads=[src, SM], writes=[dst])
                    P.dve(lambda e, s=src, t=dst, mn=mn, d=d, n=n: e.scalar_tensor_tensor(
                        out=t.ap[:, 0, d:256], in0=s.ap[:, 1, 0:n], scalar=mn, in1=t.ap[:, 0, d:256], op0=ALU.mult, op1=ALU.add),
                        reads=[src, dst, SM], writes=[dst])
                    P.dve(lambda e, s=src, t=dst, mr=mr, d=d, n=n: e.scalar_tensor_tensor(
                        out=t.ap[:, 1, d:256], in0=s.ap[:, 1, 0:n], scalar=mr, in1=s.ap[:, 1, d:256], op0=ALU.mult, op1=ALU.add),
                        reads=[src, SM], writes=[dst])
                    P.dve(lambda e, s=src, t=dst, mi=mi, d=d, n=n: e.scalar_tensor_tensor(
                        out=t.ap[:, 1, d:256], in0=s.ap[:, 0, 0:n], scalar=mi, in1=t.ap[:, 1, d:256], op0=ALU.mult, op1=ALU.add),
                        reads=[src, dst, SM], writes=[dst])
                    P.pool(lambda e, s=src, t=dst, d=d: e.tensor_copy(out=t.ap[:, :, 0:d], in_=s.ap[:, :, 0:d]), reads=[src], writes=[dst])
                    src, dst = dst, src
                fin = src
                P.pool(lambda e, fin=fin, j=j: e.tensor_copy(out=FS.ap[:, :, 0, j], in_=fin.ap[:, :, 255]), reads=[fin], writes=[FS])
                P.pool(lambda e, fin=fin, j=j: e.tensor_copy(out=FS.ap[:, :, 1:5, j], in_=fin.ap[:, :, 256:260]), reads=[fin], writes=[FS])
                P.pool(lambda e, hp=hp, p4=p4: e.memset(hp.ap[:, p4, :, 0:1], 0.0), writes=[hp])
                P.act(lambda e, hp=hp, p4=p4, fin=fin: e.activation(out=hp.ap[:, p4, :, 1:256], in_=fin.ap[:, :, 0:255], func=AF.Copy),
                      reads=[fin], writes=[hp])
                P.act(lambda e, hp=hp, p4=p4, j=j: e.activation(out=hp.ap[:, p4, :, 256:260], in_=h0.ap[:, :, j, :], func=AF.Copy),
                      reads=[h0], writes=[hp])
            caps = []
            for p4 in range(4):
                with P.capture() as cap_:
                    pair_body(p4)
                caps.append(cap_)
            P.zip(caps[0], caps[1])
            P.zip(caps[2], caps[3])
            for half in range(2):
                for t4 in range(4):
                    tau = half * 4 + t4
                    b = 4 + t4
                    for tp in range(tau + 1):
                        P.pe(lambda e, b=b, tau=tau, tp=tp, quad=quad: e.matmul(
                            ps[b][:, 0:NCH], lhsT=KT.ap[:, tau - tp, quad, :], rhs=uTb.ap[:, quad, tp:NT:8],
                            start=(tp == 0), stop=False), reads=[KT, uTb], writes=psb(b))
                    for p4 in range(4):
                        j = quad * 4 + p4
                        pb = 32 * p4
                        out = ps[b][pb:pb + 32, 0:NCH]
                        P.pe(lambda e, out=out, tau=tau, j=j, hp=hp, p4=p4, pb=pb: e.matmul(
                            out, lhsT=WC.ap[:, tau, j, 0, :], rhs=hp.ap[:, p4, 0, :], start=False, stop=False, tile_position=(0, pb)),
                            reads=[WC, hp], writes=psb(b))
                        P.pe(lambda e, out=out, tau=tau, j=j, hp=hp, p4=p4, pb=pb: e.matmul(
                            out, lhsT=WC.ap[:, tau, j, 1, :], rhs=hp.ap[:, p4, 1, :], start=False, stop=True, tile_position=(0, pb)),
                            reads=[WC, hp], writes=psb(b))
                for t4 in range(4):
                    tau = half * 4 + t4
                    b = 4 + t4
                    P.act(lambda e, b=b, tau=tau, quad=quad: e.activation(out=mix.ap[:, quad, tau:NT:8], in_=ps[b][:, 0:NCH],
                                                                           func=AF.Gelu_apprx_tanh), reads=psb(b), writes=[mix])
        for ti, (c0, n) in enumerate(self.ntiles()):
            for m in range(4):
                b = m % 4
                for k in range(4):
                    P.pe(lambda e, b=b, m=m, k=k, c0=c0, n=n: e.matmul(ps[b][:, 0:n], lhsT=wglu.ap[:, k, m * 128:(m + 1) * 128],
                                                                       rhs=mix.ap[:, k, c0:c0 + n], start=(k == 0), stop=(k == 3)),
                         reads=[wglu, mix], writes=psb(b))
                P.act(lambda e, b=b, m=m, n=n: e.activation(out=sg.ap[:, m, 0:n], in_=ps[b][:, 0:n], func=AF.Sigmoid,
                                                            bias=bglu.ap[:, m:m + 1]), reads=psb(b) + [bglu], writes=[sg])
            P.dve(lambda e, c0=c0, n=n: e.tensor_tensor(out=mix.ap[:, 0:4, c0:c0 + n], in0=mix.ap[:, 0:4, c0:c0 + n], in1=sg.ap[:, :, 0:n],
                                                        op=ALU.mult), reads=[mix, sg], writes=[mix])
        FT = A.alloc([2, 128])
        for ri in range(2):
            b = ri
            P.pe(lambda e, b=b, ri=ri: e.transpose(out=ps[b][0:80, 0:128], in_=FS.ap[:, ri, :, :].rearrange("p a b -> p (a b)"),
                                                   identity=self.ident.ap), reads=[FS, self.ident], writes=psb(b))
            P.act(lambda e, b=b, ri=ri: e.activation(out=FT.ap[0:80, ri, :], in_=ps[b][0:80, 0:128], func=AF.Copy),
                  reads=psb(b), writes=[FT])
            op_, os_ = (self.o_s5re_p, self.o_s5re_s) if ri == 0 else (self.o_s5im_p, self.o_s5im_s)
            P.dma(lambda e, ri=ri, op_=op_: e.dma_start(out=op_.rearrange("(p q) -> p q", q=128), in_=FT.ap[0:16, ri, :]),
                  reads=[FT], writes=["o_s5p%d" % ri])
            for s in range(NSQ):
                P.dma(lambda e, ri=ri, os_=os_, s=s: e.dma_start(out=os_[s].rearrange("(p q) -> p q", q=128),
                                                                 in_=FT.ap[16 * (s + 1):16 * (s + 2), ri, :]),
                      reads=[FT], writes=["o_s5s%d_%d" % (ri, s)])
        self.dbgsrc.update({"mix": (mix, [128, 8 * NT], BF16), "uTb": (uTb, [128, 4 * NT], BF16), "KT": (KT, [128, 8 * 4 * 128], BF16),
                       "WBT": (WBT, [128, 8 * 4 * 2 * 128], BF16), "WC": (WC, [128, 8 * 16 * 2 * 32], BF16), "SM": (SM, [128, 66 * 16], F32)})
        self.s5_end_mark = A.mark()

    def stage_ssd(self):
        P, A, nc = self.P, self.A, self.nc
        ps = self.ps
        mix = self.mix
        A.reset(0)
        NE = 2051 + 11 * NSQ
        xb = A.alloc([8, NE], BF16)
        wz = A.alloc([8, 512], BF16)
        wdt = A.alloc([8, 8], BF16)
        cw = A.alloc([8, 4]); cbias = A.alloc([8])
        cst = A.alloc([8, 12]); cso = A.alloc([8, 15])
        dtb = A.alloc([8]); aneg = A.alloc([8]); dsk = A.alloc([8]); ng = A.alloc([512])
        hT = A.alloc([512]); hTb = A.alloc([512], BF16)
        m_stage = A.mark()
        wx = A.alloc([8, 1024], BF16)
        ext = [A.alloc([NE]) for _ in range(2)]
        acc = [A.alloc([NE]) for _ in range(2)]
        craw = A.alloc([1024])
        ncd = lambda src: dict(in_=src, allow_slow_non_contiguous=True)
        P.dma(lambda e: e.dma_start(out=wx.ap, in_=self.w_in0_d[:, 1024:2048].rearrange("(k p) n -> p k n", p=128)), writes=[wx], eng="pool")
        P.dma(lambda e: e.dma_start(out=wz.ap, in_=self.w_in0_d[:, 512:1024].rearrange("(k p) n -> p k n", p=128)), writes=[wz], eng="pool")
        P.dma(lambda e: e.dma_start(out=wdt.ap, in_=self.w_in0_d[:, 2048:2056].rearrange("(k p) n -> p k n", p=128)), writes=[wdt], eng="pool")
        for k in range(4):
            P.dma(lambda e, k=k: e.dma_start(out=cw.ap[:, :, k], **ncd(self.ssd_conv_w_d[k].rearrange("(t p) -> p t", p=128))), writes=[cw])
        P.dma(lambda e: e.dma_start(out=cbias.ap, **ncd(self.ssd_conv_b_d.rearrange("(t p) -> p t", p=128))), writes=[cbias])
        P.dma(lambda e: e.dma_start(out=dtb.ap, in_=self.ssd_dt_bias_d.partition_broadcast(128)), writes=[dtb])
        P.dma(lambda e: e.dma_start(out=aneg.ap, in_=self.ssd_a_log_d.partition_broadcast(128)), writes=[aneg])
        P.dma(lambda e: e.dma_start(out=dsk.ap, in_=self.ssd_d_d.partition_broadcast(128)), writes=[dsk])
        P.dma(lambda e: e.dma_start(out=ng.ap, in_=self.ssd_norm_g_d.partition_broadcast(128)), writes=[ng])
        P.act(lambda e: e.activation(out=aneg.ap, in_=aneg.ap, func=AF.Exp), reads=[aneg], writes=[aneg])
        P.dve(lambda e: e.tensor_scalar(out=aneg.ap, in0=aneg.ap, scalar1=-1.0, scalar2=None, op0=ALU.mult), reads=[aneg], writes=[aneg])
        P.dma(lambda e: e.dma_start(out=craw.ap[0:12, :], in_=self.ssdconv0_d[:, :]), writes=[craw])
        for m in range(8):
            P.pe(lambda e, m=m: e.transpose(out=ps[0][:, m * 12:(m + 1) * 12], in_=craw.ap[0:12, m * 128:(m + 1) * 128],
                                            identity=self.ident.ap[0:12, 0:12]), reads=[craw, self.ident], writes=psb(0))
        P.act(lambda e: e.activation(out=cst.ap.rearrange("p a b -> p (a b)"), in_=ps[0][:, 0:96], func=AF.Copy), reads=psb(0), writes=[cst])
        for i in range(2):
            P.pool(lambda e, i=i: e.memset(ext[i].ap[:, 0:3], 0.0), writes=[ext[i]])
        for m in range(8):
            ex, ac = ext[m % 2], acc[m % 2]
            for ti, (c0, n) in enumerate(self.ntiles()):
                b = 1 + (m * 5 + ti) % 3
                for k in range(8):
                    P.pe(lambda e, b=b, m=m, k=k, c0=c0, n=n: e.matmul(
                        ps[b][:, 0:n], lhsT=wx.ap[:, k, m * 128:(m + 1) * 128], rhs=self.xhi.ap[:, k, c0:c0 + n],
                        start=(k == 0), stop=(k == 7)), reads=[wx] + self.xt("xhi", k, k + 1, c0, c0 + n), writes=psb(b))
                if c0 < LP:
                    P.act(lambda e, b=b, ex=ex, c0=c0, n=n: e.activation(out=ex.ap[:, 3 + c0:3 + c0 + n], in_=ps[b][:, 0:n], func=AF.Copy),
                          reads=psb(b), writes=[ex])
                else:
                    P.act(lambda e, b=b, ex=ex: e.activation(out=ex.ap[:, 2051:NE].rearrange("p (s c) -> p s c", c=11)[:, :, 3:11],
                                                             in_=ps[b][:, 0:NSQ * LS].rearrange("p (s c) -> p s c", c=LS), func=AF.Copy),
                          reads=psb(b), writes=[ex])
            P.pool(lambda e, ex=ex, m=m: e.tensor_copy(out=ex.ap[:, 2051:NE].rearrange("p (s c) -> p s c", c=11)[:, :, 0:3],
                                                       in_=cst.ap[:, m, :].rearrange("p (s c) -> p s c", c=3)), reads=[cst], writes=[ex])
            P.pool(lambda e, ex=ex, m=m: e.tensor_copy(out=cso.ap[:, m, 0:3], in_=ex.ap[:, 2048:2051]), reads=[ex], writes=[cso])
            P.pool(lambda e, ex=ex, m=m: e.tensor_copy(out=cso.ap[:, m, 3:15].rearrange("p (s c) -> p s c", c=3),
                                                       in_=ex.ap[:, 2051:NE].rearrange("p (s c) -> p s c", c=11)[:, :, 8:11]), reads=[ex], writes=[cso])
            P.act(lambda e, ex=ex, ac=ac, m=m: e.activation(out=ac.ap[:, 3:NE], in_=ex.ap[:, 3:NE], func=AF.Identity,
                                                            scale=cw.ap[:, m, 3:4], bias=cbias.ap[:, m:m + 1]), reads=[ex, cw, cbias], writes=[ac])
            for tap in range(3):
                P.dve(lambda e, ex=ex, ac=ac, m=m, tap=tap: e.scalar_tensor_tensor(
                    out=ac.ap[:, 3:NE], in0=ex.ap[:, tap:NE - 3 + tap], scalar=cw.ap[:, m, tap:tap + 1], in1=ac.ap[:, 3:NE],
                    op0=ALU.mult, op1=ALU.add), reads=[ex, ac, cw], writes=[ac])
            P.act(lambda e, ac=ac, m=m: e.activation(out=xb.ap[:, m, 3:NE], in_=ac.ap[:, 3:NE], func=AF.Silu), reads=[ac], writes=[xb])
        A.reset(m_stage)
        cso_t = A.alloc([1024])
        for m in range(8):
            b = 1 + m // 4
            P.pe(lambda e, b=b, m=m: e.transpose(out=ps[b][0:15, (m % 4) * 128:(m % 4 + 1) * 128], in_=cso.ap[:, m, :], identity=self.ident.ap),
                 reads=[cso, self.ident], writes=psb(b))
        for hb in range(2):
            P.act(lambda e, hb=hb: e.activation(out=cso_t.ap[0:15, hb * 512:(hb + 1) * 512], in_=ps[1 + hb][0:15, :], func=AF.Copy),
                  reads=psb(1 + hb), writes=[cso_t])
        P.dma(lambda e: e.dma_start(out=self.o_ssdconv_p[:, :], in_=cso_t.ap[0:3, :]), reads=[cso_t], writes=["o_ssdconv_p"])
        for s_ in range(NSQ):
            P.dma(lambda e, s_=s_: e.dma_start(out=self.o_ssdconv_s[s_], in_=cso_t.ap[3 + 3 * s_:6 + 3 * s_, :]), reads=[cso_t], writes=["o_ssdconv_s%d" % s_])

        dtr = A.alloc([8]); dt_ = A.alloc([8]); dtA = A.alloc([8]); cs = A.alloc([8]); te = A.alloc([8])
        ecT2 = [A.alloc([8]) for _ in range(2)]
        rhsR = A.alloc([8, 128])
        Ebuf = A.alloc([8, 128])
        Mb2 = [A.alloc([8, 128], BF16) for _ in range(2)]
        cbm = A.alloc([2, 128])
        Cd2 = [A.alloc([8, 128], BF16) for _ in range(2)]
        eR = A.alloc([8, 128])
        xs2 = [A.alloc([512], BF16) for _ in range(2)]
        Bt2 = [A.alloc([256], BF16) for _ in range(2)]
        xdt2 = [A.alloc([512], BF16) for _ in range(2)]
        xdte2 = [A.alloc([512], BF16) for _ in range(2)]
        sz2 = [A.alloc([512]) for _ in range(2)]
        y1 = A.alloc([512]); junk = A.alloc([512]); ms = A.alloc([2]); ys_tok = A.alloc([512], BF16)
        stio = A.alloc([4, 128])

        def bq(ap, shape):
            return ap.to_broadcast(shape)

        def front(ci, Tc, tc0, ec0):
            par = ci % 2
            Mb, Cd, xs_tok, B_tok, xdt, xdte, sz, ecT = Mb2[par], Cd2[par], xs2[par], Bt2[par], xdt2[par], xdte2[par], sz2[par], ecT2[par]
            T = slice(0, Tc)
            for k in range(8):
                P.pe(lambda e, k=k: e.matmul(ps[0][T, 0:8], lhsT=self.xhi.ap[:, k, tc0:tc0 + Tc], rhs=wdt.ap[:, k, :], start=(k == 0), stop=(k == 7)),
                     reads=[wdt] + self.xt("xhi", k, k + 1, tc0, tc0 + Tc), writes=psb(0))
            P.dve(lambda e: e.tensor_tensor(out=dtr.ap[T, :], in0=ps[0][T, 0:8], in1=dtb.ap[T, :], op=ALU.add), reads=psb(0) + [dtb], writes=[dtr])
            P.act(lambda e: e.activation(out=dtr.ap[T, :], in_=dtr.ap[T, :], func=AF.Exp), reads=[dtr], writes=[dtr])
            P.act(lambda e: e.activation(out=dt_.ap[T, :], in_=dtr.ap[T, :], func=AF.Ln, bias=1.0), reads=[dtr], writes=[dt_])
            P.dve(lambda e: e.tensor_tensor(out=dtA.ap[T, :], in0=dt_.ap[T, :], in1=aneg.ap[T, :], op=ALU.mult), reads=[dt_, aneg], writes=[dtA])
            P.pe(lambda e: e.matmul(ps[0][T, 8:16], lhsT=self.utri.ap[T, T], rhs=dtA.ap[T, :], start=True, stop=True), reads=[self.utri, dtA], writes=psb(0))
            P.act(lambda e: e.activation(out=cs.ap[T, :], in_=ps[0][T, 8:16], func=AF.Copy), reads=psb(0), writes=[cs])
            P.dve(lambda e: e.tensor_tensor(out=rhsR.ap[T, :, T], in0=bq(self.utri.ap[T, T].unsqueeze(1), [Tc, 8, Tc]),
                                            in1=bq(dtA.ap[T, :].unsqueeze(2), [Tc, 8, Tc]), op=ALU.mult), reads=[self.utri, dtA], writes=[rhsR])
            for hb in range(2):
                P.pe(lambda e, hb=hb: e.matmul(ps[1 + hb][:, 0:4 * Tc].rearrange("p (h t) -> p h t", h=4), lhsT=self.ones.ap[T, :],
                                               rhs=rhsR.ap[T, 4 * hb:4 * hb + 4, T], start=True, stop=True), reads=[self.ones, rhsR], writes=psb(1 + hb))
            Rv = [ps[1 + hb][:, 0:4 * Tc].rearrange("p (h t) -> p h t", h=4) for hb in range(2)]
            for hb in range(2):
                P.dve(lambda e, hb=hb: e.tensor_tensor(out=Ebuf.ap[T, 4 * hb:4 * hb + 4, T], in0=Rv[hb][T, :, :],
                                                       in1=bq(cs.ap[T, 4 * hb:4 * hb + 4].unsqueeze(2), [Tc, 4, Tc]), op=ALU.subtract),
                      reads=psb(1 + hb) + [cs], writes=[Ebuf])
            P.dve(lambda e: e.tensor_scalar(out=Ebuf.ap[T, :, T], in0=Ebuf.ap[T, :, T], scalar1=0.0, scalar2=None, op0=ALU.min), reads=[Ebuf], writes=[Ebuf])
            P.act(lambda e: e.activation(out=Ebuf.ap[T, :, T], in_=Ebuf.ap[T, :, T], func=AF.Exp), reads=[Ebuf], writes=[Ebuf])
            for g in range(2):
                P.pe(lambda e, g=g: e.matmul(ps[3][T, g * 128:g * 128 + Tc], lhsT=xb.ap[:, 4 + g, ec0:ec0 + Tc], rhs=xb.ap[:, 6 + g, ec0:ec0 + Tc],
                                             start=True, stop=True), reads=[xb], writes=psb(3))
            P.dve(lambda e: e.tensor_tensor(out=cbm.ap[T, :, T], in0=ps[3][T, 0:256].rearrange("p (g t) -> p g t", g=2)[:, :, T],
                                            in1=bq(self.utri.ap[T, T].unsqueeze(1), [Tc, 2, Tc]), op=ALU.mult), reads=psb(3) + [self.utri], writes=[cbm])
            for g in range(2):
                P.dve(lambda e, g=g: e.tensor_tensor(out=Mb.ap[T, 4 * g:4 * g + 4, T], in0=Ebuf.ap[T, 4 * g:4 * g + 4, T],
                                                     in1=bq(cbm.ap[T, g, T].unsqueeze(1), [Tc, 4, Tc]), op=ALU.mult), reads=[Ebuf, cbm], writes=[Mb])
            for hb in range(2):
                P.act(lambda e, hb=hb: e.activation(out=eR.ap[:, 4 * hb:4 * hb + 4, T], in_=Rv[hb], func=AF.Exp), reads=psb(1 + hb), writes=[eR])
            for g in range(2):
                P.dve(lambda e, g=g: e.tensor_tensor(out=Cd.ap[:, 4 * g:4 * g + 4, T], in0=eR.ap[:, 4 * g:4 * g + 4, T],
                                                     in1=bq(xb.ap[:, 6 + g, ec0:ec0 + Tc].unsqueeze(1), [128, 4, Tc]), op=ALU.mult), reads=[eR, xb], writes=[Cd])
            for hb in range(2):
                P.dve(lambda e, hb=hb: e.tensor_tensor(out=te.ap[T, 4 * hb:4 * hb + 4], in0=Rv[hb][T, :, Tc - 1], in1=cs.ap[T, 4 * hb:4 * hb + 4],
                                                       op=ALU.subtract), reads=psb(1 + hb) + [cs], writes=[te])
                P.act(lambda e, hb=hb: e.activation(out=ecT.ap[:, 4 * hb:4 * hb + 4], in_=Rv[hb][:, :, Tc - 1], func=AF.Exp), reads=psb(1 + hb), writes=[ecT])
            P.act(lambda e: e.activation(out=te.ap[T, :], in_=te.ap[T, :], func=AF.Exp), reads=[te], writes=[te])
            pt = ps[4][:, :].bitcast(BF16)
            for m in range(6):
                P.pe(lambda e, m=m: e.transpose(out=pt[T, m * 128:(m + 1) * 128], in_=xb.ap[:, m, ec0:ec0 + Tc], identity=self.identb.ap),
                     reads=[xb, self.identb], writes=psb(4))
            P.act(lambda e: e.activation(out=xs_tok.ap[T, :], in_=pt[T, 0:512], func=AF.Copy), reads=psb(4), writes=[xs_tok])
            P.act(lambda e: e.activation(out=B_tok.ap[T, :], in_=pt[T, 512:768], func=AF.Copy), reads=psb(4), writes=[B_tok])
            P.dve(lambda e: e.tensor_tensor(out=xdt.ap[T, :].rearrange("p (h q) -> p h q", h=8), in0=xs_tok.ap[T, :].rearrange("p (h q) -> p h q", h=8),
                                            in1=bq(dt_.ap[T, :].unsqueeze(2), [Tc, 8, 64]), op=ALU.mult), reads=[xs_tok, dt_], writes=[xdt])
            P.dve(lambda e: e.tensor_tensor(out=xdte.ap[T, :].rearrange("p (h q) -> p h q", h=8), in0=xdt.ap[T, :].rearrange("p (h q) -> p h q", h=8),
                                            in1=bq(te.ap[T, :].unsqueeze(2), [Tc, 8, 64]), op=ALU.mult), reads=[xdt, te], writes=[xdte])
            for k in range(8):
                P.pe(lambda e, k=k: e.matmul(ps[7][T, :], lhsT=self.xhi.ap[:, k, tc0:tc0 + Tc], rhs=wz.ap[:, k, :], start=(k == 0), stop=(k == 7)),
                     reads=[wz] + self.xt("xhi", k, k + 1, tc0, tc0 + Tc), writes=psb(7))
            P.act(lambda e: e.activation(out=sz.ap[T, :], in_=ps[7][T, :], func=AF.Silu), reads=psb(7), writes=[sz])

        def back(ci, Tc, tc0):
            par = ci % 2
            Mb, Cd, xs_tok, B_tok, xdt, xdte, sz, ecT = Mb2[par], Cd2[par], xs2[par], Bt2[par], xdt2[par], xdte2[par], sz2[par], ecT2[par]
            T = slice(0, Tc)
            for h in range(8):
                P.pe(lambda e, h=h: e.matmul(ps[5][T, 64 * h:64 * h + 64], lhsT=Mb.ap[T, h, T], rhs=xdt.ap[T, 64 * h:64 * h + 64], start=True, stop=False),
                     reads=[Mb, xdt], writes=psb(5))
                P.pe(lambda e, h=h: e.matmul(ps[5][T, 64 * h:64 * h + 64], lhsT=Cd.ap[:, h, T], rhs=hTb.ap[:, 64 * h:64 * h + 64], start=False, stop=True),
                     reads=[Cd, hTb], writes=psb(5))
            for g in range(2):
                P.pe(lambda e, g=g: e.matmul(ps[6][:, 256 * g:256 * g + 256], lhsT=B_tok.ap[T, 128 * g:128 * g + 128], rhs=xdte.ap[T, 256 * g:256 * g + 256],
                                             start=True, stop=True), reads=[B_tok, xdte], writes=psb(6))
            P.dve(lambda e: e.tensor_tensor(out=hT.ap.rearrange("p (h q) -> p h q", h=8), in0=hT.ap.rearrange("p (h q) -> p h q", h=8),
                                            in1=bq(ecT.ap.unsqueeze(2), [128, 8, 64]), op=ALU.mult), reads=[hT, ecT], writes=[hT])
            P.dve(lambda e: e.tensor_tensor(out=hT.ap, in0=hT.ap, in1=ps[6][:, :], op=ALU.add), reads=[hT] + psb(6), writes=[hT])
            P.pool(lambda e: e.tensor_copy(out=hTb.ap, in_=hT.ap), reads=[hT], writes=[hTb])
            P.dve(lambda e: e.tensor_tensor(out=y1.ap[T, :].rearrange("p (h q) -> p h q", h=8), in0=xs_tok.ap[T, :].rearrange("p (h q) -> p h q", h=8),
                                            in1=bq(dsk.ap[T, :].unsqueeze(2), [Tc, 8, 64]), op=ALU.mult), reads=[xs_tok, dsk], writes=[y1])
            P.dve(lambda e: e.tensor_tensor(out=y1.ap[T, :], in0=y1.ap[T, :], in1=ps[5][T, :], op=ALU.add), reads=[y1] + psb(5), writes=[y1])
            P.dve(lambda e: e.tensor_tensor(out=y1.ap[T, :], in0=y1.ap[T, :], in1=sz.ap[T, :], op=ALU.mult), reads=[y1, sz], writes=[y1])
            for g in range(2):
                P.act(lambda e, g=g: e.activation(out=junk.ap[T, 256 * g:256 * g + 256], in_=y1.ap[T, 256 * g:256 * g + 256], func=AF.Square,
                                                  accum_out=ms.ap[T, g:g + 1]), reads=[y1], writes=[junk, ms])
            P.dve(lambda e: e.tensor_scalar(out=ms.ap[T, :], in0=ms.ap[T, :], scalar1=1.0 / 256, scalar2=EPS, op0=ALU.mult, op1=ALU.add), reads=[ms], writes=[ms])
            P.act(lambda e: e.activation(out=ms.ap[T, :], in_=ms.ap[T, :], func=AF.Sqrt), reads=[ms], writes=[ms])
            P.dve(lambda e: e.reciprocal(out=ms.ap[T, :], in_=ms.ap[T, :]), reads=[ms], writes=[ms])
            P.dve(lambda e: e.tensor_tensor(out=y1.ap[T, :].rearrange("p (g q) -> p g q", g=2), in0=y1.ap[T, :].rearrange("p (g q) -> p g q", g=2),
                                            in1=bq(ms.ap[T, :].unsqueeze(2), [Tc, 2, 256]), op=ALU.mult), reads=[y1, ms], writes=[y1])
            P.dve(lambda e: e.tensor_tensor(out=ys_tok.ap[T, :], in0=y1.ap[T, :], in1=ng.ap[T, :], op=ALU.mult), reads=[y1, ng], writes=[ys_tok])
            pt = ps[6][:, :].bitcast(BF16)
            for c in range(4):
                P.pe(lambda e, c=c: e.transpose(out=pt[:, c * 128:c * 128 + Tc], in_=ys_tok.ap[T, c * 128:(c + 1) * 128], identity=self.identb.ap[T, T]),
                     reads=[ys_tok, self.identb], writes=psb(6))
            P.act(lambda e: e.activation(out=mix.ap[:, 4:8, tc0:tc0 + Tc], in_=pt[:, 0:512].rearrange("p (c t) -> p c t", c=4)[:, :, T], func=AF.Copy),
                  reads=psb(6), writes=[mix])

        def state_out(dst):
            for c in range(4):
                P.pe(lambda e, c=c: e.transpose(out=ps[6][:, c * 128:(c + 1) * 128], in_=hT.ap[:, c * 128:(c + 1) * 128], identity=self.ident.ap),
                     reads=[hT, self.ident], writes=psb(6))
            P.act(lambda e: e.activation(out=stio.ap.rearrange("p a b -> p (a b)"), in_=ps[6][:, :], func=AF.Copy), reads=psb(6), writes=[stio])
            P.dma(lambda e: e.dma_start(out=dst.rearrange("(c p) n -> p c n", p=128), in_=stio.ap), reads=[stio], writes=["o_ssd_state"])

        def state_in(s_):
            P.dma(lambda e: e.dma_start(out=stio.ap, in_=self.ssd0_d[s_].rearrange("(c p) n -> p c n", p=128)), writes=[stio])
            for c in range(4):
                P.pe(lambda e, c=c: e.transpose(out=ps[6][:, c * 128:(c + 1) * 128], in_=stio.ap[:, c, :], identity=self.ident.ap),
                     reads=[stio, self.ident], writes=psb(6))
            P.act(lambda e: e.activation(out=hT.ap, in_=ps[6][:, :], func=AF.Copy), reads=psb(6), writes=[hT])
            P.pool(lambda e: e.tensor_copy(out=hTb.ap, in_=hT.ap), reads=[hT], writes=[hTb])

        P.pool(lambda e: e.memset(hT.ap, 0.0), writes=[hT])
        P.pool(lambda e: e.memset(hTb.ap, 0.0), writes=[hTb])
        chunks = [(128, i * 128, 3 + i * 128, None) for i in range(LP // 128)] + [(LS, LP + LS * s_, 2051 + 11 * s_ + 3, s_) for s_ in range(NSQ)]
        front(0, *chunks[0][0:3])
        for ci, (Tc, tc0, ec0, s_) in enumerate(chunks):
            with P.capture() as la:
                if s_ is not None:
                    state_in(s_)
                back(ci, Tc, tc0)
                if ci == LP // 128 - 1:
                    state_out(self.o_ssd_p)
                if s_ is not None:
                    state_out(self.o_ssd_s[s_])
            with P.capture() as lb:
                if ci + 1 < len(chunks):
                    front(ci + 1, *chunks[ci + 1][0:3])
            P.zip(la, lb)
        self.dbgsrc["mix"] = (mix, [128, 8 * NT], BF16)
        self.dbgsrc["xb"] = (xb, [128, 8 * NE], BF16)

    def ln_params(self, g_d, b_d, idx):
        g = self.sbt(f"lng{idx}", [128, 8])
        b = self.sbt(f"lnb{idx}", [128, 8])
        self.P.dma(lambda e: e.dma_start(out=g.ap, in_=g_d.rearrange("(m p) -> p m", p=128), allow_slow_non_contiguous=True), writes=[g])
        self.P.dma(lambda e: e.dma_start(out=b.ap, in_=b_d.rearrange("(m p) -> p m", p=128), allow_slow_non_contiguous=True), writes=[b])
        return g, b

    def ln_tile(self, R, c0, n, g, b, tmp, final_out=None, stat_banks=(6, 7)):
        P, ps = self.P, self.ps
        rb, rq, mt, msq = tmp
        b6, b7 = stat_banks
        Rn = R.ap[:, :, 0:n]
        P.act(lambda e: e.activation(out=rb.ap[:, :, 0:n], in_=Rn, func=AF.Copy), reads=[R], writes=[rb])
        P.act(lambda e: e.activation(out=rq.ap[:, :, 0:n], in_=Rn, func=AF.Square), reads=[R], writes=[rq])
        for m in range(8):
            P.pe(lambda e, m=m: e.matmul(ps[b6][:, 0:n], lhsT=self.onesb.ap, rhs=rb.ap[:, m, 0:n], start=(m == 0), stop=(m == 7)),
                 reads=[self.onesb, rb], writes=psb(b6))
        for m in range(8):
            P.pe(lambda e, m=m: e.matmul(ps[b7][:, 0:n], lhsT=self.onesb.ap, rhs=rq.ap[:, m, 0:n], start=(m == 0), stop=(m == 7)),
                 reads=[self.onesb, rq], writes=psb(b7))
        P.dve(lambda e: e.tensor_scalar(out=mt.ap[:, 0:n], in0=ps[b6][:, 0:n], scalar1=1.0 / D, scalar2=None, op0=ALU.mult), reads=psb(b6), writes=[mt])
        P.dve(lambda e: e.tensor_tensor(out=msq.ap[:, 0:n], in0=mt.ap[:, 0:n], in1=mt.ap[:, 0:n], op=ALU.mult), reads=[mt], writes=[msq])
        P.dve(lambda e: e.scalar_tensor_tensor(out=msq.ap[:, 0:n], in0=ps[b7][:, 0:n], scalar=1.0 / D, in1=msq.ap[:, 0:n], op0=ALU.mult, op1=ALU.subtract),
              reads=psb(b7) + [msq], writes=[msq])
        P.dve(lambda e: e.tensor_scalar(out=msq.ap[:, 0:n], in0=msq.ap[:, 0:n], scalar1=EPS, scalar2=None, op0=ALU.add), reads=[msq], writes=[msq])
        P.act(lambda e: e.activation(out=msq.ap[:, 0:n], in_=msq.ap[:, 0:n], func=AF.Sqrt), reads=[msq], writes=[msq])
        P.dve(lambda e: e.reciprocal(out=msq.ap[:, 0:n], in_=msq.ap[:, 0:n]), reads=[msq], writes=[msq])
        P.dve(lambda e: e.tensor_tensor(out=Rn, in0=Rn, in1=mt.ap[:, 0:n].unsqueeze(1).to_broadcast([128, 8, n]), op=ALU.subtract), reads=[R, mt], writes=[R])
        P.dve(lambda e: e.tensor_tensor(out=Rn, in0=Rn, in1=msq.ap[:, 0:n].unsqueeze(1).to_broadcast([128, 8, n]), op=ALU.mult), reads=[R, msq], writes=[R])
        for m in range(8):
            P.act(lambda e, m=m: e.activation(out=R.ap[:, m, 0:n], in_=R.ap[:, m, 0:n], func=AF.Identity, scale=g.ap[:, m:m + 1], bias=b.ap[:, m:m + 1]),
                  reads=[R, g, b], writes=[R])
        if final_out is None:
            thi = self.xt("xhi", 0, 8, c0, c0 + n)
            tlo = self.xt("xlo", 0, 8, c0, c0 + n)
            hi = self.xhi.ap[:, :, c0:c0 + n]
            P.act(lambda e: e.activation(out=hi, in_=Rn, func=AF.Copy), reads=[R], writes=thi)
            P.dve(lambda e: e.tensor_tensor(out=Rn, in0=Rn, in1=hi, op=ALU.subtract), reads=[R] + thi, writes=[R])
            P.pool(lambda e: e.tensor_copy(out=self.xlo.ap[:, :, c0:c0 + n], in_=Rn), reads=[R], writes=tlo)
        else:
            final_out(R, c0, n)

    def resid_into(self, Rdst, psrc, m, c0, n):
        P = self.P
        P.dve(lambda e: e.scalar_tensor_tensor(out=Rdst, in0=self.xhi.ap[:, m, c0:c0 + n], scalar=ALPHA, in1=psrc[0], op0=ALU.mult, op1=ALU.add),
              reads=self.xt("xhi", m, m + 1, c0, c0 + n) + psrc[1], writes=psrc[2])
        P.dve(lambda e: e.scalar_tensor_tensor(out=Rdst, in0=self.xlo.ap[:, m, c0:c0 + n], scalar=ALPHA, in1=Rdst, op0=ALU.mult, op1=ALU.add),
              reads=self.xt("xlo", m, m + 1, c0, c0 + n) + psrc[2], writes=psrc[2])

    def stage_outproj_ln(self, w_d, g, b, rhs_view, lo_mark):
        P, A, ps = self.P, self.A, self.ps
        A.reset(lo_mark)
        wo = A.alloc([8, D], BF16)
        Rs = [A.alloc([8, 512]) for _ in range(2)]
        tmps = [(A.alloc([8, 512], BF16), A.alloc([8, 512], BF16), A.alloc([512]), A.alloc([512])) for _ in range(2)]
        P.dma(lambda e: e.dma_start(out=wo.ap, in_=w_d.rearrange("(k p) n -> p k n", p=128)), writes=[wo], eng="pool")
        caps = []
        for ti, (c0, n) in enumerate(self.ntiles()):
            par = ti % 2
            R, tmp = Rs[par], tmps[par]
            banks = (0, 1) if par == 0 else (2, 3)
            stat = (6, 7) if par == 0 else (4, 5)
            with P.capture() as cap_:
                for m in range(8):
                    bnk = banks[m % 2]
                    for k in range(8):
                        P.pe(lambda e, bnk=bnk, m=m, k=k, c0=c0, n=n: e.matmul(ps[bnk][:, 0:n], lhsT=wo.ap[:, k, m * 128:(m + 1) * 128],
                                                                               rhs=rhs_view.ap[:, k, c0:c0 + n], start=(k == 0), stop=(k == 7)),
                             reads=[wo, rhs_view], writes=psb(bnk))
                    self.resid_into(R.ap[:, m, 0:n], (ps[bnk][:, 0:n], psb(bnk), [R]), m, c0, n)
                self.ln_tile(R, c0, n, g, b, tmp, stat_banks=stat)
            caps.append(cap_)
        P.zip(caps[0], caps[1])
        P.zip(caps[2], caps[3])
        P.zip(caps[4])

    def stage_ffn(self, layer, g, b, final=False):
        P, A, ps = self.P, self.A, self.ps
        A.top = A.nbytes
        A.reset(0)
        wup_d = self.ffn_w_up_d[layer]
        wdn_d = self.ffn_w_down_d[layer]
        NEA, NEB = 1026, 1026 + 10 * NSQ
        G = A.alloc([22, NEB], BF16)
        Rbig = A.alloc([8, NEB - 2])
        wd = [A.alloc([22, 128], BF16) for _ in range(2)]
        fcw = A.alloc([44, 3]); fcb = A.alloc([44]); fst = A.alloc([44, 2 * NSQ]); fco = A.alloc([44, 2 + 2 * NSQ]); hsave = A.alloc([44, 2])
        m_t = A.mark()
        wbl = [A.alloc([8, 512], BF16) for _ in range(3)]
        hraw = [A.alloc([NEB]) for _ in range(2)]
        acc = [A.alloc([NEB]) for _ in range(2)]
        m_end = A.mark()
        A.reset(m_t)
        tmp = (A.alloc([8, 256], BF16), A.alloc([8, 256], BF16), A.alloc([256]), A.alloc([256]))
        tmp2 = (A.alloc([8, 256], BF16), A.alloc([8, 256], BF16), A.alloc([256]), A.alloc([256]))
        io_t = A.alloc([1408])
        if final:
            self.obuf = [A.alloc([D]) for _ in range(2)]
        assert A.mark() <= m_end + 8192
        A.reset(max(m_end, A.mark()))
        ncd = lambda src: dict(in_=src, allow_slow_non_contiguous=True)
        for k in range(3):
            P.dma(lambda e, k=k: e.dma_start(out=fcw.ap[:, :, k], **ncd(self.ffn_conv_w_d[layer, k].rearrange("(t p) -> p t", p=128))), writes=[fcw])
        P.dma(lambda e: e.dma_start(out=fcb.ap, **ncd(self.ffn_conv_b_d[layer].rearrange("(t p) -> p t", p=128))), writes=[fcb])
        for q in range(4):
            P.dma(lambda e, q=q: e.dma_start(out=io_t.ap[0:2 * NSQ, :], in_=self.ffnconv0_d[layer][:, q * 1408:(q + 1) * 1408]), writes=[io_t])
            for t in range(11):
                P.pe(lambda e, q=q, t=t: e.transpose(out=ps[6][:, (q * 11 + t) * 8:(q * 11 + t) * 8 + 8], in_=io_t.ap[0:2 * NSQ, t * 128:(t + 1) * 128],
                                                     identity=self.ident.ap[0:2 * NSQ, 0:2 * NSQ]), reads=[io_t, self.ident], writes=psb(6))
        P.act(lambda e: e.activation(out=fst.ap.rearrange("p a b -> p (a b)"), in_=ps[6][:, 0:44 * 8], func=AF.Copy), reads=psb(6), writes=[fst])

        halves = [
            dict(ne=NEA, up=[(0, 512, "p", 2), (512, 512, "p", 514)], dn=[(2, 512, 0), (514, 512, 512)], sample=False),
            dict(ne=NEB, up=[(1024, 512, "p", 2), (1536, 512, "p", 514), (2048, NSQ * LS, "s", 1026)],
                 dn=[(2, 512, 1024), (514, 512, 1536), (1026, 10 * NSQ, 2048)], sample=True),
        ]
        cnt = 0
        wcnt = 0
        for hf in halves:
            ne = hf["ne"]
            for m in range(2):
                P.dma(lambda e, m=m: e.dma_start(out=wd[m].ap, in_=wdn_d[:, m * 128:(m + 1) * 128].rearrange("(k p) n -> p k n", p=128)),
                      writes=[wd[m]], eng="pool")
            order = []
            for j in range(22):
                order += [j, j + 22]
            def load_blk(q):
                bb, side = q // 2, q % 2
                if bb > 5:
                    return
                ncol = min(512, DFF - 512 * bb)
                c0w = side * DFF + 512 * bb
                wv_ = wbl[q % 3]
                P.dma(lambda e, wv_=wv_, c0w=c0w, ncol=ncol: e.dma_start(out=wv_.ap[:, :, 0:ncol], in_=wup_d[:, c0w:c0w + ncol].rearrange("(k p) n -> p k n", p=128)),
                      writes=[wv_], eng="pool")
            load_blk(0)
            load_blk(1)
            load_blk(2)
            for idx, j in enumerate(order):
                pi_ = idx // 2
                q_ = 2 * (pi_ // 4) + (idx % 2)
                wv_ = wbl[q_ % 3]
                wu = View(wv_.ap[:, :, (pi_ % 4) * 128:(pi_ % 4 + 1) * 128], wv_.toks)
                hr, ac = hraw[idx % 2], acc[idx % 2]
                if not hf["sample"]:
                    P.pool(lambda e, hr=hr: e.memset(hr.ap[:, 0:2], 0.0), writes=[hr])
                else:
                    P.pool(lambda e, hr=hr, j=j: e.tensor_copy(out=hr.ap[:, 0:2], in_=hsave.ap[:, j, :]), reads=[hsave], writes=[hr])
                for (c0, n, kind, e0) in hf["up"]:
                    bnk = cnt % 6
                    cnt += 1
                    for k in range(8):
                        P.pe(lambda e, bnk=bnk, wu=wu, k=k, c0=c0, n=n: e.matmul(ps[bnk][:, 0:n], lhsT=wu.ap[:, k, :], rhs=self.xhi.ap[:, k, c0:c0 + n],
                                                                                 start=(k == 0), stop=(k == 7)),
                             reads=[wu] + self.xt("xhi", k, k + 1, c0, c0 + n), writes=psb(bnk))
                    if kind == "p":
                        P.act(lambda e, bnk=bnk, hr=hr, e0=e0, n=n: e.activation(out=hr.ap[:, e0:e0 + n], in_=ps[bnk][:, 0:n], func=AF.Copy),
                              reads=psb(bnk), writes=[hr])
                    else:
                        P.act(lambda e, bnk=bnk, hr=hr: e.activation(out=hr.ap[:, 1026:NEB].rearrange("p (s c) -> p s c", c=10)[:, :, 2:10],
                                                                     in_=ps[bnk][:, 0:NSQ * LS].rearrange("p (s c) -> p s c", c=LS), func=AF.Copy),
                              reads=psb(bnk), writes=[hr])
                if not hf["sample"]:
                    P.pool(lambda e, hr=hr, j=j: e.tensor_copy(out=hsave.ap[:, j, :], in_=hr.ap[:, 1024:1026]), reads=[hr], writes=[hsave])
                if hf["sample"]:
                    P.pool(lambda e, hr=hr, j=j: e.tensor_copy(out=hr.ap[:, 1026:NEB].rearrange("p (s c) -> p s c", c=10)[:, :, 0:2],
                                                               in_=fst.ap[:, j, :].rearrange("p (s c) -> p s c", c=2)), reads=[fst], writes=[hr])
                    P.pool(lambda e, hr=hr, j=j: e.tensor_copy(out=fco.ap[:, j, 0:2], in_=hr.ap[:, 1024:1026]), reads=[hr], writes=[fco])
                    P.pool(lambda e, hr=hr, j=j: e.tensor_copy(out=fco.ap[:, j, 2:2 + 2 * NSQ].rearrange("p (s c) -> p s c", c=2),
                                                               in_=hr.ap[:, 1026:NEB].rearrange("p (s c) -> p s c", c=10)[:, :, 8:10]), reads=[hr], writes=[fco])
                P.act(lambda e, hr=hr, ac=ac, j=j, ne=ne: e.activation(out=ac.ap[:, 2:ne], in_=hr.ap[:, 2:ne], func=AF.Identity,
                                                                       scale=fcw.ap[:, j, 2:3], bias=fcb.ap[:, j:j + 1]), reads=[hr, fcw, fcb], writes=[ac])
                for tap in range(2):
                    P.dve(lambda e, hr=hr, ac=ac, j=j, tap=tap, ne=ne: e.scalar_tensor_tensor(
                        out=ac.ap[:, 2:ne], in0=hr.ap[:, tap:ne - 2 + tap], scalar=fcw.ap[:, j, tap:tap + 1], in1=ac.ap[:, 2:ne],
                        op0=ALU.mult, op1=ALU.add), reads=[hr, ac, fcw], writes=[ac])
                if j < 22:
                    P.act(lambda e, ac=ac, ne=ne: e.activation(out=ac.ap[:, 2:ne], in_=ac.ap[:, 2:ne], func=AF.Silu), reads=[ac], writes=[ac])
                else:
                    sa = acc[(idx - 1) % 2]
                    P.dve(lambda e, ac=ac, sa=sa, j=j, ne=ne: e.tensor_tensor(out=G.ap[:, j - 22, 2:ne], in0=sa.ap[:, 2:ne], in1=ac.ap[:, 2:ne], op=ALU.mult),
                          reads=[ac, sa], writes=[G])
                npair_in_grp = 4 if pi_ // 4 < 5 else 2
                if pi_ % 4 == npair_in_grp - 1:
                    load_blk(q_ + 3)
            self.mark('ffn%d_up_end' % layer)
            for m in range(8):
                w = wd[m % 2]
                if m >= 1 and m + 1 < 8:
                    wn = wd[(m + 1) % 2]
                    P.dma(lambda e, wn=wn, m=m: e.dma_start(out=wn.ap, in_=wdn_d[:, (m + 1) * 128:(m + 2) * 128].rearrange("(k p) n -> p k n", p=128)),
                          writes=[wn], eng="pool")
                for (e0, n, t0) in hf["dn"]:
                    bnk = cnt % 6
                    cnt += 1
                    for k in range(22):
                        P.pe(lambda e, bnk=bnk, w=w, k=k, e0=e0, n=n: e.matmul(ps[bnk][:, 0:n], lhsT=w.ap[:, k, :], rhs=G.ap[:, k, e0:e0 + n],
                                                                               start=(k == 0), stop=(k == 21)), reads=[w, G], writes=psb(bnk))
                    if t0 < LP:
                        self.resid_into(Rbig.ap[:, m, e0 - 2:e0 - 2 + n], (ps[bnk][:, 0:n], psb(bnk), [Rbig]), m, t0, n)
                    else:
                        Rd = Rbig.ap[:, m, e0 - 2:e0 - 2 + n].rearrange("p (s c) -> p s c", c=10)[:, :, 2:10]
                        pv = ps[bnk][:, 0:n].rearrange("p (s c) -> p s c", c=10)[:, :, 2:10]
                        xh = self.xhi.ap[:, m, LP:NT].rearrange("p (s c) -> p s c", c=LS)
                        xl = self.xlo.ap[:, m, LP:NT].rearrange("p (s c) -> p s c", c=LS)
                        P.dve(lambda e, Rd=Rd, pv=pv, xh=xh: e.scalar_tensor_tensor(out=Rd, in0=xh, scalar=ALPHA, in1=pv, op0=ALU.mult, op1=ALU.add),
                              reads=self.xt("xhi", m, m + 1, LP, NT) + psb(bnk), writes=[Rbig])
                        P.dve(lambda e, Rd=Rd, xl=xl: e.scalar_tensor_tensor(out=Rd, in0=xl, scalar=ALPHA, in1=Rd, op0=ALU.mult, op1=ALU.add),
                              reads=self.xt("xlo", m, m + 1, LP, NT) + [Rbig], writes=[Rbig])
            self.mark('ffn%d_down_end' % layer)
            for (e0, n, t0) in hf["dn"]:
                if t0 < LP:
                    caps_ = []
                    for hh in range(2):
                        Rv = View(Rbig.ap[:, :, e0 - 2 + hh * 256:e0 - 2 + hh * 256 + 256], ["Rbig.%d.%d" % (e0, hh)])
                        with P.capture() as cp_:
                            P.dve(lambda e: e.engine_nop(), reads=[Rbig], writes=Rv.toks)
                            fo = None
                            if final:
                                fo = (lambda R_, c_, n_, bk=((0, 1) if hh == 0 else (2, 3)), ob_=hh: self.final_out(R_, c_, n_, banks=bk, only_buf=ob_))
                            self.ln_tile(Rv, t0 + hh * 256, 256, g, b, tmp if hh == 0 else tmp2, final_out=fo, stat_banks=(6, 7) if hh == 0 else (4, 5))
                            P.dve(lambda e: e.engine_nop(), reads=Rv.toks, writes=[Rbig])
                        caps_.append(cp_)
                    P.zip(caps_[0], caps_[1])
                else:
                    Rs = View(Rbig.ap[:, :, 0:NSQ * LS], Rbig.toks)
                    P.dve(lambda e, e0=e0, n=n: e.tensor_copy(out=Rbig.ap[:, :, 0:NSQ * LS].rearrange("p m (s c) -> p m s c", c=LS),
                                                              in_=Rbig.ap[:, :, e0 - 2:e0 - 2 + n].rearrange("p m (s c) -> p m s c", c=10)[:, :, :, 2:10]),
                          reads=[Rbig], writes=[Rbig])
                    self.ln_tile(Rs, LP, NSQ * LS, g, b, tmp, final_out=self.final_out if final else None)
        self.mark('ffn%d_ln_end' % layer)
        for q in range(4):
            for t in range(11):
                bnk = 6 + (t // 4) % 2
                P.pe(lambda e, bnk=bnk, q=q, t=t: e.transpose(out=ps[bnk][0:2 + 2 * NSQ, (t % 4) * 128:(t % 4 + 1) * 128], in_=fco.ap[:, q * 11 + t, :],
                                                              identity=self.ident.ap), reads=[fco, self.ident], writes=psb(bnk))
                if t % 4 == 3 or t == 10:
                    t0 = (t // 4) * 4
                    nn = (t - t0 + 1) * 128
                    P.act(lambda e, bnk=bnk, t0=t0, nn=nn: e.activation(out=io_t.ap[0:2 + 2 * NSQ, t0 * 128:t0 * 128 + nn], in_=ps[bnk][0:2 + 2 * NSQ, 0:nn], func=AF.Copy),
                          reads=psb(bnk), writes=[io_t])
            P.dma(lambda e, q=q: e.dma_start(out=self.o_ffnconv_p[layer][:, q * 1408:(q + 1) * 1408], in_=io_t.ap[0:2, :]), reads=[io_t], writes=["o_ffnconv"])
            for s_ in range(NSQ):
                P.dma(lambda e, q=q, s_=s_: e.dma_start(out=self.o_ffnconv_s[layer][s_][:, q * 1408:(q + 1) * 1408], in_=io_t.ap[2 + 2 * s_:4 + 2 * s_, :]),
                      reads=[io_t], writes=["o_ffnconv"])

    def final_out(self, R, c0, n, banks=(0, 1), only_buf=None):
        P, ps = self.P, self.ps
        ob = self.obuf
        for t in range((n + 127) // 128):
            rows = min(128, n - t * 128)
            o = ob[self.ocnt % 2] if only_buf is None else ob[only_buf]
            self.ocnt += 1
            for half in range(2):
                bnk = banks[half]
                for kk in range(4):
                    m = half * 4 + kk
                    P.pe(lambda e, bnk=bnk, kk=kk, m=m, t=t, rows=rows: e.transpose(out=ps[bnk][0:rows, kk * 128:(kk + 1) * 128],
                                                                                 in_=R.ap[:, m, t * 128:t * 128 + rows], identity=self.ident.ap),
                         reads=[R, self.ident], writes=psb(bnk))
                P.act(lambda e, bnk=bnk, o=o, half=half, rows=rows: e.activation(out=o.ap[0:rows, half * 512:(half + 1) * 512], in_=ps[bnk][0:rows, :], func=AF.Copy),
                      reads=psb(bnk), writes=[o])
            r0 = c0 + t * 128
            dst = self.o_yp[r0:r0 + rows, :] if r0 < LP else self.o_ys[r0 - LP:r0 - LP + rows, :]
            P.dma(lambda e, dst=dst, o=o, rows=rows: e.dma_start(out=dst, in_=o.ap[0:rows, :]), reads=[o], writes=["o_y"])

    def _qrope_tile(self, i, c0, rows, KQ, wuq, cosT, sinT, q1, q2, qrt, bnk_a, bnk_b, qrT_dst, qrS):
        P, ps = self.P, self.ps
        T = slice(0, rows)
        for kc in range(3):
            P.pe(lambda e, kc=kc: e.matmul(ps[bnk_a][T, :], lhsT=KQ.ap[:, kc, c0:c0 + rows],
                                           rhs=wuq.ap[:, kc, :], start=(kc == 0), stop=(kc == 2)),
                 reads=[KQ, wuq], writes=psb(bnk_a))
        qv = ps[bnk_a][T, :].rearrange("p (h a f) -> p h a f", h=8, a=2)
        cos4 = cosT.ap[T, i, :].unsqueeze(1).unsqueeze(1).to_broadcast([rows, 8, 2, 32])
        sin4 = sinT.ap[T, i, :].unsqueeze(1).unsqueeze(1).to_broadcast([rows, 8, 2, 32])
        P.dve(lambda e: e.tensor_tensor(out=q1.ap[T], in0=qv, in1=cos4, op=ALU.mult), reads=psb(bnk_a) + [cosT], writes=[q1])
        P.dve(lambda e: e.tensor_tensor(out=q2.ap[T], in0=qv, in1=sin4, op=ALU.mult), reads=psb(bnk_a) + [sinT], writes=[q2])
        qo = qrt.ap[T, :].rearrange("p (h a f) -> p h a f", h=8, a=2)
        P.dve(lambda e: e.tensor_tensor(out=qo[:, :, 0, :], in0=q1.ap[T, :, 0, :], in1=q2.ap[T, :, 1, :], op=ALU.subtract), reads=[q1, q2], writes=[qrt])
        P.dve(lambda e: e.tensor_tensor(out=qo[:, :, 1, :], in0=q2.ap[T, :, 0, :], in1=q1.ap[T, :, 1, :], op=ALU.add), reads=[q1, q2], writes=[qrt])
        pq = ps[bnk_b][:, :].bitcast(BF16)
        if qrT_dst is not None:
            dst, dcol = qrT_dst
            for pr in range(4):
                P.pe(lambda e, pr=pr: e.transpose(out=pq[:, pr * 128:pr * 128 + rows], in_=qrt.ap[T, pr * 128:(pr + 1) * 128], identity=self.identb.ap[T, T]),
                     reads=[qrt, self.identb], writes=psb(bnk_b))
            P.act(lambda e: e.activation(out=dst.ap[:, :, dcol:dcol + rows], in_=pq[:, 0:512].rearrange("p (c t) -> p c t", c=4)[:, :, 0:rows],
                                         func=AF.Copy, scale=SM_SCALE), reads=psb(bnk_b), writes=[dst])
        else:
            s_ = (c0 - LP) // LS
            for h in range(8):
                P.pe(lambda e, h=h: e.transpose(out=pq[0:64, h * 8:h * 8 + rows], in_=qrt.ap[T, h * 64:(h + 1) * 64], identity=self.identb.ap[T, T]),
                     reads=[qrt, self.identb], writes=psb(bnk_b))
            P.act(lambda e: e.activation(out=qrS.ap[0:64, s_, :], in_=pq[0:64, 0:64], func=AF.Copy, scale=SM_SCALE), reads=psb(bnk_b), writes=[qrS])

    def stage_mla(self, g, b):
        P, A, ps, nc = self.P, self.A, self.ps, self.nc
        A.top = A.nbytes
        A.reset(0)
        NTL = LP // 128 + NSQ
        tiles = [(i * 128, 128) for i in range(LP // 128)] + [(LP + LS * s_, LS) for s_ in range(NSQ)]
        KQ = A.alloc([6, NT], BF16)
        ckv_tok = A.alloc([NTL, 256], BF16)
        qrS = A.alloc([NSQ, 8 * LS], BF16)
        qlatS = A.alloc([2, NSQ, 8, LS], BF16)
        wuq = A.alloc([3, 1536], BF16)
        wukT = A.alloc([8, 256], BF16)
        wuv = A.alloc([2, D], BF16)
        wuqr = A.alloc([3, 512], BF16)
        gq = A.alloc([384]); gkv = A.alloc([256])
        cosT = A.alloc([NTL, 32]); sinT = A.alloc([NTL, 32])
        oT = A.alloc([8, 512], BF16)
        oTs = A.alloc([8, NSQ * LS], BF16)
        smask = A.alloc([8])
        IDX = A.alloc([NSQ * NPAGES], I32)
        kmg = A.alloc([8])
        m_stage = A.mark()
        P.dma(lambda e: e.dma_start(out=wuq.ap, in_=self.mla_w_uq_d.rearrange("(k p) n -> p k n", p=128)), writes=[wuq], eng="pool")
        P.dma(lambda e: e.dma_start(out=wuv.ap, in_=self.mla_w_uv_d.rearrange("(k p) n -> p k n", p=128)), writes=[wuv], eng="pool")
        for kc in range(3):
            P.dve(lambda e, kc=kc: e.tensor_copy(out=wuqr.ap[:, kc, :].rearrange("p (h f) -> p h f", h=8),
                                                 in_=wuq.ap[:, kc, :].rearrange("p (h f) -> p h f", h=8)[:, :, 128:192]), reads=[wuq], writes=[wuqr])
        P.dma(lambda e: e.dma_start(out=gq.ap, in_=self.mla_q_norm_g_d.partition_broadcast(128)), writes=[gq])
        P.dma(lambda e: e.dma_start(out=gkv.ap, in_=self.mla_kv_norm_g_d.partition_broadcast(128)), writes=[gkv])
        P.dma(lambda e: e.dma_start(out=cosT.ap[:, 0:16, :], in_=self.rope_cos_d[0:LP, :].rearrange("(i p) f -> p i f", p=128)), writes=[cosT])
        P.dma(lambda e: e.dma_start(out=sinT.ap[:, 0:16, :], in_=self.rope_sin_d[0:LP, :].rearrange("(i p) f -> p i f", p=128)), writes=[sinT])
        P.dma(lambda e: e.dma_start(out=cosT.ap[0:LS, 16:NTL, :], in_=self.rope_cos_d[LP:NT, :].rearrange("(s p) f -> p s f", p=LS)), writes=[cosT])
        P.dma(lambda e: e.dma_start(out=sinT.ap[0:LS, 16:NTL, :], in_=self.rope_sin_d[LP:NT, :].rearrange("(s p) f -> p s f", p=LS)), writes=[sinT])
        P.dma(lambda e: e.dma_start(out=smask.ap[0:64, :], in_=self.smask_d[:, :]), writes=[smask])
        ptf = A.alloc([NSQ * NPAGES]); iof = A.alloc([1]); ioi = A.alloc([1], I32)
        P.dma(lambda e: e.dma_start(out=IDX.ap, in_=self.page_table_d.rearrange("s j -> (s j)").partition_broadcast(128)), writes=[IDX])
        P.pool(lambda e: e.iota(ioi.ap, pattern=[[0, 1]], base=0, channel_multiplier=1), writes=[ioi])
        P.dve(lambda e: e.tensor_copy(out=ptf.ap, in_=IDX.ap), reads=[IDX], writes=[ptf])
        P.dve(lambda e: e.tensor_copy(out=iof.ap, in_=ioi.ap), reads=[ioi], writes=[iof])
        P.dve(lambda e: e.tensor_scalar(out=IDX.ap, in0=ptf.ap, scalar1=128.0, scalar2=iof.ap[:, 0:1], op0=ALU.mult, op1=ALU.add), reads=[ptf, iof], writes=[IDX])
        wuk = A.alloc([2, D], BF16)
        win = A.alloc([8, 704], BF16)
        P.dma(lambda e: e.dma_start(out=wuk.ap, in_=self.mla_w_uk_d.rearrange("(k p) n -> p k n", p=128)), writes=[wuk], eng="pool")
        P.dma(lambda e: e.dma_start(out=win.ap, in_=self.w_in1_d.rearrange("(k p) n -> p k n", p=128)), writes=[win], eng="pool")
        for h in range(8):
            pt = ps[2 + h % 2][:, :].bitcast(BF16)
            for ch in range(2):
                P.pe(lambda e, pt=pt, h=h, ch=ch: e.transpose(out=pt[:, ch * 128:(ch + 1) * 128], in_=wuk.ap[:, ch, h * 128:(h + 1) * 128], identity=self.identb.ap),
                     reads=[wuk, self.identb], writes=psb(2 + h % 2))
            P.act(lambda e, pt=pt, h=h: e.activation(out=wukT.ap[:, h, :], in_=pt[:, 0:256], func=AF.Copy), reads=psb(2 + h % 2), writes=[wukT])
        stg = [A.alloc([768], BF16) for _ in range(2)]
        ckvf = [A.alloc([256]) for _ in range(2)]
        krf = [A.alloc([64]) for _ in range(2)]
        t1_ = [A.alloc([2, 32]) for _ in range(2)]; t2_ = [A.alloc([2, 32]) for _ in range(2)]; ssq_ = [A.alloc([2]) for _ in range(2)]; junk_ = [A.alloc([384]) for _ in range(2)]
        q1 = A.alloc([8, 2, 32]); q2 = A.alloc([8, 2, 32]); qrt = A.alloc([512], BF16)
        def tile_ops(i, c0, rows):
            T = slice(0, rows)
            st_, cf, kf = stg[i % 2], ckvf[i % 2], krf[i % 2]
            t1, t2, ssq, junk = t1_[i % 2], t2_[i % 2], ssq_[i % 2], junk_[i % 2]
            B0, B1, B2 = (0, 1, 2) if i % 2 == 0 else (3, 4, 5)
            for k in range(8):
                P.pe(lambda e, k=k, c0=c0, rows=rows, T=T: e.matmul(ps[B0][T, 0:384], lhsT=self.xhi.ap[:, k, c0:c0 + rows], rhs=win.ap[:, k, 0:384],
                                                                    start=(k == 0), stop=(k == 7)), reads=[win] + self.xt("xhi", k, k + 1, c0, c0 + rows), writes=psb(B0))
            for k in range(8):
                P.pe(lambda e, k=k, c0=c0, rows=rows, T=T: e.matmul(ps[B1][T, 0:320], lhsT=self.xhi.ap[:, k, c0:c0 + rows], rhs=win.ap[:, k, 384:704],
                                                                    start=(k == 0), stop=(k == 7)), reads=[win] + self.xt("xhi", k, k + 1, c0, c0 + rows), writes=psb(B1))
            P.act(lambda e, T=T: e.activation(out=junk.ap[T, 0:384], in_=ps[B0][T, 0:384], func=AF.Square, accum_out=ssq.ap[T, 0:1]), reads=psb(B0), writes=[junk, ssq])
            P.act(lambda e, T=T: e.activation(out=junk.ap[T, 0:256], in_=ps[B1][T, 0:256], func=AF.Square, accum_out=ssq.ap[T, 1:2]), reads=psb(B1), writes=[junk, ssq])
            P.dve(lambda e, T=T: e.tensor_scalar(out=ssq.ap[T, 0:1], in0=ssq.ap[T, 0:1], scalar1=1.0 / 384, scalar2=EPS, op0=ALU.mult, op1=ALU.add), reads=[ssq], writes=[ssq])
            P.dve(lambda e, T=T: e.tensor_scalar(out=ssq.ap[T, 1:2], in0=ssq.ap[T, 1:2], scalar1=1.0 / 256, scalar2=EPS, op0=ALU.mult, op1=ALU.add), reads=[ssq], writes=[ssq])
            P.act(lambda e, T=T: e.activation(out=ssq.ap[T, :], in_=ssq.ap[T, :], func=AF.Sqrt), reads=[ssq], writes=[ssq])
            P.dve(lambda e, T=T: e.reciprocal(out=ssq.ap[T, :], in_=ssq.ap[T, :]), reads=[ssq], writes=[ssq])
            P.dve(lambda e, T=T, st_=st_: e.scalar_tensor_tensor(out=st_.ap[T, 0:384], in0=ps[B0][T, 0:384], scalar=ssq.ap[T, 0:1], in1=gq.ap[T, :], op0=ALU.mult, op1=ALU.mult),
                  reads=psb(B0) + [ssq, gq], writes=[st_])
            P.dve(lambda e, T=T, cf=cf: e.scalar_tensor_tensor(out=cf.ap[T, :], in0=ps[B1][T, 0:256], scalar=ssq.ap[T, 1:2], in1=gkv.ap[T, :], op0=ALU.mult, op1=ALU.mult),
                  reads=psb(B1) + [ssq, gkv], writes=[cf])
            P.act(lambda e, T=T, cf=cf, st_=st_: e.activation(out=st_.ap[T, 384:640], in_=cf.ap[T, :], func=AF.Copy), reads=[cf], writes=[st_])
            P.pool(lambda e, T=T, cf=cf, i=i: e.tensor_copy(out=ckv_tok.ap[T, i, :], in_=cf.ap[T, :]), reads=[cf], writes=[ckv_tok])
            krv = ps[B1][T, 256:320].rearrange("p (a f) -> p a f", a=2)
            cosb = cosT.ap[T, i, :].unsqueeze(1).to_broadcast([rows, 2, 32])
            sinb = sinT.ap[T, i, :].unsqueeze(1).to_broadcast([rows, 2, 32])
            P.dve(lambda e, T=T, krv=krv, cosb=cosb: e.tensor_tensor(out=t1.ap[T, :, :], in0=krv, in1=cosb, op=ALU.mult), reads=psb(B1) + [cosT], writes=[t1])
            P.dve(lambda e, T=T, krv=krv, sinb=sinb: e.tensor_tensor(out=t2.ap[T, :, :], in0=krv, in1=sinb, op=ALU.mult), reads=psb(B1) + [sinT], writes=[t2])
            P.dve(lambda e, T=T, kf=kf: e.tensor_tensor(out=kf.ap[T, 0:32], in0=t1.ap[T, 0, :], in1=t2.ap[T, 1, :], op=ALU.subtract), reads=[t1, t2], writes=[kf])
            P.dve(lambda e, T=T, kf=kf: e.tensor_tensor(out=kf.ap[T, 32:64], in0=t2.ap[T, 0, :], in1=t1.ap[T, 1, :], op=ALU.add), reads=[t1, t2], writes=[kf])
            P.act(lambda e, T=T, kf=kf, st_=st_, rows=rows: e.activation(out=st_.ap[T, 640:768].rearrange("p (a f) -> p a f", a=2),
                                                            in_=kf.ap[T, :].unsqueeze(1).to_broadcast([rows, 2, 64]), func=AF.Copy), reads=[kf], writes=[st_])
            if c0 < LP:
                P.dma(lambda e, cf=cf, c0=c0, rows=rows: e.dma_start(out=self.o_ckv_p[c0:c0 + rows, :], in_=cf.ap[0:rows, :]), reads=[cf], writes=["o_ckv"])
                P.dma(lambda e, kf=kf, c0=c0, rows=rows: e.dma_start(out=self.o_krope_p[c0:c0 + rows, :], in_=kf.ap[0:rows, :]), reads=[kf], writes=["o_kr"])
            else:
                P.dma(lambda e, cf=cf, c0=c0, rows=rows: e.dma_start(out=self.o_ckv_s[c0 - LP:c0 - LP + rows, :], in_=cf.ap[0:rows, :]), reads=[cf], writes=["o_ckv"])
                P.dma(lambda e, kf=kf, c0=c0, rows=rows: e.dma_start(out=self.o_krope_s[c0 - LP:c0 - LP + rows, :], in_=kf.ap[0:rows, :]), reads=[kf], writes=["o_kr"])
            pt = ps[B2][:, :].bitcast(BF16)
            for c in range(6):
                P.pe(lambda e, pt=pt, c=c, T=T, rows=rows, st_=st_: e.transpose(out=pt[:, c * 128:c * 128 + rows], in_=st_.ap[T, c * 128:(c + 1) * 128], identity=self.identb.ap[T, T]),
                     reads=[st_, self.identb], writes=psb(B2))
            P.act(lambda e, pt=pt, c0=c0, rows=rows: e.activation(out=KQ.ap[:, :, c0:c0 + rows], in_=pt[:, 0:768].rearrange("p (c t) -> p c t", c=6)[:, :, 0:rows], func=AF.Copy),
                  reads=psb(B2), writes=[KQ])
            if c0 >= LP:
                self._qrope_tile(i, c0, rows, KQ, wuqr, cosT, sinT, q1, q2, qrt, 6, 7, None, qrS)
        caps = []
        for i, (c0, rows) in enumerate(tiles):
            with P.capture() as cap_:
                tile_ops(i, c0, rows)
            caps.append(cap_)
        for i in range(0, LP // 128, 2):
            P.zip(caps[i], caps[i + 1])
        for i in range(LP // 128, len(caps)):
            P.zip(caps[i])
        self.mark('mla_phase1_end')
        A.reset(m_stage)
        sq = A.alloc([2, 512], BF16)
        mx = A.alloc([8])
        for ti in range(4):
            c0 = ti * 512
            P.act(lambda e, c0=c0: e.activation(out=sq.ap, in_=KQ.ap[:, 3:5, c0:c0 + 512], func=AF.Square), reads=[KQ], writes=[sq])
            for ch in range(2):
                P.pe(lambda e, ch=ch: e.matmul(ps[5][:, :], lhsT=self.onesb.ap, rhs=sq.ap[:, ch, :], start=(ch == 0), stop=(ch == 1)), reads=[self.onesb, sq], writes=psb(5))
            P.dve(lambda e, ti=ti: e.reduce_max(out=mx.ap[:, ti:ti + 1], in_=ps[5][:, :], axis=AX.X), reads=psb(5), writes=[mx])
            P.act(lambda e, c0=c0: e.activation(out=sq.ap[0:64, 0, :], in_=KQ.ap[0:64, 5, c0:c0 + 512], func=AF.Square), reads=[KQ], writes=[sq])
            P.pe(lambda e: e.matmul(ps[5][:, :], lhsT=self.onesb.ap[0:64, :], rhs=sq.ap[0:64, 0, :], start=True, stop=True), reads=[self.onesb, sq], writes=psb(5))
            P.dve(lambda e, ti=ti: e.reduce_max(out=mx.ap[:, 4 + ti:5 + ti], in_=ps[5][:, :], axis=AX.X), reads=psb(5), writes=[mx])
        P.dve(lambda e: e.reduce_max(out=kmg.ap[:, 0:1], in_=mx.ap[:, 0:4], axis=AX.X), reads=[mx], writes=[kmg])
        P.dve(lambda e: e.reduce_max(out=kmg.ap[:, 1:2], in_=mx.ap[:, 4:8], axis=AX.X), reads=[mx], writes=[kmg])

        wo = [A.alloc([8, 128], BF16) for _ in range(2)]
        m_att = A.mark()
        R = A.alloc([8, 256])
        tmp = (A.alloc([8, 256], BF16), A.alloc([8, 256], BF16), A.alloc([256]), A.alloc([256]))
        m_ln_end = A.mark()
        A.reset(m_att)
        qn = A.alloc([512], BF16)
        PT = [A.alloc([512], BF16) for _ in range(2)]
        rl = A.alloc([512]); olat = A.alloc([2, 512], BF16)
        q1 = A.alloc([8, 2, 32]); q2 = A.alloc([8, 2, 32]); qrt = A.alloc([512], BF16)
        A.reset(max(A.mark(), m_ln_end))
        qlat2 = [A.alloc([2, 512], BF16) for _ in range(2)]
        qrT2 = [A.alloc([4, 512], BF16) for _ in range(2)]
        km2 = [A.alloc([8]) for _ in range(2)]
        wocnt = [0]
        MB = [5]

        def q_lat(h, c0, n, dst, dst_toks, sample=False):
            bnk = MB[0]
            for kc in range(3):
                P.pe(lambda e, bnk=bnk, kc=kc: e.matmul(ps[bnk][:, 0:n], lhsT=wuq.ap[:, kc, 192 * h:192 * h + 128], rhs=KQ.ap[:, kc, c0:c0 + n], start=(kc == 0), stop=(kc == 2)),
                     reads=[wuq, KQ], writes=psb(bnk))
            P.act(lambda e, bnk=bnk: e.activation(out=qn.ap[:, 0:n], in_=ps[bnk][:, 0:n], func=AF.Copy), reads=psb(bnk), writes=[qn])
            for ch in range(2):
                P.pe(lambda e, bnk=bnk, ch=ch: e.matmul(ps[bnk][:, 0:n], lhsT=wukT.ap[:, h, ch * 128:(ch + 1) * 128], rhs=qn.ap[:, 0:n], start=True, stop=True),
                     reads=[wukT, qn], writes=psb(bnk))
                src_ = ps[bnk][:, 0:n].rearrange("p (s t) -> p s t", t=LS) if sample else ps[bnk][:, 0:n]
                P.act(lambda e, bnk=bnk, ch=ch, src_=src_: e.activation(out=dst(ch), in_=src_, func=AF.Copy, scale=SM_SCALE), reads=psb(bnk), writes=dst_toks)

        for h in range(8):
            q_lat(h, LP, NSQ * LS, lambda ch, h=h: qlatS.ap[:, ch, :, h, :], [qlatS], sample=True)

        GP = 4
        NGS = NPAGES // GP
        NBUF = 4
        pgc = [A.alloc([GP, 256], BF16) for _ in range(NBUF)]
        pgr = [A.alloc([GP, 64], BF16) for _ in range(NBUF)]
        KTs = A.alloc([3, GP * 128], BF16)
        Pm = A.alloc([GP * 128], BF16)
        PTs = A.alloc([GP, 64], BF16)
        sm_ = A.alloc([16])
        acc = A.alloc([8])
        Oacc = A.alloc([256]); olS = A.alloc([256], BF16); olST = A.alloc([2, 64], BF16)
        cache_c = self.cache_ckv_d
        cache_r = self.cache_krope_d
        groups = [(s_, g_) for s_ in range(NSQ) for g_ in range(NGS)]

        def issue_gather(gi):
            if gi >= len(groups):
                return
            s_, g_ = groups[gi]
            bc, br = pgc[gi % NBUF], pgr[gi % NBUF]
            for pp in range(GP):
                col = s_ * NPAGES + g_ * GP + pp
                P.dma(lambda e, bc=bc, pp=pp, col=col: e.indirect_dma_start(out=bc.ap[:, pp, :], out_offset=None, in_=cache_c,
                                                                            in_offset=bass.IndirectOffsetOnAxis(ap=IDX.ap[:, col:col + 1], axis=0)),
                      reads=[IDX], writes=[bc], eng="pool")
                P.dma(lambda e, br=br, pp=pp, col=col: e.indirect_dma_start(out=br.ap[:, pp, :], out_offset=None, in_=cache_r,
                                                                            in_offset=bass.IndirectOffsetOnAxis(ap=IDX.ap[:, col:col + 1], axis=0)),
                      reads=[IDX], writes=[br], eng="pool")

        def seq_init():
            P.pool(lambda e: e.memset(acc.ap[0:64, 0:1], -1e30), writes=[acc])
            P.pool(lambda e: e.memset(acc.ap[0:64, 1:2], 0.0), writes=[acc])
            P.pool(lambda e: e.memset(Oacc.ap[0:64, :], 0.0), writes=[Oacc])

        def scores_softmax_pv(s_, nkeys, kt_c, kt_r, vfn, npg, mask=False):
            kn = nkeys
            for ch in range(2):
                P.pe(lambda e, ch=ch: e.matmul(ps[7][0:64, 0:kn], lhsT=qlatS.ap[:, ch, s_, :, :].rearrange("p h t -> p (h t)"), rhs=kt_c[0](ch),
                                               start=(ch == 0), stop=False), reads=[qlatS] + kt_c[1], writes=psb(7))
            P.pe(lambda e: e.matmul(ps[7][0:64, 0:kn], lhsT=qrS.ap[0:64, s_, :], rhs=kt_r[0], start=False, stop=True), reads=[qrS] + kt_r[1], writes=psb(7))
            if mask:
                P.dve(lambda e: e.tensor_tensor(out=ps[7][0:64, 0:kn], in0=ps[7][0:64, 0:kn], in1=smask.ap[0:64, 0:kn], op=ALU.add), reads=psb(7) + [smask], writes=psb(7))
            P.dve(lambda e: e.reduce_max(out=sm_.ap[0:64, 0:1], in_=ps[7][0:64, 0:kn], axis=AX.X), reads=psb(7), writes=[sm_])
            P.dve(lambda e: e.tensor_tensor(out=acc.ap[0:64, 2:3], in0=acc.ap[0:64, 0:1], in1=sm_.ap[0:64, 0:1], op=ALU.max), reads=[acc, sm_], writes=[acc])
            P.dve(lambda e: e.tensor_tensor(out=acc.ap[0:64, 3:4], in0=acc.ap[0:64, 0:1], in1=acc.ap[0:64, 2:3], op=ALU.subtract), reads=[acc], writes=[acc])
            P.dve(lambda e: e.tensor_scalar(out=acc.ap[0:64, 4:5], in0=acc.ap[0:64, 2:3], scalar1=-1.0, scalar2=None, op0=ALU.mult), reads=[acc], writes=[acc])
            P.act(lambda e: e.activation(out=acc.ap[0:64, 3:4], in_=acc.ap[0:64, 3:4], func=AF.Exp), reads=[acc], writes=[acc])
            P.act(lambda e: e.activation(out=Pm.ap[0:64, 0:kn], in_=ps[7][0:64, 0:kn], func=AF.Exp, bias=acc.ap[0:64, 4:5], accum_out=sm_.ap[0:64, 1:2]),
                  reads=psb(7) + [acc], writes=[Pm, sm_])
            P.dve(lambda e: e.scalar_tensor_tensor(out=acc.ap[0:64, 1:2], in0=acc.ap[0:64, 1:2], scalar=acc.ap[0:64, 3:4], in1=sm_.ap[0:64, 1:2], op0=ALU.mult, op1=ALU.add),
                  reads=[acc, sm_], writes=[acc])
            P.dve(lambda e: e.tensor_copy(out=acc.ap[0:64, 0:1], in_=acc.ap[0:64, 2:3]), reads=[acc], writes=[acc])
            P.dve(lambda e: e.tensor_scalar(out=Oacc.ap[0:64, :], in0=Oacc.ap[0:64, :], scalar1=acc.ap[0:64, 3:4], scalar2=None, op0=ALU.mult), reads=[Oacc, acc], writes=[Oacc])
            ptp = ps[6][:, :].bitcast(BF16)
            for pp in range(npg):
                k1 = min(128, nkeys - pp * 128)
                P.pe(lambda e, pp=pp, k1=k1: e.transpose(out=ptp[0:k1, pp * 64:(pp + 1) * 64], in_=Pm.ap[0:64, pp * 128:pp * 128 + k1], identity=self.identb.ap[0:64, 0:64]),
                     reads=[Pm, self.identb], writes=psb(6))
            kmax = min(128, nkeys)
            P.act(lambda e: e.activation(out=PTs.ap[0:kmax, 0:npg, :], in_=ptp[0:kmax, 0:npg * 64].rearrange("p (g c) -> p g c", c=64), func=AF.Copy),
                  reads=psb(6), writes=[PTs])
            for pp in range(npg):
                k1 = min(128, nkeys - pp * 128)
                P.pe(lambda e, pp=pp, k1=k1: e.matmul(ps[7][0:64, 0:256], lhsT=PTs.ap[0:k1, pp, :], rhs=vfn[0](pp, k1), start=(pp == 0), stop=(pp == npg - 1)),
                     reads=[PTs] + vfn[1], writes=psb(7))
            P.dve(lambda e: e.tensor_tensor(out=Oacc.ap[0:64, :], in0=Oacc.ap[0:64, :], in1=ps[7][0:64, 0:256], op=ALU.add), reads=[Oacc] + psb(7), writes=[Oacc])

        def process_group(gi):
            s_, g_ = groups[gi]
            bc, br = pgc[gi % NBUF], pgr[gi % NBUF]
            if g_ == 0:
                seq_init()
            ptk = ps[6][:, :].bitcast(BF16)
            for ch in range(2):
                for pp in range(GP):
                    P.pe(lambda e, ch=ch, pp=pp, bc=bc: e.transpose(out=ptk[:, ch * 512 + pp * 128:ch * 512 + (pp + 1) * 128], in_=bc.ap[:, pp, ch * 128:(ch + 1) * 128], identity=self.identb.ap),
                         reads=[bc, self.identb], writes=psb(6))
            P.act(lambda e: e.activation(out=KTs.ap[:, 0:2, :], in_=ptk[:, 0:1024].rearrange("p (c k) -> p c k", c=2), func=AF.Copy), reads=psb(6), writes=[KTs])
            ptr = ps[7][:, :].bitcast(BF16)
            for pp in range(GP):
                P.pe(lambda e, pp=pp, br=br: e.transpose(out=ptr[0:64, pp * 128:(pp + 1) * 128], in_=br.ap[:, pp, :], identity=self.identb.ap),
                     reads=[br, self.identb], writes=psb(7))
            P.dve(lambda e: e.tensor_copy(out=KTs.ap[0:64, 2, :], in_=ptr[0:64, 0:GP * 128]), reads=psb(7), writes=[KTs])
            scores_softmax_pv(s_, GP * 128, (lambda ch: KTs.ap[:, ch, :], [KTs]), (KTs.ap[0:64, 2, :], [KTs]),
                              (lambda pp, k1, bc=bc: bc.ap[0:k1, pp, :], [bc]), GP)
            issue_gather(gi + NBUF)
            if g_ == NGS - 1:
                finish_seq(s_)

        def finish_seq(s_):
            c0 = LP + s_ * LS
            scores_softmax_pv(s_, LS, (lambda ch: KQ.ap[:, 3 + ch, c0:c0 + LS], [KQ]), (KQ.ap[0:64, 5, c0:c0 + LS], [KQ]),
                              (lambda pp, k1: ckv_tok.ap[0:k1, 16 + s_, :], [ckv_tok]), 1, mask=True)
            P.dve(lambda e: e.reciprocal(out=acc.ap[0:64, 5:6], in_=acc.ap[0:64, 1:2]), reads=[acc], writes=[acc])
            P.dve(lambda e: e.tensor_scalar(out=olS.ap[0:64, :], in0=Oacc.ap[0:64, :], scalar1=acc.ap[0:64, 5:6], scalar2=None, op0=ALU.mult), reads=[Oacc, acc], writes=[olS])
            ptp = ps[6][:, :].bitcast(BF16)
            for ch in range(2):
                P.pe(lambda e, ch=ch: e.transpose(out=ptp[:, ch * 64:(ch + 1) * 64], in_=olS.ap[0:64, ch * 128:(ch + 1) * 128], identity=self.identb.ap[0:64, 0:64]),
                     reads=[olS, self.identb], writes=psb(6))
            P.act(lambda e: e.activation(out=olST.ap, in_=ptp[:, 0:128].rearrange("p (c t) -> p c t", c=2), func=AF.Copy), reads=psb(6), writes=[olST])
            for h in range(8):
                for ch in range(2):
                    P.pe(lambda e, h=h, ch=ch: e.matmul(ps[7][:, h * LS:(h + 1) * LS], lhsT=wuv.ap[:, ch, h * 128:(h + 1) * 128], rhs=olST.ap[:, ch, h * LS:(h + 1) * LS],
                                                        start=(ch == 0), stop=(ch == 1)), reads=[wuv, olST], writes=psb(7))
            P.act(lambda e: e.activation(out=oTs.ap[:, :, s_ * LS:(s_ + 1) * LS], in_=ps[7][:, 0:64].rearrange("p (h t) -> p h t", h=8), func=AF.Copy), reads=psb(7), writes=[oTs])

        def outproj_ln(src, sc0, tc0, n, banks):
            for m in range(8):
                w = wo[wocnt[0] % 2]
                wocnt[0] += 1
                P.dma(lambda e, w=w, m=m: e.dma_start(out=w.ap, in_=self.w_out1_d[:, m * 128:(m + 1) * 128].rearrange("(k p) n -> p k n", p=128)), writes=[w], eng="pool")
                bnk = banks[m % len(banks)]
                for k in range(8):
                    P.pe(lambda e, bnk=bnk, w=w, k=k: e.matmul(ps[bnk][:, 0:n], lhsT=w.ap[:, k, :], rhs=src.ap[:, k, sc0:sc0 + n], start=(k == 0), stop=(k == 7)),
                         reads=[w, src], writes=psb(bnk))
                self.resid_into(R.ap[:, m, 0:n], (ps[bnk][:, 0:n], psb(bnk), [R]), m, tc0, n)
            self.ln_tile(R, tc0, n, g, b, tmp, stat_banks=(banks[0], banks[1]))

        def prologue(Q, h, par):
            q0 = Q * 512
            hb_ = 64 * (h % 2)
            qlat, km, qrT = qlat2[par], km2[par], qrT2[Q % 2]
            if h == 0:
                for tt in range(4):
                    self._qrope_tile(Q * 4 + tt, q0 + tt * 128, 128, KQ, wuqr, cosT, sinT, q1, q2, qrt, 5, 5, (qrT, tt * 128), qrS)
            q_lat(h, q0, 512, lambda ch: qlat.ap[:, ch, :], [qlat])
            P.act(lambda e: e.activation(out=sq.ap, in_=qlat.ap, func=AF.Square), reads=[qlat], writes=[sq])
            for ch in range(2):
                P.pe(lambda e, ch=ch: e.matmul(ps[5][:, :], lhsT=self.onesb.ap, rhs=sq.ap[:, ch, :], start=(ch == 0), stop=(ch == 1)), reads=[self.onesb, sq], writes=psb(5))
            P.dve(lambda e: e.reduce_max(out=km.ap[:, 2:3], in_=ps[5][:, :], axis=AX.X), reads=psb(5), writes=[km])
            P.act(lambda e: e.activation(out=sq.ap[hb_:hb_ + 64, 0, :], in_=qrT.ap[hb_:hb_ + 64, h // 2, :], func=AF.Square), reads=[qrT], writes=[sq])
            P.pe(lambda e: e.matmul(ps[5][:, :], lhsT=self.onesb.ap[hb_:hb_ + 64, :], rhs=sq.ap[hb_:hb_ + 64, 0, :], start=True, stop=True), reads=[self.onesb, sq], writes=psb(5))
            P.dve(lambda e: e.reduce_max(out=km.ap[:, 3:4], in_=ps[5][:, :], axis=AX.X), reads=psb(5), writes=[km])
            P.dve(lambda e: e.tensor_tensor(out=km.ap[:, 4:6], in0=kmg.ap[:, 0:2], in1=km.ap[:, 2:4], op=ALU.mult), reads=[km, kmg], writes=[km])
            P.act(lambda e: e.activation(out=km.ap[:, 4:6], in_=km.ap[:, 4:6], func=AF.Sqrt), reads=[km], writes=[km])
            P.dve(lambda e: e.tensor_tensor(out=km.ap[:, 6:7], in0=km.ap[:, 4:5], in1=km.ap[:, 5:6], op=ALU.add), reads=[km], writes=[km])
            P.dve(lambda e: e.tensor_scalar(out=km.ap[:, 7:8], in0=km.ap[:, 6:7], scalar1=-1.0, scalar2=None, op0=ALU.mult), reads=[km], writes=[km])

        def attend(Q, h, par):
            hb_ = 64 * (h % 2)
            qlat, km, qrT = qlat2[par], km2[par], qrT2[Q % 2]
            nkb = 4 * Q + 4

            def emit_S(kb):
                k0 = kb * 128
                qs = max(0, 128 * (kb - 4 * Q))
                nq = 512 - qs
                sb_ = kb % 2
                for ch in range(2):
                    P.pe(lambda e, ch=ch: e.matmul(ps[sb_][:, 0:nq], lhsT=KQ.ap[:, 3 + ch, k0:k0 + 128], rhs=qlat.ap[:, ch, qs:512], start=(ch == 0), stop=False),
                         reads=[KQ, qlat], writes=psb(sb_))
                P.pe(lambda e: e.matmul(ps[sb_][:, 0:nq], lhsT=KQ.ap[hb_:hb_ + 64, 5, k0:k0 + 128], rhs=qrT.ap[hb_:hb_ + 64, h // 2, qs:512], start=False, stop=True),
                     reads=[KQ, qrT], writes=psb(sb_))

            def emit_PV(kb):
                qs = max(0, 128 * (kb - 4 * Q))
                nq = 512 - qs
                sb_ = kb % 2
                pT = PT[kb % 2]
                P.act(lambda e: e.activation(out=pT.ap[:, 0:nq], in_=ps[sb_][:, 0:nq], func=AF.Exp, bias=km.ap[:, 7:8]), reads=psb(sb_) + [km], writes=[pT])
                if kb >= 4 * Q:
                    P.dve(lambda e: e.tensor_tensor(out=pT.ap[:, 0:128], in0=pT.ap[:, 0:128], in1=self.utrib.ap, op=ALU.mult), reads=[pT, self.utrib], writes=[pT])
                for ch in range(2):
                    P.pe(lambda e, ch=ch: e.matmul(ps[2 + ch][:, qs:512], lhsT=ckv_tok.ap[:, kb, ch * 128:(ch + 1) * 128], rhs=pT.ap[:, 0:nq],
                                                   start=(kb == 0), stop=(kb == nkb - 1)), reads=[ckv_tok, pT], writes=psb(2 + ch))
                P.pe(lambda e: e.matmul(ps[4][:, qs:512], lhsT=self.onesb.ap, rhs=pT.ap[:, 0:nq], start=(kb == 0), stop=(kb == nkb - 1)),
                     reads=[self.onesb, pT], writes=psb(4))

            emit_S(0)
            for kb in range(nkb):
                if kb + 1 < nkb:
                    emit_S(kb + 1)
                emit_PV(kb)
            P.dve(lambda e: e.reciprocal(out=rl.ap, in_=ps[4][:, :]), reads=psb(4), writes=[rl])
            for ch in range(2):
                P.dve(lambda e, ch=ch: e.tensor_tensor(out=olat.ap[:, ch, :], in0=ps[2 + ch][:, :], in1=rl.ap, op=ALU.mult), reads=psb(2 + ch) + [rl], writes=[olat])
            for ch in range(2):
                P.pe(lambda e, ch=ch: e.matmul(ps[0][:, :], lhsT=wuv.ap[:, ch, h * 128:(h + 1) * 128], rhs=olat.ap[:, ch, :], start=(ch == 0), stop=(ch == 1)),
                     reads=[wuv, olat], writes=psb(0))
            P.act(lambda e: e.activation(out=oT.ap[:, h, :], in_=ps[0][:, :], func=AF.Copy), reads=psb(0), writes=[oT])

        self.mark('mla_units_start')
        units = [(Q, h) for Q in range(4) for h in range(8)]
        for gi in range(NBUF):
            issue_gather(gi)
        prologue(0, 0, 0)
        gnext = 0
        LN_GROUPS = 5
        wsum = sum(4 * Q_ + 4 for (Q_, h_) in units)
        wacc = 0
        for ui, (Q, h) in enumerate(units):
            with P.capture() as la:
                attend(Q, h, ui % 2)
            wacc += 4 * Q + 4
            target = -(-(len(groups) - 4 * LN_GROUPS) * wacc // wsum) + LN_GROUPS * Q
            with P.capture() as lb:
                while gnext < min(target, len(groups)):
                    process_group(gnext)
                    gnext += 1
            with P.capture() as lc:
                if ui + 1 < len(units):
                    prologue(units[ui + 1][0], units[ui + 1][1], (ui + 1) % 2)
            P.zip(la, lb, lc)
            if h == 7:
                self.mark('mla_Q%d_attn_end' % Q)
                with P.capture() as lo:
                    for hf2 in range(2):
                        outproj_ln(oT, hf2 * 256, Q * 512 + hf2 * 256, 256, (0, 1, 5))
                with P.capture() as lg:
                    for _ in range(LN_GROUPS):
                        if gnext < len(groups):
                            process_group(gnext)
                            gnext += 1
                P.zip(lo, lg)
        self.mark('mla_units_end')
        while gnext < len(groups):
            process_group(gnext)
            gnext += 1
        outproj_ln(oTs, 0, LP, NSQ * LS, (0, 1, 5))

    def stage_debug(self):
        if not hasattr(self, "dbgsrc"):
            self.dbgsrc = {}
        self.dbgsrc["xhi"] = (self.xhi, [128, 8 * NT], BF16)
        self.dbgsrc["xlo"] = (self.xlo, [128, 8 * NT], BF16)
        for name in self.dbg:
            src, shape, dt = self.dbgsrc[name]
            o = self.dout("dbg_" + name, shape, dt)
            rd = [src]
            if name in ("xhi", "xlo"):
                rd = self.xt(name, 0, 8, 0, NT)
            self.P.dma(lambda e, o=o, src=src: e.dma_start(out=o, in_=src.ap), reads=rd, writes=["dbg_" + name])


def _consts():
    c = {}
    c["c_ident"] = np.eye(128, dtype=np.float32)
    c["c_utri"] = np.triu(np.ones((128, 128), np.float32))
    m2 = np.zeros((128, 2), np.float32)
    m2[:64, 0] = 1.0
    m2[64:, 1] = 1.0
    c["c_mask2"] = m2
    c["c_bdmask"] = np.kron(np.eye(4, dtype=np.float32), np.ones((32, 32), np.float32))
    inv = (10000.0 ** (-np.arange(0, 64, 2, dtype=np.float32) / np.float32(64))).astype(np.float32)
    pos = np.concatenate([np.arange(LP), np.tile(16384 + np.arange(LS), NSQ)]).astype(np.float32)
    ang = (pos[:, None] * inv[None, :]).astype(np.float32)
    c["c_rope_cos"] = np.cos(ang).astype(np.float32)
    c["c_rope_sin"] = np.sin(ang).astype(np.float32)
    sm = np.zeros((8, 8, 8), np.float32)
    for t in range(8):
        sm[:, t, t + 1:] = -1e9
    c["c_smask"] = sm.reshape(64, 8)
    return c


def core_inputs(inp, core):
    f = np.ascontiguousarray
    d = dict(_consts())
    sl = slice(NSQ * core, NSQ * core + NSQ)
    d["xp"] = f(inp["x_prompt"][core])
    d["xs"] = f(inp["x_sample"][sl].reshape(NSQ * LS, D))
    d["w_in0"] = f(inp["w_in0"][0])
    d["s5_a_re"] = f(inp["s5_a_re"][0].reshape(2048))
    d["s5_a_im"] = f(inp["s5_a_im"][0].reshape(2048))
    d["s5_log_dt"] = f(inp["s5_log_dt"][0])
    d["s5_b_re"] = f(inp["s5_b_re"][0].reshape(2048, 16))
    d["s5_b_im"] = f(inp["s5_b_im"][0].reshape(2048, 16))
    d["s5_c_re"] = f(inp["s5_c_re"][0])
    d["s5_c_im"] = f(inp["s5_c_im"][0])
    d["s5_d"] = f(inp["s5_d"][0].reshape(512))
    d["s5_w_glu"] = f(inp["s5_w_glu"][0])
    d["s5_b_glu"] = f(inp["s5_b_glu"][0])
    d["w_in1"] = f(inp["w_in1"][0])
    d["mla_q_norm_g"] = f(inp["mla_q_norm_g"][0]); d["mla_kv_norm_g"] = f(inp["mla_kv_norm_g"][0])
    d["mla_w_uq"] = f(inp["mla_w_uq"][0])
    d["mla_w_uk"] = f(inp["mla_w_uk"][0].reshape(256, 1024)); d["mla_w_uv"] = f(inp["mla_w_uv"][0].reshape(256, 1024))
    d["w_out1"] = f(inp["w_out1"][0])
    d["cache_ckv"] = inp["cache_ckv"].reshape(5120 * 128, 256); d["cache_krope"] = inp["cache_krope"].reshape(5120 * 128, 64)
    d["page_table"] = f(inp["page_table"][sl])
    d["w_out0"] = f(inp["w_out0"][0])
    for k in ("ln1_g", "ln1_b", "ln2_g", "ln2_b", "ffn_w_up", "ffn_conv_w", "ffn_conv_b", "ffn_w_down"):
        d[k] = f(inp[k])
    d["ffnconv0"] = f(inp["state_ffn_conv"][:, sl].reshape(2, NSQ * 2, 2 * DFF))
    d["ssd_conv_w"] = f(inp["ssd_conv_w"][0])
    d["ssd_conv_b"] = f(inp["ssd_conv_b"][0])
    d["ssd_dt_bias"] = f(inp["ssd_dt_bias"][0])
    d["ssd_a_log"] = f(inp["ssd_a_log"][0])
    d["ssd_d"] = f(inp["ssd_d"][0])
    d["ssd_norm_g"] = f(inp["ssd_norm_g"][0])
    d["ssd0"] = f(inp["state_ssd"][0, sl].reshape(NSQ, 512, 128))
    d["ssdconv0"] = f(inp["state_ssd_conv"][0, sl].reshape(NSQ * 3, 1024))
    d["s5re0"] = f(inp["state_s5_re"][0, sl].reshape(NSQ, 2048))
    d["s5im0"] = f(inp["state_s5_im"][0, sl].reshape(NSQ, 2048))
    return d


_NC_CACHE = {}


def kernel(**inputs):
    inp = {k: np.asarray(v) for k, v in inputs.items()}
    if "nc" not in _NC_CACHE:
        B = Builder()
        _NC_CACHE["nc"] = (B.build(), list(B.ins))
    nc, in_names = _NC_CACHE["nc"]
    maps = []
    for c in range(NCORES):
        d = core_inputs(inp, c)
        maps.append({k: d[k] for k in in_names})
    res = run_bass_kernel_spmd(nc, maps, core_ids=list(range(NCORES)))
    r = res.results
    cat = lambda name: np.stack([np.asarray(r[c][name]) for c in range(NCORES)], 0)
    f32 = lambda a: np.ascontiguousarray(a, dtype=np.float32)
    y_p = f32(cat("o_yp"))
    y_s = f32(cat("o_ys").reshape(NCORES * NSQ, LS, D))
    s5re_p = f32(cat("o_s5re_p").reshape(1, NCORES, 32, 64))
    s5im_p = f32(cat("o_s5im_p").reshape(1, NCORES, 32, 64))
    ssd_p = f32(cat("o_ssd_p").reshape(1, NCORES, 8, 64, 128))
    ssdc_p = f32(cat("o_ssdconv_p").reshape(1, NCORES, 3, 1024))
    ckv_p = f32(cat("o_ckv_p").reshape(NCORES, 1, LP, 256))
    kr_p = f32(cat("o_krope_p").reshape(NCORES, 1, LP, 64))
    ffc_p = f32(cat("o_ffnconv_p").transpose(1, 0, 2, 3))
    s5re_s = f32(cat("o_s5re_s").reshape(1, NCORES * NSQ, 32, 64))
    s5im_s = f32(cat("o_s5im_s").reshape(1, NCORES * NSQ, 32, 64))
    ssd_s = f32(cat("o_ssd_s").reshape(1, NCORES * NSQ, 8, 64, 128))
    ssdc_s = f32(cat("o_ssdconv_s").reshape(1, NCORES * NSQ, 3, 1024))
    ckv_s = f32(cat("o_ckv_s").reshape(NCORES * NSQ, 1, LS, 256))
    kr_s = f32(cat("o_krope_s").reshape(NCORES * NSQ, 1, LS, 64))
    ffc_s = f32(cat("o_ffnconv_s").transpose(1, 0, 2, 3, 4).reshape(2, NCORES * NSQ, 2, 2 * DFF))
    return (y_p, y_s, s5re_p, s5im_p, ssd_p, ssdc_p, ckv_p, kr_p, ffc_p, s5re_s, s5im_s, ssd_s, ssdc_s, ckv_s, kr_s, ffc_s)
```
